# Optimizing a Trainium2 kernel written in Bass

```python
import jax, jax.numpy as jnp
from jax import lax
import numpy as np

D_MODEL = 1024
BATCH = 8
SEQ = 2048
DEPTH = 1

CHUNK = 64
D_PLE = 256
D_A = 512
H_A = 4
DK_A = D_A // H_A
DV_A = D_A // H_A
D_B = 512
H_B = 4
DK_B = (D_B // 2) // H_B
DV_B = D_B // H_B
GLA_RANK = 16
GLA_TAU = 16
NORM_EPS = 1e-6
IN_WIDTHS = (D_A, D_A, D_A, D_A,
             H_B * DK_B, H_B * DK_B, D_B, D_B,
             GLA_RANK,
             D_MODEL, D_MODEL)
D_IN = 4 * D_A + 2 * H_B * DK_B + 2 * D_B + GLA_RANK + 2 * D_MODEL

kernel_name = "hgrn2_gla_gated_hybrid_block"


def rms_norm(x, gain):
    xf = x.astype(jnp.float32)
    y = xf * lax.rsqrt(jnp.mean(xf * xf, axis=-1, keepdims=True) + NORM_EPS)
    return (y * gain.astype(jnp.float32)).astype(x.dtype)


def head_rms_norm(o, gain, n_heads):
    b, l, d = o.shape
    oh = o.reshape(b, l, n_heads, d // n_heads)
    return rms_norm(oh, gain.reshape(n_heads, d // n_heads)).reshape(b, l, d)


def gated_linear_attention_chunked(q, k, v, log_g):
    bsz, seq, heads, dk = q.shape
    dv = v.shape[-1]
    n_chunks = seq // CHUNK

    def to_chunks(t):
        return t.astype(jnp.float32).reshape(bsz, n_chunks, CHUNK, heads, t.shape[-1]).transpose(1, 0, 3, 2, 4)

    qc, kc, vc, gc = to_chunks(q), to_chunks(k), to_chunks(v), to_chunks(log_g)
    causal = jnp.tril(jnp.ones((CHUNK, CHUNK), dtype=bool))[:, :, None]

    def step(state, inp):
        q_, k_, v_, g_ = inp
        b = jnp.cumsum(g_, axis=2)
        o_inter = jnp.einsum('bhcd,bhde->bhce', q_ * jnp.exp(b), state)
        diff = b[:, :, :, None, :] - b[:, :, None, :, :]
        decay = jnp.exp(jnp.where(causal, diff, -jnp.inf))
        scores = jnp.sum(q_[:, :, :, None, :] * k_[:, :, None, :, :] * decay, axis=-1)
        o_intra = jnp.einsum('bhij,bhje->bhie', scores, v_)
        b_last = b[:, :, -1:, :]
        new_state = (jnp.exp(b_last[:, :, 0, :])[..., None] * state
                     + jnp.einsum('bhcd,bhce->bhde', k_ * jnp.exp(b_last - b), v_))
        return new_state, o_inter + o_intra

    s0 = jnp.zeros((bsz, heads, dk, dv), jnp.float32)
    _, o = lax.scan(step, s0, (qc, kc, vc, gc))
    return o.transpose(1, 0, 3, 2, 4).reshape(bsz, seq, heads, dv).astype(v.dtype)


def setup_inputs(seed: int = 0) -> dict:
    key = jax.random.key(seed)
    ks = jax.random.split(key, 18)
    f32 = jnp.float32
    nrm = lambda k, shape, s: jax.random.normal(k, shape, f32) * s
    return {
        "x": nrm(ks[0], (BATCH, SEQ, D_MODEL), 1.0),
        "p": nrm(ks[1], (DEPTH, BATCH, SEQ, D_PLE), 1.0),
        "w_in": nrm(ks[2], (DEPTH, D_MODEL, D_IN), D_MODEL ** -0.5),
        "lb_logits": nrm(ks[3], (DEPTH + 1, D_A), 0.5),
        "w_gla_up": nrm(ks[4], (DEPTH, GLA_RANK, H_B * DK_B), GLA_RANK ** -0.5),
        "b_gla": nrm(ks[5], (DEPTH, H_B * DK_B), 0.1),
        "norm_pre": 1.0 + nrm(ks[6], (DEPTH, D_MODEL), 0.02),
        "norm_post": 1.0 + nrm(ks[7], (DEPTH, D_MODEL), 0.02),
        "head_norm_a": 1.0 + nrm(ks[8], (DEPTH, D_A), 0.02),
        "head_norm_b": 1.0 + nrm(ks[9], (DEPTH, D_B), 0.02),
        "w_branch_a": nrm(ks[10], (DEPTH, D_A, D_MODEL), D_A ** -0.5),
        "w_branch_b": nrm(ks[11], (DEPTH, D_B, D_MODEL), D_B ** -0.5),
        "w_out": nrm(ks[12], (DEPTH, D_MODEL, D_MODEL), D_MODEL ** -0.5),
        "w_ple": nrm(ks[13], (DEPTH, D_PLE, D_MODEL), D_PLE ** -0.5),
        "w_ple_gate": nrm(ks[14], (DEPTH, D_MODEL, D_MODEL), D_MODEL ** -0.5),
        "norm_ple": 1.0 + nrm(ks[15], (DEPTH, D_MODEL), 0.02),
    }


def reference(x, p, w_in, lb_logits, w_gla_up, b_gla, norm_pre, norm_post,
              head_norm_a, head_norm_b, w_branch_a, w_branch_b, w_out,
              w_ple, w_ple_gate, norm_ple):
    bsz, seq, _ = x.shape
    split_points = [int(s) for s in np.cumsum(IN_WIDTHS)[:-1]]
    lower_bounds = jnp.cumsum(jax.nn.softmax(lb_logits.astype(jnp.float32), axis=0), axis=0)
    for i in range(DEPTH):
        h = rms_norm(x, norm_pre[i])
        proj = h @ w_in[i]
        qA, fA, iA, gA, qB, kB, vB, gB, aB, zA, zB = jnp.split(proj, split_points, axis=-1)

        lb = lower_bounds[i]
        forget = lb + (1.0 - lb) * jax.nn.sigmoid(fA.astype(jnp.float32))
        log_f = jnp.log(forget)
        k_a = 1.0 - forget
        q_a = jax.nn.silu(qA) * (DK_A ** -0.5)
        o_a = gated_linear_attention_chunked(
            q_a.reshape(bsz, seq, H_A, DK_A), k_a.reshape(bsz, seq, H_A, DK_A),
            iA.reshape(bsz, seq, H_A, DV_A), log_f.reshape(bsz, seq, H_A, DK_A)
        ).reshape(bsz, seq, D_A)
        o_a = head_rms_norm(o_a, head_norm_a[i], H_A) * jax.nn.silu(gA)

        log_alpha = jax.nn.log_sigmoid((aB @ w_gla_up[i] + b_gla[i]).astype(jnp.float32)) / GLA_TAU
        o_b = gated_linear_attention_chunked(
            (qB * (DK_B ** -0.5)).reshape(bsz, seq, H_B, DK_B), kB.reshape(bsz, seq, H_B, DK_B),
            vB.reshape(bsz, seq, H_B, DV_B), log_alpha.reshape(bsz, seq, H_B, DK_B)
        ).reshape(bsz, seq, D_B)
        o_b = head_rms_norm(o_b, head_norm_b[i], H_B) * jax.nn.silu(gB)

        y = jax.nn.sigmoid(zA) * (o_a @ w_branch_a[i]) + jax.nn.sigmoid(zB) * (o_b @ w_branch_b[i])
        y = y @ w_out[i]
        x = x + rms_norm(y, norm_post[i])

        e = rms_norm(p[i] @ w_ple[i], norm_ple[i])
        x = x + jax.nn.sigmoid(x @ w_ple_gate[i]) * e
    return x
```

```python
import contextlib
import sys
import numpy as np
import concourse.bass as bass
import concourse.mybir as mybir
from concourse.alu_op_type import AluOpType as ALU
from concourse.bass_utils import run_bass_kernel_spmd

F32 = mybir.dt.float32
BF16 = mybir.dt.bfloat16
AF = mybir.ActivationFunctionType
AX = mybir.AxisListType

ENGS = ("sp", "act", "pool", "dve", "pe")

T = 2048
D = 1024
NT = 16
NS = 4
DIN = 5648
EPS = 1e-6
TSWITCH = 1.3


class Op:
    __slots__ = ("idx", "eng", "fn", "reads", "writes", "raw", "oth", "need_inc",
                 "dma_key", "dma_cnt", "inc_cnt", "cost", "lat", "tset", "deps", "prio", "line", "start", "bind", "dbytes")


def c_pe(cols):
    return sum(n / 2400.0 + 0.008 for n in cols)


def c_act(n, naps=0, acc=False):
    return 0.1 + n / 1100.0 + 0.06 * naps + (0.09 if acc else 0.0)


def c_dve(n, scan=False):
    return 0.15 + (2 * n if scan else n) / 960.0


def c_pool(n):
    return 0.3 + n / 520.0


class Prog:
    def __init__(self):
        self.ops = []
        self.last_writer = {}
        self.readers = {}
        self.dma_counts = {}
        self.last_dma = {}
        self.last_compute = {}
        self.pending = {e: set() for e in ENGS}
        self.max_ops = None
        self.marks = []
        self.do_schedule = True
        self.has_succ = set()
        self.bar_start = 0
        self.join = None
        self.join_fn = None

    def mark(self, name):
        self.marks.append((name, len(self.ops)))

    def add(self, eng, fn, reads=(), writes=(), dma_key=None, force=False, cost=0.3, lat=None, tset=None):
        if self.max_ops is not None and len(self.ops) >= self.max_ops and not force:
            return None
        op = Op()
        op.idx = len(self.ops)
        op.line = sys._getframe(1).f_lineno if sys._getframe(1).f_code.co_name not in ("dma", "mm_acc") \
            else sys._getframe(2).f_lineno
        op.eng = eng
        op.fn = fn
        op.reads = tuple(reads)
        op.writes = tuple(writes)
        op.need_inc = False
        op.dma_key = dma_key
        op.dma_cnt = 0
        op.inc_cnt = 0
        op.cost = cost
        op.lat = cost if lat is None else lat
        op.tset = tset
        op.dbytes = 0
        raw, oth = set(), set()
        for r in op.reads:
            if r in self.last_writer:
                raw.add(self.last_writer[r])
        for w in op.writes:
            if w in self.last_writer:
                oth.add(self.last_writer[w])
            for rd in self.readers.get(w, ()):
                oth.add(rd)
        for w in op.writes:
            self.last_writer[w] = op.idx
            self.readers[w] = []
        for r in op.reads:
            if r not in op.writes:
                self.readers.setdefault(r, []).append(op.idx)
        if self.pending[eng]:
            raw |= self.pending[eng]
            self.pending[eng] = set()
        if self.join is not None:
            raw.add(self.join)
        raw.discard(op.idx)
        oth.discard(op.idx)
        op.raw = raw
        op.oth = oth - raw
        self.has_succ |= raw
        self.has_succ |= oth
        if dma_key is not None:
            self.dma_counts[dma_key] = self.dma_counts.get(dma_key, 0) + 1
            op.dma_cnt = self.dma_counts[dma_key]
            self.last_dma[dma_key] = op.idx
        elif fn is not None:
            self.last_compute[eng] = op.idx
        self.ops.append(op)
        return op

    def barrier(self):
        sinks = set(i for i in range(self.bar_start, len(self.ops)) if i not in self.has_succ)
        self.pending["pool"] |= sinks
        j = self.add("pool", self.join_fn, cost=0.3, force=True)
        self.join = j.idx
        self.bar_start = j.idx

    def _sync_deps(self, op):
        out = []
        for d in sorted(op.raw | op.oth):
            dop = self.ops[d]
            if dop.dma_key is None and dop.eng == op.eng and op.dma_key is None and op.eng == "pe":
                continue
            out.append(dop)
        return out

    def schedule(self):
        ops = self.ops
        n = len(ops)
        succ = [[] for _ in range(n)]
        ndeps = [0] * n
        for op in ops:
            op.deps = sorted(op.raw | op.oth)
            ndeps[op.idx] = len(op.deps)
            for d in op.deps:
                succ[d].append(op.idx)
        for op in reversed(ops):
            best = 0.0
            for s in succ[op.idx]:
                best = max(best, ops[s].prio)
            op.prio = best + op.lat
        order = {e: [] for e in ENGS}
        if not self.do_schedule:
            for op in ops:
                order[op.eng].append(op)
            return order
        finish = [0.0] * n
        free = {e: 0.0 for e in ENGS}
        dma_busy = [0.0]
        cur_tset = [None]
        ready = {e: [] for e in ENGS}
        for op in ops:
            if ndeps[op.idx] == 0:
                ready[op.eng].append(op)
        left = n
        while left:
            best = None
            for e in ENGS:
                for op in ready[e]:
                    r = 0.0
                    rb = None
                    for d in op.deps:
                        f = finish[d] + (0.0 if (ops[d].eng == e and ops[d].dma_key is None) else 0.15)
                        if f > r:
                            r = f
                            rb = d
                    s = max(free[e], r)
                    op.bind = rb if r >= free[e] else -1
                    if e == "act" and op.tset is not None and cur_tset[0] is not None and cur_tset[0] != op.tset:
                        s += TSWITCH
                    key = (s, -op.prio, op.idx)
                    if best is None or key < best[0]:
                        best = (key, op)
            (s, _, _), op = best
            e = op.eng
            ready[e].remove(op)
            if e == "act" and op.tset is not None:
                cur_tset[0] = op.tset
            if op.bind == -1:
                op.bind = ("eng", order[e][-1].idx if order[e] else None)
            else:
                op.bind = ("dep", op.bind)
            op.start = s
            free[e] = s + op.cost
            if op.dma_key is not None and op.dbytes:
                t0 = max(s + 1.5, dma_busy[0])
                dma_busy[0] = t0 + op.dbytes / 260e3
                finish[op.idx] = dma_busy[0] + 0.5
            else:
                finish[op.idx] = s + op.lat
            order[e].append(op)
            left -= 1
            for sidx in succ[op.idx]:
                ndeps[sidx] -= 1
                if ndeps[sidx] == 0:
                    ready[ops[sidx].eng].append(ops[sidx])
        self.est_time = max(finish) if n else 0.0
        self.finish = finish
        self.busy = {e: sum(op.cost for op in order[e]) for e in ENGS}
        return order

    def emit(self, nc, stack):
        order = self.schedule()
        for op in self.ops:
            for d in self._sync_deps(op):
                if d.dma_key is None:
                    d.need_inc = True
        eng_sem = {e: stack.enter_context(nc.semaphore("s_" + e)) for e in ENGS}
        key_sem = {k: stack.enter_context(nc.semaphore("d_" + k)) for k in self.dma_counts}
        for e in ENGS:
            c = 0
            for op in order[e]:
                if op.dma_key is None and op.need_inc:
                    c += 1
                    op.inc_cnt = c
        block = stack.enter_context(nc.Block())
        prog = self

        def run(engname, eng):
            waited = {}
            for op in order[engname]:
                for d in prog._sync_deps(op):
                    if d.dma_key is not None:
                        sem, val, k = key_sem[d.dma_key], 16 * d.dma_cnt, "d_" + d.dma_key
                    else:
                        sem, val, k = eng_sem[d.eng], d.inc_cnt, "s_" + d.eng
                    if waited.get(k, 0) >= val:
                        continue
                    waited[k] = val
                    eng.wait_ge(sem, val)
                if op.fn is None:
                    continue
                ins = op.fn(eng)
                if op.dma_key is not None:
                    ins.then_inc(key_sem[op.dma_key], 16)
                elif op.need_inc:
                    ins.then_inc(eng_sem[op.eng], 1)

        @block.sync
        def _(e):
            run("sp", e)

        @block.scalar
        def _(e):
            run("act", e)

        @block.gpsimd
        def _(e):
            run("pool", e)

        @block.vector
        def _(e):
            run("dve", e)

        @block.tensor
        def _(e):
            run("pe", e)


class Arena:
    def __init__(self, nc, name, nbytes, ap=None):
        self.words = nbytes // 4
        if ap is None:
            self.t = nc.alloc_sbuf_tensor(name, [128, self.words], F32)
            self.ap = self.t.ap()
        else:
            self.ap = ap
        self.off = 0
        self.hi = 0
        self.last = (0, 0)

    def sub(self, rng):
        off, words = rng
        return Arena(None, None, words * 4, ap=self.ap[:, off:off + words])

    def mark(self):
        return self.off

    def reset(self, m):
        self.off = m

    def alloc(self, free_shape, dt):
        n = 1
        for s in free_shape:
            n *= s
        words = n if dt == F32 else (n + 1) // 2
        words = (words + 7) // 8 * 8
        assert self.off + words <= self.words, ("arena overflow", self.off, words, self.words)
        v = self.ap[:, self.off:self.off + words]
        self.last = (self.off, words)
        self.off += words
        self.hi = max(self.hi, self.off)
        if dt != F32:
            v = v.bitcast(dt)
        v = v[:, 0:n]
        if len(free_shape) == 2:
            v = v.rearrange("p (a b) -> p a b", a=free_shape[0])
        elif len(free_shape) == 3:
            v = v.rearrange("p (a b c) -> p a b c", a=free_shape[0], b=free_shape[1])
        return v


def build_nc(debug=False, max_ops=None, do_schedule=True):
    nc = bass.Bass("TRN2", target_bir_lowering=False)

    def din(name, shape):
        return nc.dram_tensor(name, list(shape), F32, kind="ExternalInput").ap()

    x_d = din("x", [T, D])
    p_d = din("p", [T, 256])
    win_d = din("w_in", [D, DIN])
    wa_d = din("w_a", [512, D])
    wb_d = din("w_b", [512, D])
    wout_d = din("w_out", [D, D])
    wple_d = din("w_ple", [256, D])
    wg_d = din("w_g", [D, D])
    wup_d = din("w_up", [16, 256])
    lbl_d = din("lbl", [128, 8])
    gpre_d = din("gpre", [1, D])
    hna_d = din("hna", [128, 4])
    hnb_d = din("hnb", [128, 4])
    bgl_d = din("bgl", [128, 2])
    gpost_d = din("gpost", [1, D])
    gple_d = din("gple", [1, D])
    ident_d = din("ident", [128, 128])
    maskbd_d = din("maskbd", [128, 128])
    mask01_d = din("mask01", [128, 512])
    out_d = nc.dram_tensor("out", [T, D], F32, kind="ExternalOutput").ap()
    if debug:
        dbg_a = nc.dram_tensor("dbg_a", [128, 4 * T], F32, kind="ExternalOutput").ap()
        dbg_b = nc.dram_tensor("dbg_b", [128, 4 * T], F32, kind="ExternalOutput").ap()

    win_v = win_d.rearrange("(k p) n -> p k n", p=128)
    wa_v = wa_d.rearrange("(k p) n -> p k n", p=128)
    wb_v = wb_d.rearrange("(k p) n -> p k n", p=128)
    wout_v = wout_d.rearrange("(k p) n -> p k n", p=128)
    wple_v = wple_d.rearrange("(k p) n -> p k n", p=128)
    wg_v = wg_d.rearrange("(k p) n -> p k n", p=128)

    P = Prog()
    P.max_ops = max_ops
    P.do_schedule = do_schedule
    with contextlib.ExitStack() as st:
        ar = Arena(nc, "arena", 207 * 1024 + 768)
        banks = [st.enter_context(nc.psum_tensor("bank%d" % i, [128, 512], F32)) for i in range(8)]

        def bankf(i):
            return banks[i][:]

        def bankb(i):
            return banks[i][:].bitcast(BF16)

        hT = ar.alloc([8, T], BF16)
        hT_rng = ar.last
        oTa = ar.alloc([4, T], BF16)
        oTa_rng = ar.last
        oTb = ar.alloc([4, T], BF16)
        oTb_rng = ar.last
        ident = ar.alloc([128], BF16)
        maskbd = ar.alloc([128], F32)
        mask01 = ar.alloc([512], F32)
        sm_lbl = ar.alloc([8], F32)
        hna = ar.alloc([4], F32)
        hnb = ar.alloc([4], F32)
        hna2 = ar.alloc([4], F32)
        hnb2 = ar.alloc([4], F32)
        bgl = ar.alloc([2], F32)
        nbgl = ar.alloc([2], F32)
        c1 = ar.alloc([4], F32)
        nc1 = ar.alloc([4], F32)
        lnc1 = ar.alloc([4], F32)
        c0 = ar.alloc([4], F32)
        lbd = ar.alloc([4], F32)
        lbt = ar.alloc([4], F32)
        epsc = ar.alloc([1], F32)
        onec = ar.alloc([1], F32)
        nhalf = ar.alloc([8], F32)
        wup = ar.alloc([256], BF16)
        stage = None
        bufX = ar.alloc([8, 2048], BF16)
        bufX_rng = ar.last
        bufY = ar.alloc([8, 1552], BF16)
        bufY_rng = ar.last
        dummy = ar.alloc([8], F32)
        dummy2 = ar.alloc([8], F32)
        P.join_fn = lambda e: e.memset(dummy, 0.0)
        work0 = ar.mark()

        ukey = [0]

        def dma(dst, src, key=None, reads=(), writes=(), q="sp", nbytes=65536):
            if key is None:
                ukey[0] += 1
                key = "u%d" % ukey[0]
            lat = 2.0 + nbytes * 128 / 200e3
            occ = 0.45 if q == "sp" else 1.1
            o_ = P.add(q, lambda e: e.dma_start(out=dst, in_=src), reads=reads, writes=writes, dma_key=key,
                       cost=occ, lat=lat)
            if o_ is not None:
                nel = 1
                for s_ in src.shape:
                    nel *= s_
                o_.dbytes = max(nel * 4, 16384)

        stage_ctr = [0]

        def load_cast(dst, src, ncols, scale, dst_res, eng="pool", extra_reads=()):
            c = 0
            while c < ncols:
                w = min(1024, ncols - c)
                sl = stage_ctr[0] % 2
                stage_ctr[0] += 1
                sres = "stg%d" % sl
                sv = stage[sl][:, 0:w]
                dma(sv, src[:, c:c + w], sres, writes=[sres], nbytes=4 * w)
                dv = dst[:, c:c + w]
                if eng == "pool":
                    P.add("pool", lambda e, dv=dv, sv=sv: e.tensor_scalar(out=dv, in0=sv, scalar1=scale, scalar2=0.0,
                                                                          op0=ALU.mult, op1=ALU.add),
                          reads=[sres] + list(extra_reads), writes=[dst_res], cost=c_pool(w) * 0.6)
                else:
                    P.add("dve", lambda e, dv=dv, sv=sv: e.tensor_scalar(out=dv, in0=sv, scalar1=scale, scalar2=None,
                                                                         op0=ALU.mult),
                          reads=[sres] + list(extra_reads), writes=[dst_res], cost=c_dve(w))
                c += w

        def mm_acc(out, pairs, reads, writes):
            def fn(e):
                n = len(pairs)
                for i, (l, r) in enumerate(pairs):
                    ins = e.matmul(out, lhsT=l, rhs=r, start=(i == 0), stop=(i == n - 1))
                return ins
            cols = [r.shape[-1] for (_, r) in pairs]
            P.add("pe", fn, reads=reads, writes=writes, cost=c_pe(cols))

        dma(ident, ident_d, reads=(), writes=["ident"], q="pool", nbytes=256)
        dma(wup[0:16, :], wup_d, writes=["wup"], q="pool", nbytes=512)
        dma(maskbd, maskbd_d, writes=["maskbd"], nbytes=512)
        dma(mask01, mask01_d, writes=["mask01"], nbytes=2048)
        dma(sm_lbl, lbl_d, writes=["lbl"], nbytes=32)
        dma(hna, hna_d, writes=["hna"], nbytes=16)
        dma(hnb, hnb_d, writes=["hnb"], nbytes=16)
        dma(bgl, bgl_d, writes=["bgl"], nbytes=8)
        P.add("pool", lambda e: e.memset(epsc, EPS), writes=["epsc"], cost=0.3)
        P.add("pool", lambda e: e.memset(onec, 1.0), writes=["onec"], cost=0.3)
        P.add("pool", lambda e: e.memset(nhalf, -0.5), writes=["nhalf"], cost=0.3)
        P.add("dve", lambda e: e.tensor_tensor(out=lbd, in0=sm_lbl[:, 0:4], in1=sm_lbl[:, 4:8], op=ALU.subtract),
              reads=["lbl"], writes=["lbd"], cost=0.2)
        P.add("act", lambda e: e.activation(out=lbt, in_=lbd, func=AF.Tanh, scale=0.5), reads=["lbd"], writes=["lbt"],
              cost=0.3, tset="tanh")
        P.add("dve", lambda e: e.tensor_scalar(out=c1, in0=lbt, scalar1=-0.25, scalar2=0.25, op0=ALU.mult, op1=ALU.add),
              reads=["lbt"], writes=["c1"], cost=0.2)
        P.add("dve", lambda e: e.tensor_scalar(out=c0, in0=lbt, scalar1=0.25, scalar2=0.75, op0=ALU.mult, op1=ALU.add),
              reads=["lbt"], writes=["c0"], cost=0.2)
        P.add("dve", lambda e: e.tensor_scalar(out=nc1, in0=lbt, scalar1=0.25, scalar2=-0.25, op0=ALU.mult, op1=ALU.add),
              reads=["lbt"], writes=["nc1"], cost=0.2)
        P.add("act", lambda e: e.activation(out=lnc1, in_=c1, func=AF.Ln), reads=["c1"], writes=["lnc1"], cost=0.3,
              tset="ln")
        P.add("dve", lambda e: e.tensor_scalar(out=nbgl, in0=bgl, scalar1=-1.0, scalar2=None, op0=ALU.mult),
              reads=["bgl"], writes=["nbgl"], cost=0.2)
        P.add("dve", lambda e: e.tensor_scalar(out=hna2, in0=hna, scalar1=1.0, scalar2=None, op0=ALU.mult),
              reads=["hna"], writes=["hna2"], cost=0.2)
        P.add("dve", lambda e: e.tensor_scalar(out=hnb2, in0=hnb, scalar1=1.0, scalar2=None, op0=ALU.mult),
              reads=["hnb"], writes=["hnb2"], cost=0.2)
        P.mark("consts_done")

        a0 = ar.sub(bufY_rng)
        gpre_bc = a0.alloc([D], F32)
        xs = [a0.alloc([D], F32) for _ in range(3)]
        hb = [a0.alloc([D], BF16) for _ in range(2)]
        junk0 = a0.alloc([D], BF16)
        st0 = [a0.alloc([4], F32) for _ in range(2)]
        dma(gpre_bc, gpre_d.partition_broadcast(128).rearrange("p a b -> p (a b)"), writes=["gpre_bc"], nbytes=4096)
        ph0_res = ["gpre_bc", "junk0"]

        def wa_load(grp, reads=()):
            c0_ = grp * 512
            dma(bufX[:, :, c0_:c0_ + 512], win_v[:, :, c0_:c0_ + 512], reads=reads, writes=["bufX_%d" % grp],
                q="pool", nbytes=8 * 1024)


        for t in range(NT):
            sl3 = t % 3
            sl = t % 2
            xr, hr, sr = "xs%d" % sl3, "hb%d" % sl, "st0_%d" % sl
            ph0_res += [xr, hr, sr + "a", sr + "b", sr + "c"]
            tpb = bankb(6 + sl).rearrange("p (k c) -> p k c", k=8)
            dma(xs[sl3], x_d[t * 128:(t + 1) * 128, :], xr, writes=[xr], nbytes=4096)
            if t < 4:
                wa_load(t)
            P.add("act", lambda e, sl=sl, sl3=sl3: e.activation(out=junk0, in_=xs[sl3], func=AF.Square,
                                                               accum_out=st0[sl][:, 0:1]),
                  reads=[xr], writes=["junk0", sr + "a"], cost=c_act(D, acc=True))
            P.add("act", lambda e, sl=sl: e.activation(out=st0[sl][:, 1:2], in_=st0[sl][:, 0:1], func=AF.Ln,
                                                       scale=1.0 / D, bias=epsc[:, 0:1]),
                  reads=[sr + "a", "epsc"], writes=[sr + "b"], cost=0.3, tset="ln")
            P.add("act", lambda e, sl=sl: e.activation(out=st0[sl][:, 2:3], in_=st0[sl][:, 1:2], func=AF.Exp,
                                                       scale=-0.5),
                  reads=[sr + "b"], writes=[sr + "c"], cost=0.3)
            P.add("dve", lambda e, sl=sl, sl3=sl3: e.scalar_tensor_tensor(out=hb[sl], in0=xs[sl3],
                                                                         scalar=st0[sl][:, 2:3], in1=gpre_bc,
                                                                         op0=ALU.mult, op1=ALU.mult),
                  reads=[xr, sr + "c", "gpre_bc"], writes=[hr], cost=c_dve(D))

            def tr(e, sl=sl, tpb=tpb):
                for k in range(8):
                    ins = e.transpose(out=tpb[:, k, :], in_=hb[sl][:, k * 128:(k + 1) * 128], identity=ident)
                return ins
            P.add("pe", tr, reads=[hr, "ident"], writes=["bank%d" % (6 + sl)], cost=8 * 0.08)
            P.add("dve", lambda e, t=t, tpb=tpb: e.tensor_copy(out=hT[:, :, t * 128:(t + 1) * 128], in_=tpb),
                  reads=["bank%d" % (6 + sl)], writes=["hT%d" % (t // 4)], cost=c_dve(D) * 0.7)
        P.mark("phase0_done")

        ar.reset(work0)
        NB = 4
        tmp = [[ar.alloc([512], F32) for _ in range(6)] for _ in range(2)]
        qTs = [ar.alloc([4, 512], BF16) for _ in range(2)]
        kTs = [ar.alloc([NB, 512], BF16) for _ in range(2)]
        ecs = [ar.alloc([NB, 8], F32) for _ in range(3)]
        NBT = 5
        vbfs = [ar.alloc([512], BF16) for _ in range(NBT)]
        Gs = [ar.alloc([512], F32) for _ in range(NBT)]
        tgs = Gs
        kts = [ar.alloc([4, 128], BF16) for _ in range(NBT)]
        smks = [ar.alloc([4, 128], BF16) for _ in range(NBT)]
        onbs = [ar.alloc([512], BF16) for _ in range(NBT)]
        on1 = ar.alloc([4, 128], F32)
        Tts = [ar.alloc([NB, 128], F32) for _ in range(2)]
        Sbs = [ar.alloc([NB, 128], BF16) for _ in range(2)]
        st4s = [ar.alloc([16], F32) for _ in range(5)]
        aTs = [ar.alloc([512], BF16) for _ in range(2)]
        junk4 = ar.alloc([4, 128], BF16)

        def gla_branch(br, prefetch=None):
            isA = br == "A"
            nblk = 4 if isA else 2
            dk = 128 if isA else 64
            W = bufX if isA else bufY
            oT = oTa if isA else oTb
            ores = "oTa" if isA else "oTb"
            qc0, fc0, vc0, gc0 = (0, 512, 1024, 1536) if isA else (0, 256, 512, 1024)
            if isA:
                wq, wf, wv, wg_ = ["bufX_0"], ["bufX_1"], ["bufX_2"], ["bufX_3"]
            else:
                wq, wf, wv, wg_ = ["bufY_0"], ["bufY_0"], ["bufY_1"], ["bufY_2"]

            P.add("pool", lambda e: e.memset(Tts[1], 0.0), writes=["Tt1_%d" % b for b in range(NB)], cost=c_pool(512))
            P.add("pool", lambda e: e.memset(Sbs[0], 0.0), writes=["Sb0"], cost=c_pool(256))
            for i in range(3):
                P.add("pool", lambda e, i=i: e.memset(ecs[i], 0.0), writes=["ec%d" % i], cost=0.3)
            if not isA:
                for i in range(2):
                    P.add("pool", lambda e, i=i: e.memset(qTs[i], 0.0), writes=["qT%d" % i], cost=c_pool(1024))
                for i in range(NBT):
                    P.add("pool", lambda e, i=i: e.memset(kts[i], 0.0), writes=["kt%d" % i], cost=c_pool(256))

            def prep(s):
                ss = s % 2
                qT, kT, ec = qTs[ss], kTs[ss], ecs[s % 3]
                qres, kres, ecres = "qT%d" % ss, "kT%d" % ss, "ec%d" % (s % 3)
                tok = slice(s * 512, (s + 1) * 512)
                hres = "hT%d" % s
                aT = aTs[ss]
                if not isA:
                    mm_acc(bankf(4)[0:16, :], [(W[:, k, 1536:1552], hT[:, k, tok]) for k in range(8)],
                           reads=["bufY_3", hres], writes=["bank4"])
                    P.add("act", lambda e, aT=aT: e.activation(out=aT[0:16, :], in_=bankf(4)[0:16, :], func=AF.Copy),
                          reads=["bank4"], writes=["aT%d" % ss], cost=c_act(512))
                for blk in range(nblk):
                    ts_ = (s * nblk + blk) % 2
                    t1, t2, t3, t4, Ebt, Enb = tmp[ts_]
                    r1, r2, r3, r4, rE, rN = ["tmp%d_%d" % (ts_, i) for i in range(6)]
                    qcol = slice(qc0 + blk * 128, qc0 + (blk + 1) * 128)
                    fcol = slice(fc0 + blk * 128, fc0 + (blk + 1) * 128)
                    mm_acc(bankf(0), [(W[:, k, qcol], hT[:, k, tok]) for k in range(8)],
                           reads=wq + [hres], writes=["bank0"])
                    mm_acc(bankf(1), [(W[:, k, fcol], hT[:, k, tok]) for k in range(8)],
                           reads=wf + [hres], writes=["bank1"])
                    if isA:
                        P.add("act", lambda e, t1=t1: e.activation(out=t1, in_=bankf(0), func=AF.Silu),
                              reads=["bank0"], writes=[r1], cost=c_act(512), tset="tanh")
                        P.add("act", lambda e, t2=t2: e.activation(out=t2, in_=bankf(1), func=AF.Tanh, scale=0.5),
                              reads=["bank1"], writes=[r2], cost=c_act(512), tset="tanh")
                        P.add("act", lambda e, blk=blk, t2=t2, t3=t3: e.activation(
                            out=t3, in_=t2, func=AF.Ln, scale=c1[:, blk:blk + 1], bias=c0[:, blk:blk + 1]),
                            reads=[r2, "c1", "c0"], writes=[r3], cost=c_act(512, 2), tset="ln")
                    else:
                        mm_acc(bankf(4), [(wup[0:16, blk * 128:(blk + 1) * 128], aT[0:16, :])],
                               reads=["wup", "aT%d" % ss], writes=["bank4"])
                        P.add("act", lambda e, blk=blk, t1=t1: e.activation(out=t1, in_=bankf(4), func=AF.Exp, scale=-1.0,
                                                                            bias=nbgl[:, blk:blk + 1]),
                              reads=["bank4", "nbgl"], writes=[r1], cost=c_act(512, 1))
                        P.add("act", lambda e, t1=t1, t3=t3: e.activation(out=t3, in_=t1, func=AF.Ln, bias=onec[:, 0:1]),
                              reads=[r1, "onec"], writes=[r3], cost=c_act(512, 1), tset="ln")
                    P.add("dve", lambda e, t3=t3, t4=t4: e.tensor_tensor_scan(out=t4, data0=mask01, data1=t3, initial=0.0,
                                                                              op0=ALU.mult, op1=ALU.add),
                          reads=[r3, "mask01"], writes=[r4], cost=c_dve(512, scan=True))
                    sE, sN = (1.0, -1.0) if isA else (-1.0 / 16, 1.0 / 16)
                    P.add("act", lambda e, t4=t4, Ebt=Ebt, sE=sE: e.activation(out=Ebt, in_=t4, func=AF.Exp, scale=sE),
                          reads=[r4], writes=[rE], cost=c_act(512))
                    if isA:
                        P.add("act", lambda e, t4=t4, Enb=Enb, blk=blk: e.activation(out=Enb, in_=t4, func=AF.Exp,
                                                                                     scale=-1.0, bias=lnc1[:, blk:blk + 1]),
                              reads=[r4, "lnc1"], writes=[rN], cost=c_act(512, 1))
                    else:
                        P.add("act", lambda e, t4=t4, Enb=Enb, sN=sN: e.activation(out=Enb, in_=t4, func=AF.Exp, scale=sN),
                              reads=[r4], writes=[rN], cost=c_act(512))
                    P.add("dve", lambda e, Ebt=Ebt, ec=ec, blk=blk: e.tensor_copy(
                        out=ec[:, blk, :], in_=Ebt.rearrange("p (c j) -> p c j", j=64)[:, :, 63]),
                        reads=[rE], writes=[ecres], cost=0.2)
                    if isA:
                        P.add("dve", lambda e, blk=blk, t1=t1, Ebt=Ebt, qT=qT: e.scalar_tensor_tensor(
                            out=qT[:, blk, :], in0=t1, scalar=-(dk ** -0.5), in1=Ebt, op0=ALU.mult, op1=ALU.mult),
                            reads=[r1, rE], writes=[qres], cost=c_dve(512))
                        P.add("dve", lambda e, blk=blk, t2=t2, Enb=Enb, kT=kT: e.scalar_tensor_tensor(
                            out=kT[:, blk, :], in0=t2, scalar=-1.0, in1=Enb, op0=ALU.add, op1=ALU.mult),
                            reads=[r2, rN, r3], writes=[kres], cost=c_dve(512))
                    else:
                        for hh in range(2):
                            pq = slice(hh * 64, (hh + 1) * 64)
                            P.add("dve", lambda e, blk=blk, hh=hh, pq=pq, Ebt=Ebt, qT=qT: e.scalar_tensor_tensor(
                                out=qT[pq, 2 * blk + hh, :], in0=bankf(0)[pq, :], scalar=dk ** -0.5,
                                in1=Ebt[pq, :], op0=ALU.mult, op1=ALU.mult),
                                reads=["bank0", rE], writes=[qres], cost=c_dve(512))
                        P.add("dve", lambda e, blk=blk, Enb=Enb, kT=kT: e.tensor_tensor(out=kT[:, blk, :], in0=bankf(1),
                                                                                       in1=Enb, op=ALU.mult),
                              reads=["bank1", rN], writes=[kres], cost=c_dve(512))

            def tile(t):
                s, tt = divmod(t, 4)
                ss = s % 2
                tp_ = t % NBT
                qT, kT, ec = qTs[ss], kTs[ss], ecs[s % 3]
                qres, kres, ecres = "qT%d" % ss, "kT%d" % ss, "ec%d" % (s % 3)
                vbf, tg, G, kt, smk, onb, st4 = vbfs[tp_], tgs[tp_], Gs[tp_], kts[tp_], smks[tp_], onbs[tp_], st4s[tp_]
                rv, rtg, rG, rkt, rsm, ron, rst = ["%s%d" % (n_, tp_) for n_ in ("vbf", "tg", "G", "kt", "smk", "onb", "st4")]
                hres = "hT%d" % s
                tl = slice(tt * 128, (tt + 1) * 128)
                tg_ = slice(t * 128, (t + 1) * 128)
                mm_acc(bankf(2), [(hT[:, k, tg_], W[:, k, vc0:vc0 + 512]) for k in range(8)],
                       reads=wv + [hres], writes=["bank2"])
                P.add("act", lambda e: e.activation(out=vbf, in_=bankf(2), func=AF.Copy),
                      reads=["bank2"], writes=[rv], cost=c_act(512))
                mm_acc(bankf(2), [(hT[:, k, tg_], W[:, k, gc0:gc0 + 512]) for k in range(8)],
                       reads=wg_ + [hres], writes=["bank2"])
                P.add("act", lambda e: e.activation(out=G, in_=bankf(2), func=AF.Silu),
                      reads=["bank2"], writes=[rG], cost=c_act(512), tset="tanh")
                tpk = bankb(3)[:, 0:nblk * 128].rearrange("p (b c) -> p b c", b=nblk)

                def trk(e):
                    for blk in range(nblk):
                        ins = e.transpose(out=tpk[:, blk, :], in_=kT[:, blk, tl], identity=ident)
                    return ins
                P.add("pe", trk, reads=[kres, "ident"], writes=["bank3"], cost=nblk * 0.08)
                if isA:
                    P.add("act", lambda e: e.activation(out=kt, in_=tpk, func=AF.Copy),
                          reads=["bank3"], writes=[rkt], cost=c_act(512))
                else:
                    ktv = kt.rearrange("p (b h) c -> p b h c", h=2)
                    for hh in range(2):
                        cs_ = slice(hh * 64, (hh + 1) * 64)
                        P.add("act", lambda e, hh=hh, cs_=cs_: e.activation(
                            out=ktv[:, :, hh, cs_], in_=tpk[:, :, cs_], func=AF.Copy),
                            reads=["bank3"], writes=[rkt], cost=c_act(128))
                scp = bankf(4).rearrange("p (h c) -> p h c", h=4)

                def scf(e):
                    for h in range(4):
                        blk = h if isA else h // 2
                        ins = e.matmul(scp[:, h, :], lhsT=kT[:, blk, tl], rhs=qT[:, h, tl], start=True, stop=True)
                    return ins
                P.add("pe", scf, reads=[kres, qres], writes=["bank4"], cost=c_pe([128] * 4))
                P.add("dve", lambda e: e.tensor_tensor(out=smk, in0=scp,
                                                       in1=maskbd.unsqueeze(1).to_broadcast([128, 4, 128]),
                                                       op=ALU.mult),
                      reads=["bank4", "maskbd"], writes=[rsm], cost=c_dve(512))
                op_ = bankf(5).rearrange("p (h c) -> p h c", h=4)

                def oin(e):
                    for h in range(4):
                        ins = e.matmul(op_[:, h, :], lhsT=smk[:, h, :], rhs=vbf[:, h * 128:(h + 1) * 128],
                                       start=(h == 0), stop=False, skip_group_check=True)
                    return ins
                P.add("pe", oin, reads=[rsm, rv], writes=["bank5"], cost=c_pe([128] * 4))
                pp = bankf(6)[:, 0:nblk * 128].rearrange("p (b c) -> p b c", b=nblk)
                for c in range(2):
                    cg = 2 * t + c
                    cl8 = 2 * tt + c
                    cl = slice(tt * 128 + c * 64, tt * 128 + (c + 1) * 64)
                    pr = slice(c * 64, (c + 1) * 64)
                    Sb_cur, Sb_nxt = Sbs[cg % 2], Sbs[(cg + 1) % 2]
                    rSc, rSn = "Sb%d" % (cg % 2), "Sb%d" % ((cg + 1) % 2)

                    def ointer(e, cl=cl, pr=pr, c=c, Sb_cur=Sb_cur):
                        for h in range(4):
                            blk = h if isA else h // 2
                            ins = e.matmul(op_[pr, h, :], lhsT=qT[:, h, cl], rhs=Sb_cur[:, blk, :],
                                           start=False, stop=(c == 1 and h == 3), skip_group_check=True)
                        return ins
                    P.add("pe", ointer, reads=[qres, rSc, "bank5"], writes=["bank5"], cost=c_pe([128] * 4))

                    def pst(e, pr=pr):
                        for h in range(4):
                            blk = h if isA else h // 2
                            first = True if isA else (h % 2 == 0)
                            last = True if isA else (h % 2 == 1)
                            ins = e.matmul(pp[:, blk, :], lhsT=kt[pr, h, :],
                                           rhs=vbf[pr, h * 128:(h + 1) * 128], start=first, stop=last)
                        return ins
                    P.add("pe", pst, reads=[rkt, rv], writes=["bank6"], cost=c_pe([128] * 4))
                    if cl8 == 0:
                        ecp, ecpres, pcol = ecs[(s - 1) % 3], "ec%d" % ((s - 1) % 3), 7
                    else:
                        ecp, ecpres, pcol = ec, ecres, cl8 - 1
                    T_prev, T_cur = Tts[(cg + 1) % 2], Tts[cg % 2]
                    rTp, rTc = "Tt%d" % ((cg + 1) % 2), "Tt%d" % (cg % 2)
                    for blk in range(nblk):
                        P.add("dve", lambda e, blk=blk, ecp=ecp, pcol=pcol, T_prev=T_prev, T_cur=T_cur:
                              e.scalar_tensor_tensor(out=T_cur[:, blk, :], in0=T_prev[:, blk, :],
                                                     scalar=ecp[:, blk, pcol:pcol + 1], in1=pp[:, blk, :],
                                                     op0=ALU.mult, op1=ALU.add),
                              reads=[rTp + "_%d" % blk, ecpres, "bank6"], writes=[rTc + "_%d" % blk],
                              cost=c_dve(128) + 0.07)
                    ebc = ec[:, 0:nblk, cl8:cl8 + 1].to_broadcast([128, nblk, 128])
                    P.add("pool", lambda e, ebc=ebc, Sb_nxt=Sb_nxt, T_cur=T_cur: e.tensor_tensor(
                        out=Sb_nxt[:, 0:nblk, :], in0=T_cur[:, 0:nblk, :], in1=ebc, op=ALU.mult),
                        reads=[rTc + "_%d" % b for b in range(nblk)] + [ecres], writes=[rSn],
                        cost=c_pool(128 * nblk))
                for h in range(4):
                    P.add("act", lambda e, h=h: e.activation(out=junk4[:, h, :], in_=op_[:, h, :], func=AF.Square,
                                                             accum_out=st4[:, h:h + 1]),
                          reads=["bank5"], writes=["junk4_%d" % h, rst + "a%d" % h], cost=c_act(128, acc=True))
                P.add("dve", lambda e: e.tensor_scalar(out=st4[:, 4:8], in0=st4[:, 0:4], scalar1=1.0 / 128,
                                                       scalar2=EPS, op0=ALU.mult, op1=ALU.add),
                      reads=[rst + "a%d" % h for h in range(4)], writes=[rst + "b"], cost=0.2)
                P.add("pool", lambda e: e.tensor_tensor(out=st4[:, 8:12], in0=st4[:, 4:8], in1=nhalf[:, 0:4],
                                                        op=ALU.pow),
                      reads=[rst + "b", "nhalf"], writes=[rst + "c"], cost=1.0)
                P.add("dve", lambda e: e.tensor_tensor(out=on1, in0=op_,
                                                       in1=st4[:, 8:12].unsqueeze(2).to_broadcast([128, 4, 128]),
                                                       op=ALU.mult),
                      reads=["bank5", rst + "c"], writes=["on1"], cost=c_dve(512))
                P.add("pool", lambda e: e.tensor_tensor(out=onb, in0=on1.rearrange("p h c -> p (h c)"), in1=G,
                                                        op=ALU.mult),
                      reads=["on1", rG], writes=[ron], cost=c_pool(512))
                tpo = bankb(7)[:, 512:1024].rearrange("p (b c) -> p b c", b=4)

                def tro(e):
                    for h in range(4):
                        ins = e.transpose(out=tpo[:, h, :], in_=onb[:, h * 128:(h + 1) * 128], identity=ident)
                    return ins
                P.add("pe", tro, reads=[ron, "ident"], writes=["bank7"], cost=4 * 0.08)
                hn2 = hna2 if isA else hnb2
                for h in range(4):
                    P.add("act", lambda e, h=h: e.activation(out=oT[:, h, tg_], in_=tpo[:, h, :], func=AF.Copy,
                                                             scale=hn2[:, h:h + 1]),
                          reads=["bank7", "hna2", "hnb2"], writes=[ores + "_%d" % h], cost=c_act(128, 1))

            prep(0)
            for s in range(NS):
                if s == 1 and prefetch is not None:
                    prefetch()
                if s + 1 < NS:
                    prep(s + 1)
                P.mark(br + "_tiles_s%d" % s)
                for tt in range(4):
                    tile(4 * s + tt)

        def prefetch_WB():
            grp = [(0, 512), (512, 512), (1024, 512), (1536, 16)]
            P.add("pool", lambda e: e.memset(dummy2, 0.0), writes=ph0_res + ["ph0_done"], cost=0.3)
            for i, (c0_, w) in enumerate(grp):
                dma(bufY[:, :, c0_:c0_ + w], win_v[:, :, 2048 + c0_:2048 + c0_ + w], reads=["ph0_done"],
                    writes=["bufY_%d" % i], q="pool", nbytes=8 * 2 * w)

        def prefetch_WZ():
            for i in range(4):
                dma(bufX[:, :, i * 512:(i + 1) * 512], win_v[:, :, 3600 + i * 512:3600 + (i + 1) * 512],
                    writes=["bufX_%d" % i], q="pool", nbytes=8 * 1024)

        gla_branch("A", prefetch_WB)
        P.mark("A_done")
        gla_branch("B", prefetch_WZ)
        P.mark("B_done")
        P.barrier()

        if debug:
            ar.reset(work0)
            dbf = ar.alloc([4 * T], F32)
            P.add("dve", lambda e: e.tensor_copy(out=dbf, in_=oTa.rearrange("p a b -> p (a b)")), reads=["oTa_%d" % h for h in range(4)],
                  writes=["dbf"])
            dma(dbg_a, dbf, "dbg", reads=["dbf"], writes=["dbg_a"])
            P.add("dve", lambda e: e.tensor_copy(out=dbf, in_=oTb.rearrange("p a b -> p (a b)")),
                  reads=["oTb_%d" % h for h in range(4)] + ["dbg_a"], writes=["dbf"])
            dma(dbg_b, dbf, "dbg", reads=["dbf"], writes=["dbg_b"])
            P.barrier()

        ar.reset(work0)
        by = ar.sub(bufY_rng)
        Wab = by.alloc([8, D], BF16)
        Wple = by.alloc([2, D], BF16)
        gpost_bc = by.alloc([D], F32)
        ypT = ar.alloc([8, T], BF16)
        Wout = ar.alloc([8, D], BF16)
        gple_bc = ar.alloc([D], F32)
        tas = [ar.alloc([512], F32) for _ in range(4)]
        tbs = [ar.alloc([512], F32) for _ in range(4)]

        dma(Wab[:, 0:4, :], wa_v, writes=["Wab_a"], q="pool", nbytes=8192)
        dma(Wab[:, 4:8, :], wb_v, writes=["Wab_b"], q="pool", nbytes=8192)

        def prefetch_C2():
            for i in range(2):
                dma(Wout[:, :, i * 512:(i + 1) * 512], wout_v[:, :, i * 512:(i + 1) * 512], writes=["Wout%d" % i],
                    q="pool", nbytes=8 * 1024)
            dma(Wple, wple_v, writes=["Wple"], q="pool", nbytes=4096)
            dma(gpost_bc, gpost_d.partition_broadcast(128).rearrange("p a b -> p (a b)"), writes=["gpost"], nbytes=4096)
            dma(gple_bc, gple_d.partition_broadcast(128).rearrange("p a b -> p (a b)"), writes=["gple"], nbytes=4096)

        for s in range(NS):
            tok = slice(s * 512, (s + 1) * 512)
            hres = "hT%d" % s
            if s == 1:
                prefetch_C2()
            for fb in range(8):
                par = fb % 2
                b0 = 4 * par
                ta, tb = tas[fb % 4], tbs[fb % 4]
                rta, rtb = "ta%d" % (fb % 4), "tb%d" % (fb % 4)
                fcol = slice(fb * 128, (fb + 1) * 128)
                mm_acc(bankf(b0), [(bufX[:, k, fcol], hT[:, k, tok]) for k in range(8)],
                       reads=["bufX_%d" % (fb // 4), hres], writes=["bank%d" % b0])
                mm_acc(bankf(b0 + 1), [(bufX[:, k, 1024 + fb * 128:1024 + (fb + 1) * 128], hT[:, k, tok])
                                       for k in range(8)],
                       reads=["bufX_%d" % (2 + fb // 4), hres], writes=["bank%d" % (b0 + 1)])
                mm_acc(bankf(b0 + 2), [(Wab[:, k, fcol], oTa[:, k, tok]) for k in range(4)],
                       reads=["Wab_a"] + ["oTa_%d" % h for h in range(4)], writes=["bank%d" % (b0 + 2)])
                mm_acc(bankf(b0 + 3), [(Wab[:, 4 + k, fcol], oTb[:, k, tok]) for k in range(4)],
                       reads=["Wab_b"] + ["oTb_%d" % h for h in range(4)], writes=["bank%d" % (b0 + 3)])
                P.add("act", lambda e, ta=ta, b0=b0: e.activation(out=ta, in_=bankf(b0), func=AF.Sigmoid),
                      reads=["bank%d" % b0], writes=[rta], cost=c_act(512), tset="sig")
                P.add("act", lambda e, tb=tb, b0=b0: e.activation(out=tb, in_=bankf(b0 + 1), func=AF.Sigmoid),
                      reads=["bank%d" % (b0 + 1)], writes=[rtb], cost=c_act(512), tset="sig")
                P.add("dve", lambda e, ta=ta, b0=b0: e.tensor_tensor(out=ta, in0=ta, in1=bankf(b0 + 2), op=ALU.mult),
                      reads=[rta, "bank%d" % (b0 + 2)], writes=[rta], cost=c_dve(512))
                P.add("dve", lambda e, tb=tb, b0=b0: e.tensor_tensor(out=tb, in0=tb, in1=bankf(b0 + 3), op=ALU.mult),
                      reads=[rtb, "bank%d" % (b0 + 3)], writes=[rtb], cost=c_dve(512))
                P.add("pool", lambda e, fb=fb, tok=tok, ta=ta, tb=tb: e.tensor_tensor(out=ypT[:, fb, tok], in0=ta,
                                                                                      in1=tb, op=ALU.add),
                      reads=[rta, rtb], writes=["ypT%d" % s], cost=c_pool(512))
        P.mark("C1_done")
        P.barrier()

        ah = ar.sub(hT_rng)
        ax = ar.sub(bufX_rng)
        ao = ar.sub(oTa_rng)
        ab = ar.sub(oTb_rng)
        NBC = 3
        Wg = ax.alloc([8, D], BF16)
        xsC = [ah.alloc([D], F32) for _ in range(NBC)]
        yo = [ah.alloc([D], F32) for _ in range(NBC)]
        tgC = [(ah if i < 2 else ab).alloc([D], F32) for i in range(NBC)]
        en = [ab.alloc([D], F32) for _ in range(NBC)]
        x1b = [ax.alloc([D], BF16) for _ in range(NBC)]
        x1T = [ax.alloc([8, 128], BF16) for _ in range(NBC)]
        psb = [ao.alloc([256], F32) for _ in range(NBC)]
        pb = [ao.alloc([256], BF16) for _ in range(NBC)]
        pT = [ao.alloc([2, 128], BF16) for _ in range(NBC)]
        junkCs = [ao.alloc([512], BF16) for _ in range(4)]
        stC = [ao.alloc([16], F32) for _ in range(NBC)]
        for i in range(2):
            dma(Wg[:, :, i * 512:(i + 1) * 512], wg_v[:, :, i * 512:(i + 1) * 512], writes=["Wg%d" % i],
                q="pool", nbytes=8 * 1024)

        for t in range(NT):
            sl = t % NBC
            tl = slice(t * 128, (t + 1) * 128)
            xr, pr_, yr = "xsC%d" % sl, "psb%d" % sl, "yo%d" % sl
            sC = stC[sl]
            rs = "stC%d" % sl
            dma(xsC[sl], x_d[tl, :], xr, writes=[xr], nbytes=4096)
            dma(psb[sl], p_d[tl, :], pr_, writes=[pr_], nbytes=1024)
            for hf in range(2):
                mm_acc(bankf(4 + hf), [(ypT[:, k, tl], Wout[:, k, hf * 512:(hf + 1) * 512]) for k in range(8)],
                       reads=["ypT%d" % (t // 4), "Wout%d" % hf], writes=["bank%d" % (4 + hf)])
                P.add("act", lambda e, hf=hf, sC=sC: e.activation(out=junkCs[hf], in_=bankf(4 + hf), func=AF.Square,
                                                                  accum_out=sC[:, hf:hf + 1]),
                      reads=["bank%d" % (4 + hf)], writes=["junkC%d" % hf, rs + "a%d" % hf], cost=c_act(512, acc=True))
            P.add("dve", lambda e, sC=sC: e.tensor_tensor(out=sC[:, 2:3], in0=sC[:, 0:1], in1=sC[:, 1:2], op=ALU.add),
                  reads=[rs + "a0", rs + "a1"], writes=[rs + "b"], cost=0.2)
            P.add("dve", lambda e, sC=sC: e.tensor_scalar(out=sC[:, 3:4], in0=sC[:, 2:3], scalar1=1.0 / D, scalar2=EPS,
                                                          op0=ALU.mult, op1=ALU.add),
                  reads=[rs + "b"], writes=[rs + "c"], cost=0.2)
            P.add("pool", lambda e, sC=sC: e.tensor_tensor(out=sC[:, 4:5], in0=sC[:, 3:4], in1=nhalf[:, 0:1], op=ALU.pow),
                  reads=[rs + "c", "nhalf"], writes=[rs + "d"], cost=0.8)
            for hf in range(2):
                hs = slice(hf * 512, (hf + 1) * 512)
                P.add("dve", lambda e, hf=hf, hs=hs, sl=sl, sC=sC: e.scalar_tensor_tensor(
                    out=yo[sl][:, hs], in0=bankf(4 + hf), scalar=sC[:, 4:5], in1=gpost_bc[:, hs],
                    op0=ALU.mult, op1=ALU.mult),
                    reads=["bank%d" % (4 + hf), rs + "d", "gpost"], writes=[yr + "h%d" % hf], cost=c_dve(512))
            P.add("pool", lambda e, sl=sl: e.tensor_tensor(out=xsC[sl], in0=xsC[sl], in1=yo[sl], op=ALU.add),
                  reads=[xr, yr + "h0", yr + "h1"], writes=[xr], cost=c_pool(D))
            P.add("act", lambda e, sl=sl: e.activation(out=x1b[sl], in_=xsC[sl], func=AF.Copy),
                  reads=[xr], writes=["x1b%d" % sl], cost=c_act(D))
            tpx = bankb(7).rearrange("p (k c) -> p k c", k=8)

            def trx(e, tpx=tpx, sl=sl):
                for k in range(8):
                    ins = e.transpose(out=tpx[:, k, :], in_=x1b[sl][:, k * 128:(k + 1) * 128], identity=ident)
                return ins
            P.add("pe", trx, reads=["x1b%d" % sl, "ident"], writes=["bank7"], cost=8 * 0.08)
            P.add("act", lambda e, tpx=tpx, sl=sl: e.activation(out=x1T[sl], in_=tpx, func=AF.Copy),
                  reads=["bank7"], writes=["x1T%d" % sl], cost=c_act(D))
            P.add("pool", lambda e, sl=sl: e.tensor_copy(out=pb[sl], in_=psb[sl]), reads=[pr_], writes=["pb%d" % sl],
                  cost=c_pool(256))
            tpp = bankb(6)[:, 0:256].rearrange("p (k c) -> p k c", k=2)

            def trp(e, tpp=tpp, sl=sl):
                for k in range(2):
                    ins = e.transpose(out=tpp[:, k, :], in_=pb[sl][:, k * 128:(k + 1) * 128], identity=ident)
                return ins
            P.add("pe", trp, reads=["pb%d" % sl, "ident"], writes=["bank6"], cost=2 * 0.08)
            P.add("act", lambda e, tpp=tpp, sl=sl: e.activation(out=pT[sl], in_=tpp, func=AF.Copy),
                  reads=["bank6"], writes=["pT%d" % sl], cost=c_act(256))
            for hf in range(2):
                bk = 2 + hf
                mm_acc(bankf(bk), [(pT[sl][:, k, :], Wple[:, k, hf * 512:(hf + 1) * 512]) for k in range(2)],
                       reads=["pT%d" % sl, "Wple"], writes=["bank%d" % bk])
                P.add("act", lambda e, hf=hf, bk=bk, sC=sC: e.activation(out=junkCs[2 + hf], in_=bankf(bk), func=AF.Square,
                                                                         accum_out=sC[:, 8 + hf:9 + hf]),
                      reads=["bank%d" % bk], writes=["junkC%d" % (2 + hf), rs + "e%d" % hf], cost=c_act(512, acc=True))
            P.add("dve", lambda e, sC=sC: e.tensor_tensor(out=sC[:, 10:11], in0=sC[:, 8:9], in1=sC[:, 9:10], op=ALU.add),
                  reads=[rs + "e0", rs + "e1"], writes=[rs + "f"], cost=0.2)
            P.add("dve", lambda e, sC=sC: e.tensor_scalar(out=sC[:, 11:12], in0=sC[:, 10:11], scalar1=1.0 / D,
                                                          scalar2=EPS, op0=ALU.mult, op1=ALU.add),
                  reads=[rs + "f"], writes=[rs + "g"], cost=0.2)
            P.add("pool", lambda e, sC=sC: e.tensor_tensor(out=sC[:, 12:13], in0=sC[:, 11:12], in1=nhalf[:, 0:1],
                                                           op=ALU.pow),
                  reads=[rs + "g", "nhalf"], writes=[rs + "h"], cost=0.8)
            for hf in range(2):
                hs = slice(hf * 512, (hf + 1) * 512)
                bk = 2 + hf
                P.add("dve", lambda e, hs=hs, bk=bk, sl=sl, sC=sC: e.scalar_tensor_tensor(
                    out=en[sl][:, hs], in0=bankf(bk), scalar=sC[:, 12:13], in1=gple_bc[:, hs],
                    op0=ALU.mult, op1=ALU.mult),
                    reads=["bank%d" % bk, rs + "h", "gple"], writes=["en%d_%d" % (sl, hf)], cost=c_dve(512))
            for hf in range(2):
                hs = slice(hf * 512, (hf + 1) * 512)
                mm_acc(bankf(hf), [(x1T[sl][:, k, :], Wg[:, k, hs]) for k in range(8)],
                       reads=["x1T%d" % sl, "Wg%d" % hf], writes=["bank%d" % hf])
                P.add("act", lambda e, hf=hf, hs=hs, sl=sl: e.activation(out=tgC[sl][:, hs], in_=bankf(hf),
                                                                         func=AF.Sigmoid),
                      reads=["bank%d" % hf], writes=["tgC%d_%d" % (sl, hf)], cost=c_act(512), tset="sig")
                P.add("dve", lambda e, hs=hs, sl=sl: e.tensor_tensor(out=en[sl][:, hs], in0=tgC[sl][:, hs],
                                                                     in1=en[sl][:, hs], op=ALU.mult),
                      reads=["tgC%d_%d" % (sl, hf), "en%d_%d" % (sl, hf)], writes=["en%d_%d" % (sl, hf)],
                      cost=c_dve(512))
                P.add("dve", lambda e, hs=hs, sl=sl: e.tensor_tensor(out=yo[sl][:, hs], in0=en[sl][:, hs],
                                                                     in1=xsC[sl][:, hs], op=ALU.add),
                      reads=["en%d_%d" % (sl, hf), xr, yr + "h%d" % hf], writes=[yr + "h%d" % hf],
                      cost=c_dve(512))
            dma(out_d[tl, :], yo[sl], "out%d" % sl, reads=[yr + "h0", yr + "h1"],
                writes=[yr + "h0", yr + "h1", "OUT%d" % sl], nbytes=4096)
        P.barrier()
        P.add("sp", None, reads=["OUT0", "OUT1", "OUT2"] + (["dbg_a", "dbg_b"] if debug else []), force=True)
        build_nc.n_ops = len(P.ops)
        build_nc.marks = P.marks
        P.emit(nc, st)
        build_nc.est_time = getattr(P, "est_time", None)
        fin = getattr(P, "finish", None)
        if fin:
            build_nc.mark_times = [(nm, max(fin[:i]) if i else 0.0) for nm, i in P.marks]
            build_nc.busy = P.busy
            build_nc.P = P
    return nc


def _consts():
    ident = np.eye(128, dtype=np.float32)
    j = np.arange(128)[:, None]
    i = np.arange(128)[None, :]
    maskbd = ((j <= i) & ((j // 64) == (i // 64))).astype(np.float32)
    m01 = np.ones((128, 512), np.float32)
    m01[:, ::64] = 0.0
    return ident, maskbd, m01


def make_in_maps(x, p, w_in, lb_logits, w_gla_up, b_gla, norm_pre, norm_post, head_norm_a, head_norm_b,
                 w_branch_a, w_branch_b, w_out, w_ple, w_ple_gate, norm_ple):
    f = lambda a: np.ascontiguousarray(np.asarray(a, dtype=np.float32))
    ident, maskbd, m01 = _consts()
    lbl = f(np.asarray(lb_logits).reshape(2, 4, 128).transpose(2, 0, 1).reshape(128, 8))
    shared = {
        "w_in": f(np.asarray(w_in)[0]),
        "w_a": f(np.asarray(w_branch_a)[0]),
        "w_b": f(np.asarray(w_branch_b)[0]),
        "w_out": f(np.asarray(w_out)[0]),
        "w_ple": f(np.asarray(w_ple)[0]),
        "w_g": f(np.asarray(w_ple_gate)[0]),
        "w_up": f(np.asarray(w_gla_up)[0]),
        "lbl": lbl,
        "gpre": f(np.asarray(norm_pre)[0].reshape(1, D)),
        "hna": f(np.asarray(head_norm_a)[0].reshape(4, 128).T),
        "hnb": f(np.asarray(head_norm_b)[0].reshape(4, 128).T),
        "bgl": f(np.asarray(b_gla)[0].reshape(2, 128).T),
        "gpost": f(np.asarray(norm_post)[0].reshape(1, D)),
        "gple": f(np.asarray(norm_ple)[0].reshape(1, D)),
        "ident": ident,
        "maskbd": maskbd,
        "mask01": m01,
    }
    xs = np.asarray(x)
    ps = np.asarray(p)
    maps = []
    for b in range(8):
        m = dict(shared)
        m["x"] = f(xs[b])
        m["p"] = f(ps[0, b])
        maps.append(m)
    return maps


_NC_CACHE = {}


def kernel(x, p, w_in, lb_logits, w_gla_up, b_gla, norm_pre, norm_post, head_norm_a, head_norm_b,
           w_branch_a, w_branch_b, w_out, w_ple, w_ple_gate, norm_ple):
    in_maps = make_in_maps(x, p, w_in, lb_logits, w_gla_up, b_gla, norm_pre, norm_post, head_norm_a, head_norm_b,
                           w_branch_a, w_branch_b, w_out, w_ple, w_ple_gate, norm_ple)
    if "nc" not in _NC_CACHE:
        _NC_CACHE["nc"] = build_nc(False)
    nc = _NC_CACHE["nc"]
    res = run_bass_kernel_spmd(nc, in_maps, core_ids=list(range(8)))
    out = np.stack([np.asarray(r["out"]) for r in res.results], axis=0).astype(np.float32)
    return out
```

```python
import contextlib
import sys
import numpy as np
import concourse.bass as bass
import concourse.mybir as mybir
from concourse.alu_op_type import AluOpType as ALU
from concourse.bass_utils import run_bass_kernel_spmd

F32 = mybir.dt.float32
BF16 = mybir.dt.bfloat16
AF = mybir.ActivationFunctionType
AX = mybir.AxisListType

ENGS = ("sp", "act", "pool", "dve", "pe")

T = 2048
D = 1024
NT = 16
NS = 4
DIN = 5648
EPS = 1e-6
TSWITCH = 1.3


class Op:
    __slots__ = ("idx", "eng", "fn", "reads", "writes", "raw", "oth", "need_inc",
                 "dma_key", "dma_cnt", "inc_cnt", "cost", "lat", "tset", "deps", "prio", "line", "start", "bind", "dbytes")


def c_pe(cols):
    return sum(n / 2400.0 + 0.008 for n in cols)


def c_act(n, naps=0, acc=False):
    return 0.1 + n / 1100.0 + 0.06 * naps + (0.09 if acc else 0.0)


def c_dve(n, scan=False):
    return 0.15 + (2 * n if scan else n) / 960.0


def c_pool(n):
    return 0.3 + n / 520.0


class Prog:
    def __init__(self):
        self.ops = []
        self.last_writer = {}
        self.readers = {}
        self.dma_counts = {}
        self.last_dma = {}
        self.last_compute = {}
        self.pending = {e: set() for e in ENGS}
        self.max_ops = None
        self.marks = []
        self.do_schedule = True
        self.has_succ = set()
        self.bar_start = 0
        self.join = None
        self.join_fn = None

    def mark(self, name):
        self.marks.append((name, len(self.ops)))

    def add(self, eng, fn, reads=(), writes=(), dma_key=None, force=False, cost=0.3, lat=None, tset=None):
        if self.max_ops is not None and len(self.ops) >= self.max_ops and not force:
            return None
        op = Op()
        op.idx = len(self.ops)
        op.line = sys._getframe(1).f_lineno if sys._getframe(1).f_code.co_name not in ("dma", "mm_acc") \
            else sys._getframe(2).f_lineno
        op.eng = eng
        op.fn = fn
        op.reads = tuple(reads)
        op.writes = tuple(writes)
        op.need_inc = False
        op.dma_key = dma_key
        op.dma_cnt = 0
        op.inc_cnt = 0
        op.cost = cost
        op.lat = cost if lat is None else lat
        op.tset = tset
        op.dbytes = 0
        raw, oth = set(), set()
        for r in op.reads:
            if r in self.last_writer:
                raw.add(self.last_writer[r])
        for w in op.writes:
            if w in self.last_writer:
                oth.add(self.last_writer[w])
            for rd in self.readers.get(w, ()):
                oth.add(rd)
        for w in op.writes:
            self.last_writer[w] = op.idx
            self.readers[w] = []
        for r in op.reads:
            if r not in op.writes:
                self.readers.setdefault(r, []).append(op.idx)
        if self.pending[eng]:
            raw |= self.pending[eng]
            self.pending[eng] = set()
        if self.join is not None:
            raw.add(self.join)
        raw.discard(op.idx)
        oth.discard(op.idx)
        op.raw = raw
        op.oth = oth - raw
        self.has_succ |= raw
        self.has_succ |= oth
        if dma_key is not None:
            self.dma_counts[dma_key] = self.dma_counts.get(dma_key, 0) + 1
            op.dma_cnt = self.dma_counts[dma_key]
            self.last_dma[dma_key] = op.idx
        elif fn is not None:
            self.last_compute[eng] = op.idx
        self.ops.append(op)
        return op

    def barrier(self):
        sinks = set(i for i in range(self.bar_start, len(self.ops)) if i not in self.has_succ)
        self.pending["pool"] |= sinks
        j = self.add("pool", self.join_fn, cost=0.3, force=True)
        self.join = j.idx
        self.bar_start = j.idx

    def _sync_deps(self, op):
        out = []
        for d in sorted(op.raw | op.oth):
            dop = self.ops[d]
            if dop.dma_key is None and dop.eng == op.eng and op.dma_key is None and op.eng == "pe":
                continue
            out.append(dop)
        return out

    def schedule(self):
        ops = self.ops
        n = len(ops)
        succ = [[] for _ in range(n)]
        ndeps = [0] * n
        for op in ops:
            op.deps = sorted(op.raw | op.oth)
            ndeps[op.idx] = len(op.deps)
            for d in op.deps:
                succ[d].append(op.idx)
        for op in reversed(ops):
            best = 0.0
            for s in succ[op.idx]:
                best = max(best, ops[s].prio)
            op.prio = best + op.lat
        order = {e: [] for e in ENGS}
        if not self.do_schedule:
            for op in ops:
                order[op.eng].append(op)
            return order
        finish = [0.0] * n
        free = {e: 0.0 for e in ENGS}
        dma_busy = [0.0]
        cur_tset = [None]
        ready = {e: [] for e in ENGS}
        for op in ops:
            if ndeps[op.idx] == 0:
                ready[op.eng].append(op)
        left = n
        while left:
            best = None
            for e in ENGS:
                for op in ready[e]:
                    r = 0.0
                    rb = None
                    for d in op.deps:
                        f = finish[d] + (0.0 if (ops[d].eng == e and ops[d].dma_key is None) else 0.15)
                        if f > r:
                            r = f
                            rb = d
                    s = max(free[e], r)
                    op.bind = rb if r >= free[e] else -1
                    if e == "act" and op.tset is not None and cur_tset[0] is not None and cur_tset[0] != op.tset:
                        s += TSWITCH
                    key = (s, -op.prio, op.idx)
                    if best is None or key < best[0]:
                        best = (key, op)
            (s, _, _), op = best
            e = op.eng
            ready[e].remove(op)
            if e == "act" and op.tset is not None:
                cur_tset[0] = op.tset
            if op.bind == -1:
                op.bind = ("eng", order[e][-1].idx if order[e] else None)
            else:
                op.bind = ("dep", op.bind)
            op.start = s
            free[e] = s + op.cost
            if op.dma_key is not None and op.dbytes:
                t0 = max(s + 1.5, dma_busy[0])
                dma_busy[0] = t0 + op.dbytes / 260e3
                finish[op.idx] = dma_busy[0] + 0.5
            else:
                finish[op.idx] = s + op.lat
            order[e].append(op)
            left -= 1
            for sidx in succ[op.idx]:
                ndeps[sidx] -= 1
                if ndeps[sidx] == 0:
                    ready[ops[sidx].eng].append(ops[sidx])
        self.est_time = max(finish) if n else 0.0
        self.finish = finish
        self.busy = {e: sum(op.cost for op in order[e]) for e in ENGS}
        return order

    def emit(self, nc, stack):
        order = self.schedule()
        for op in self.ops:
            for d in self._sync_deps(op):
                if d.dma_key is None:
                    d.need_inc = True
        eng_sem = {e: stack.enter_context(nc.semaphore("s_" + e)) for e in ENGS}
        key_sem = {k: stack.enter_context(nc.semaphore("d_" + k)) for k in self.dma_counts}
        for e in ENGS:
            c = 0
            for op in order[e]:
                if op.dma_key is None and op.need_inc:
                    c += 1
                    op.inc_cnt = c
        block = stack.enter_context(nc.Block())
        prog = self

        def run(engname, eng):
            waited = {}
            for op in order[engname]:
                for d in prog._sync_deps(op):
                    if d.dma_key is not None:
                        sem, val, k = key_sem[d.dma_key], 16 * d.dma_cnt, "d_" + d.dma_key
                    else:
                        sem, val, k = eng_sem[d.eng], d.inc_cnt, "s_" + d.eng
                    if waited.get(k, 0) >= val:
                        continue
                    waited[k] = val
                    eng.wait_ge(sem, val)
                if op.fn is None:
                    continue
                ins = op.fn(eng)
                if op.dma_key is not None:
                    ins.then_inc(key_sem[op.dma_key], 16)
                elif op.need_inc:
                    ins.then_inc(eng_sem[op.eng], 1)

        @block.sync
        def _(e):
            run("sp", e)

        @block.scalar
        def _(e):
            run("act", e)

        @block.gpsimd
        def _(e):
            run("pool", e)

        @block.vector
        def _(e):
            run("dve", e)

        @block.tensor
        def _(e):
            run("pe", e)


class Arena:
    def __init__(self, nc, name, nbytes, ap=None):
        self.words = nbytes // 4
        if ap is None:
            self.t = nc.alloc_sbuf_tensor(name, [128, self.words], F32)
            self.ap = self.t.ap()
        else:
            self.ap = ap
        self.off = 0
        self.hi = 0
        self.last = (0, 0)

    def sub(self, rng):
        off, words = rng
        return Arena(None, None, words * 4, ap=self.ap[:, off:off + words])

    def mark(self):
        return self.off

    def reset(self, m):
        self.off = m

    def alloc(self, free_shape, dt):
        n = 1
        for s in free_shape:
            n *= s
        words = n if dt == F32 else (n + 1) // 2
        words = (words + 7) // 8 * 8
        assert self.off + words <= self.words, ("arena overflow", self.off, words, self.words)
        v = self.ap[:, self.off:self.off + words]
        self.last = (self.off, words)
        self.off += words
        self.hi = max(self.hi, self.off)
        if dt != F32:
            v = v.bitcast(dt)
        v = v[:, 0:n]
        if len(free_shape) == 2:
            v = v.rearrange("p (a b) -> p a b", a=free_shape[0])
        elif len(free_shape) == 3:
            v = v.rearrange("p (a b c) -> p a b c", a=free_shape[0], b=free_shape[1])
        return v


def build_nc(debug=False, max_ops=None, do_schedule=True):
    nc = bass.Bass("TRN2", target_bir_lowering=False)

    def din(name, shape):
        return nc.dram_tensor(name, list(shape), F32, kind="ExternalInput").ap()

    x_d = din("x", [T, D])
    p_d = din("p", [T, 256])
    win_d = din("w_in", [D, DIN])
    wa_d = din("w_a", [512, D])
    wb_d = din("w_b", [512, D])
    wout_d = din("w_out", [D, D])
    wple_d = din("w_ple", [256, D])
    wg_d = din("w_g", [D, D])
    wup_d = din("w_up", [16, 256])
    lbl_d = din("lbl", [128, 8])
    gpre_d = din("gpre", [1, D])
    hna_d = din("hna", [128, 4])
    hnb_d = din("hnb", [128, 4])
    bgl_d = din("bgl", [128, 2])
    gpost_d = din("gpost", [1, D])
    gple_d = din("gple", [1, D])
    ident_d = din("ident", [128, 128])
    maskbd_d = din("maskbd", [128, 128])
    mask01_d = din("mask01", [128, 512])
    out_d = nc.dram_tensor("out", [T, D], F32, kind="ExternalOutput").ap()
    if debug:
        dbg_a = nc.dram_tensor("dbg_a", [128, 4 * T], F32, kind="ExternalOutput").ap()
        dbg_b = nc.dram_tensor("dbg_b", [128, 4 * T], F32, kind="ExternalOutput").ap()

    win_v = win_d.rearrange("(k p) n -> p k n", p=128)
    wa_v = wa_d.rearrange("(k p) n -> p k n", p=128)
    wb_v = wb_d.rearrange("(k p) n -> p k n", p=128)
    wout_v = wout_d.rearrange("(k p) n -> p k n", p=128)
    wple_v = wple_d.rearrange("(k p) n -> p k n", p=128)
    wg_v = wg_d.rearrange("(k p) n -> p k n", p=128)

    P = Prog()
    P.max_ops = max_ops
    P.do_schedule = do_schedule
    with contextlib.ExitStack() as st:
        ar = Arena(nc, "arena", 207 * 1024 + 768)
        banks = [st.enter_context(nc.psum_tensor("bank%d" % i, [128, 512], F32)) for i in range(8)]

        def bankf(i):
            return banks[i][:]

        def bankb(i):
            return banks[i][:].bitcast(BF16)

        hT = ar.alloc([8, T], BF16)
        hT_rng = ar.last
        oTa = ar.alloc([4, T], BF16)
        oTa_rng = ar.last
        oTb = ar.alloc([4, T], BF16)
        oTb_rng = ar.last
        ident = ar.alloc([128], BF16)
        maskbd = ar.alloc([128], F32)
        mask01 = ar.alloc([512], F32)
        sm_lbl = ar.alloc([8], F32)
        hna = ar.alloc([4], F32)
        hnb = ar.alloc([4], F32)
        hna2 = ar.alloc([4], F32)
        hnb2 = ar.alloc([4], F32)
        bgl = ar.alloc([2], F32)
        nbgl = ar.alloc([2], F32)
        c1 = ar.alloc([4], F32)
        nc1 = ar.alloc([4], F32)
        lnc1 = ar.alloc([4], F32)
        c0 = ar.alloc([4], F32)
        lbd = ar.alloc([4], F32)
        lbt = ar.alloc([4], F32)
        epsc = ar.alloc([1], F32)
        onec = ar.alloc([1], F32)
        nhalf = ar.alloc([8], F32)
        wup = ar.alloc([256], BF16)
        stage = None
        bufX = ar.alloc([8, 2048], BF16)
        bufX_rng = ar.last
        bufY = ar.alloc([8, 1552], BF16)
        bufY_rng = ar.last
        dummy = ar.alloc([8], F32)
        dummy2 = ar.alloc([8], F32)
        P.join_fn = lambda e: e.memset(dummy, 0.0)
        work0 = ar.mark()

        ukey = [0]

        def dma(dst, src, key=None, reads=(), writes=(), q="sp", nbytes=65536):
            if key is None:
                ukey[0] += 1
                key = "u%d" % ukey[0]
            lat = 2.0 + nbytes * 128 / 200e3
            occ = 0.45 if q == "sp" else 1.1
            o_ = P.add(q, lambda e: e.dma_start(out=dst, in_=src), reads=reads, writes=writes, dma_key=key,
                       cost=occ, lat=lat)
            if o_ is not None:
                nel = 1
                for s_ in src.shape:
                    nel *= s_
                o_.dbytes = max(nel * 4, 16384)

        stage_ctr = [0]

        def load_cast(dst, src, ncols, scale, dst_res, eng="pool", extra_reads=()):
            c = 0
            while c < ncols:
                w = min(1024, ncols - c)
                sl = stage_ctr[0] % 2
                stage_ctr[0] += 1
                sres = "stg%d" % sl
                sv = stage[sl][:, 0:w]
                dma(sv, src[:, c:c + w], sres, writes=[sres], nbytes=4 * w)
                dv = dst[:, c:c + w]
                if eng == "pool":
                    P.add("pool", lambda e, dv=dv, sv=sv: e.tensor_scalar(out=dv, in0=sv, scalar1=scale, scalar2=0.0,
                                                                          op0=ALU.mult, op1=ALU.add),
                          reads=[sres] + list(extra_reads), writes=[dst_res], cost=c_pool(w) * 0.6)
                else:
                    P.add("dve", lambda e, dv=dv, sv=sv: e.tensor_scalar(out=dv, in0=sv, scalar1=scale, scalar2=None,
                                                                         op0=ALU.mult),
                          reads=[sres] + list(extra_reads), writes=[dst_res], cost=c_dve(w))
                c += w

        def mm_acc(out, pairs, reads, writes):
            def fn(e):
                n = len(pairs)
                for i, (l, r) in enumerate(pairs):
                    ins = e.matmul(out, lhsT=l, rhs=r, start=(i == 0), stop=(i == n - 1))
                return ins
            cols = [r.shape[-1] for (_, r) in pairs]
            P.add("pe", fn, reads=reads, writes=writes, cost=c_pe(cols))

        dma(ident, ident_d, reads=(), writes=["ident"], q="pool", nbytes=256)
        dma(wup[0:16, :], wup_d, writes=["wup"], q="pool", nbytes=512)
        dma(maskbd, maskbd_d, writes=["maskbd"], nbytes=512)
        dma(mask01, mask01_d, writes=["mask01"], nbytes=2048)
        dma(sm_lbl, lbl_d, writes=["lbl"], nbytes=32)
        dma(hna, hna_d, writes=["hna"], nbytes=16)
        dma(hnb, hnb_d, writes=["hnb"], nbytes=16)
        dma(bgl, bgl_d, writes=["bgl"], nbytes=8)
        P.add("pool", lambda e: e.memset(epsc, EPS), writes=["epsc"], cost=0.3)
        P.add("pool", lambda e: e.memset(onec, 1.0), writes=["onec"], cost=0.3)
        P.add("pool", lambda e: e.memset(nhalf, -0.5), writes=["nhalf"], cost=0.3)
        P.add("dve", lambda e: e.tensor_tensor(out=lbd, in0=sm_lbl[:, 0:4], in1=sm_lbl[:, 4:8], op=ALU.subtract),
              reads=["lbl"], writes=["lbd"], cost=0.2)
        P.add("act", lambda e: e.activation(out=lbt, in_=lbd, func=AF.Tanh, scale=0.5), reads=["lbd"], writes=["lbt"],
              cost=0.3, tset="tanh")
        P.add("dve", lambda e: e.tensor_scalar(out=c1, in0=lbt, scalar1=-0.25, scalar2=0.25, op0=ALU.mult, op1=ALU.add),
              reads=["lbt"], writes=["c1"], cost=0.2)
        P.add("dve", lambda e: e.tensor_scalar(out=c0, in0=lbt, scalar1=0.25, scalar2=0.75, op0=ALU.mult, op1=ALU.add),
              reads=["lbt"], writes=["c0"], cost=0.2)
        P.add("dve", lambda e: e.tensor_scalar(out=nc1, in0=lbt, scalar1=0.25, scalar2=-0.25, op0=ALU.mult, op1=ALU.add),
              reads=["lbt"], writes=["nc1"], cost=0.2)
        P.add("act", lambda e: e.activation(out=lnc1, in_=c1, func=AF.Ln), reads=["c1"], writes=["lnc1"], cost=0.3,
              tset="ln")
        P.add("dve", lambda e: e.tensor_scalar(out=nbgl, in0=bgl, scalar1=-1.0, scalar2=None, op0=ALU.mult),
              reads=["bgl"], writes=["nbgl"], cost=0.2)
        P.add("dve", lambda e: e.tensor_scalar(out=hna2, in0=hna, scalar1=1.0, scalar2=None, op0=ALU.mult),
              reads=["hna"], writes=["hna2"], cost=0.2)
        P.add("dve", lambda e: e.tensor_scalar(out=hnb2, in0=hnb, scalar1=1.0, scalar2=None, op0=ALU.mult),
              reads=["hnb"], writes=["hnb2"], cost=0.2)
        P.mark("consts_done")

        a0 = ar.sub(bufY_rng)
        gpre_bc = a0.alloc([D], F32)
        xs = [a0.alloc([D], F32) for _ in range(3)]
        hb = [a0.alloc([D], BF16) for _ in range(2)]
        junk0 = a0.alloc([D], BF16)
        st0 = [a0.alloc([4], F32) for _ in range(2)]
        dma(gpre_bc, gpre_d.partition_broadcast(128).rearrange("p a b -> p (a b)"), writes=["gpre_bc"], nbytes=4096)
        ph0_res = ["gpre_bc", "junk0"]

        def wa_load(grp, reads=()):
            c0_ = grp * 512
            dma(bufX[:, :, c0_:c0_ + 512], win_v[:, :, c0_:c0_ + 512], reads=reads, writes=["bufX_%d" % grp],
                q="pool", nbytes=8 * 1024)


        for t in range(NT):
            sl3 = t % 3
            sl = t % 2
            xr, hr, sr = "xs%d" % sl3, "hb%d" % sl, "st0_%d" % sl
            ph0_res += [xr, hr, sr + "a", sr + "b", sr + "c"]
            tpb = bankb(6 + sl).rearrange("p (k c) -> p k c", k=8)
            dma(xs[sl3], x_d[t * 128:(t + 1) * 128, :], xr, writes=[xr], nbytes=4096)
            if t < 4:
                wa_load(t)
            P.add("act", lambda e, sl=sl, sl3=sl3: e.activation(out=junk0, in_=xs[sl3], func=AF.Square,
                                                               accum_out=st0[sl][:, 0:1]),
                  reads=[xr], writes=["junk0", sr + "a"], cost=c_act(D, acc=True))
            P.add("act", lambda e, sl=sl: e.activation(out=st0[sl][:, 1:2], in_=st0[sl][:, 0:1], func=AF.Ln,
                                                       scale=1.0 / D, bias=epsc[:, 0:1]),
                  reads=[sr + "a", "epsc"], writes=[sr + "b"], cost=0.3, tset="ln")
            P.add("act", lambda e, sl=sl: e.activation(out=st0[sl][:, 2:3], in_=st0[sl][:, 1:2], func=AF.Exp,
                                                       scale=-0.5),
                  reads=[sr + "b"], writes=[sr + "c"], cost=0.3)
            P.add("dve", lambda e, sl=sl, sl3=sl3: e.scalar_tensor_tensor(out=hb[sl], in0=xs[sl3],
                                                                         scalar=st0[sl][:, 2:3], in1=gpre_bc,
                                                                         op0=ALU.mult, op1=ALU.mult),
                  reads=[xr, sr + "c", "gpre_bc"], writes=[hr], cost=c_dve(D))

            def tr(e, sl=sl, tpb=tpb):
                for k in range(8):
                    ins = e.transpose(out=tpb[:, k, :], in_=hb[sl][:, k * 128:(k + 1) * 128], identity=ident)
                return ins
            P.add("pe", tr, reads=[hr, "ident"], writes=["bank%d" % (6 + sl)], cost=8 * 0.08)
            P.add("dve", lambda e, t=t, tpb=tpb: e.tensor_copy(out=hT[:, :, t * 128:(t + 1) * 128], in_=tpb),
                  reads=["bank%d" % (6 + sl)], writes=["hT%d" % (t // 4)], cost=c_dve(D) * 0.7)
        P.mark("phase0_done")

        ar.reset(work0)
        NB = 4
        tmp = [[ar.alloc([512], F32) for _ in range(6)] for _ in range(2)]
        qTs = [ar.alloc([4, 512], BF16) for _ in range(2)]
        kTs = [ar.alloc([NB, 512], BF16) for _ in range(2)]
        ecs = [ar.alloc([NB, 8], F32) for _ in range(3)]
        NBT = 4
        vbfs = [ar.alloc([512], BF16) for _ in range(NBT)]
        Gs = [ar.alloc([512], F32) for _ in range(NBT)]
        tgs = Gs
        kts = [ar.alloc([4, 128], BF16) for _ in range(NBT)]
        smks = [ar.alloc([4, 128], BF16) for _ in range(NBT)]
        onbs = [ar.alloc([512], BF16) for _ in range(NBT)]
        on1 = ar.alloc([4, 128], F32)
        Tts = [ar.alloc([NB, 128], F32) for _ in range(2)]
        Sbs = [ar.alloc([NB, 128], BF16) for _ in range(2)]
        st4s = [ar.alloc([16], F32) for _ in range(4)]
        aTs = [ar.alloc([512], BF16) for _ in range(2)]
        junk4 = ar.alloc([4, 128], BF16)

        def gla_branch(br, prefetch=None):
            isA = br == "A"
            nblk = 4 if isA else 2
            dk = 128 if isA else 64
            W = bufX if isA else bufY
            oT = oTa if isA else oTb
            ores = "oTa" if isA else "oTb"
            qc0, fc0, vc0, gc0 = (0, 512, 1024, 1536) if isA else (0, 256, 512, 1024)
            if isA:
                wq, wf, wv, wg_ = ["bufX_0"], ["bufX_1"], ["bufX_2"], ["bufX_3"]
            else:
                wq, wf, wv, wg_ = ["bufY_0"], ["bufY_0"], ["bufY_1"], ["bufY_2"]

            P.add("pool", lambda e: e.memset(Tts[1], 0.0), writes=["Tt1_%d" % b for b in range(NB)], cost=c_pool(512))
            P.add("pool", lambda e: e.memset(Sbs[0], 0.0), writes=["Sb0"], cost=c_pool(256))
            for i in range(3):
                P.add("pool", lambda e, i=i: e.memset(ecs[i], 0.0), writes=["ec%d" % i], cost=0.3)
            if not isA:
                for i in range(2):
                    P.add("pool", lambda e, i=i: e.memset(qTs[i], 0.0), writes=["qT%d" % i], cost=c_pool(1024))
                for i in range(NBT):
                    P.add("pool", lambda e, i=i: e.memset(kts[i], 0.0), writes=["kt%d" % i], cost=c_pool(256))

            def prep(s):
                ss = s % 2
                qT, kT, ec = qTs[ss], kTs[ss], ecs[s % 3]
                qres, kres, ecres = "qT%d" % ss, "kT%d" % ss, "ec%d" % (s % 3)
                tok = slice(s * 512, (s + 1) * 512)
                hres = "hT%d" % s
                aT = aTs[ss]
                if not isA:
                    mm_acc(bankf(4)[0:16, :], [(W[:, k, 1536:1552], hT[:, k, tok]) for k in range(8)],
                           reads=["bufY_3", hres], writes=["bank4"])
                    P.add("act", lambda e, aT=aT: e.activation(out=aT[0:16, :], in_=bankf(4)[0:16, :], func=AF.Copy),
                          reads=["bank4"], writes=["aT%d" % ss], cost=c_act(512))
                for blk in range(nblk):
                    ts_ = (s * nblk + blk) % 2
                    t1, t2, t3, t4, Ebt, Enb = tmp[ts_]
                    r1, r2, r3, r4, rE, rN = ["tmp%d_%d" % (ts_, i) for i in range(6)]
                    qcol = slice(qc0 + blk * 128, qc0 + (blk + 1) * 128)
                    fcol = slice(fc0 + blk * 128, fc0 + (blk + 1) * 128)
                    mm_acc(bankf(0), [(W[:, k, qcol], hT[:, k, tok]) for k in range(8)],
                           reads=wq + [hres], writes=["bank0"])
                    mm_acc(bankf(1), [(W[:, k, fcol], hT[:, k, tok]) for k in range(8)],
                           reads=wf + [hres], writes=["bank1"])
                    if isA:
                        P.add("act", lambda e, t1=t1: e.activation(out=t1, in_=bankf(0), func=AF.Silu),
                              reads=["bank0"], writes=[r1], cost=c_act(512), tset="tanh")
                        P.add("act", lambda e, t2=t2: e.activation(out=t2, in_=bankf(1), func=AF.Tanh, scale=0.5),
                              reads=["bank1"], writes=[r2], cost=c_act(512), tset="tanh")
                        P.add("act", lambda e, blk=blk, t2=t2, t3=t3: e.activation(
                            out=t3, in_=t2, func=AF.Ln, scale=c1[:, blk:blk + 1], bias=c0[:, blk:blk + 1]),
                            reads=[r2, "c1", "c0"], writes=[r3], cost=c_act(512, 2), tset="ln")
                    else:
                        mm_acc(bankf(4), [(wup[0:16, blk * 128:(blk + 1) * 128], aT[0:16, :])],
                               reads=["wup", "aT%d" % ss], writes=["bank4"])
                        P.add("act", lambda e, blk=blk, t1=t1: e.activation(out=t1, in_=bankf(4), func=AF.Exp, scale=-1.0,
                                                                            bias=nbgl[:, blk:blk + 1]),
                              reads=["bank4", "nbgl"], writes=[r1], cost=c_act(512, 1))
                        P.add("act", lambda e, t1=t1, t3=t3: e.activation(out=t3, in_=t1, func=AF.Ln, bias=onec[:, 0:1]),
                              reads=[r1, "onec"], writes=[r3], cost=c_act(512, 1), tset="ln")
                    P.add("dve", lambda e, t3=t3, t4=t4: e.tensor_tensor_scan(out=t4, data0=mask01, data1=t3, initial=0.0,
                                                                              op0=ALU.mult, op1=ALU.add),
                          reads=[r3, "mask01"], writes=[r4], cost=c_dve(512, scan=True))
                    sE, sN = (1.0, -1.0) if isA else (-1.0 / 16, 1.0 / 16)
                    P.add("act", lambda e, t4=t4, Ebt=Ebt, sE=sE: e.activation(out=Ebt, in_=t4, func=AF.Exp, scale=sE),
                          reads=[r4], writes=[rE], cost=c_act(512))
                    if isA:
                        P.add("act", lambda e, t4=t4, Enb=Enb, blk=blk: e.activation(out=Enb, in_=t4, func=AF.Exp,
                                                                                     scale=-1.0, bias=lnc1[:, blk:blk + 1]),
                              reads=[r4, "lnc1"], writes=[rN], cost=c_act(512, 1))
                    else:
                        P.add("act", lambda e, t4=t4, Enb=Enb, sN=sN: e.activation(out=Enb, in_=t4, func=AF.Exp, scale=sN),
                              reads=[r4], writes=[rN], cost=c_act(512))
                    P.add("dve", lambda e, Ebt=Ebt, ec=ec, blk=blk: e.tensor_copy(
                        out=ec[:, blk, :], in_=Ebt.rearrange("p (c j) -> p c j", j=64)[:, :, 63]),
                        reads=[rE], writes=[ecres], cost=0.2)
                    if isA:
                        P.add("dve", lambda e, blk=blk, t1=t1, Ebt=Ebt, qT=qT: e.scalar_tensor_tensor(
                            out=qT[:, blk, :], in0=t1, scalar=-(dk ** -0.5), in1=Ebt, op0=ALU.mult, op1=ALU.mult),
                            reads=[r1, rE], writes=[qres], cost=c_dve(512))
                        P.add("dve", lambda e, blk=blk, t2=t2, Enb=Enb, kT=kT: e.scalar_tensor_tensor(
                            out=kT[:, blk, :], in0=t2, scalar=-1.0, in1=Enb, op0=ALU.add, op1=ALU.mult),
                            reads=[r2, rN, r3], writes=[kres], cost=c_dve(512))
                    else:
                        for hh in range(2):
                            pq = slice(hh * 64, (hh + 1) * 64)
                            P.add("dve", lambda e, blk=blk, hh=hh, pq=pq, Ebt=Ebt, qT=qT: e.scalar_tensor_tensor(
                                out=qT[pq, 2 * blk + hh, :], in0=bankf(0)[pq, :], scalar=dk ** -0.5,
                                in1=Ebt[pq, :], op0=ALU.mult, op1=ALU.mult),
                                reads=["bank0", rE], writes=[qres], cost=c_dve(512))
                        P.add("dve", lambda e, blk=blk, Enb=Enb, kT=kT: e.tensor_tensor(out=kT[:, blk, :], in0=bankf(1),
                                                                                       in1=Enb, op=ALU.mult),
                              reads=["bank1", rN], writes=[kres], cost=c_dve(512))

            def tile(t):
                s, tt = divmod(t, 4)
                ss = s % 2
                tp_ = t % NBT
                qT, kT, ec = qTs[ss], kTs[ss], ecs[s % 3]
                qres, kres, ecres = "qT%d" % ss, "kT%d" % ss, "ec%d" % (s % 3)
                vbf, tg, G, kt, smk, onb, st4 = vbfs[tp_], tgs[tp_], Gs[tp_], kts[tp_], smks[tp_], onbs[tp_], st4s[tp_]
                rv, rtg, rG, rkt, rsm, ron, rst = ["%s%d" % (n_, tp_) for n_ in ("vbf", "tg", "G", "kt", "smk", "onb", "st4")]
                hres = "hT%d" % s
                tl = slice(tt * 128, (tt + 1) * 128)
                tg_ = slice(t * 128, (t + 1) * 128)
                mm_acc(bankf(2), [(hT[:, k, tg_], W[:, k, vc0:vc0 + 512]) for k in range(8)],
                       reads=wv + [hres], writes=["bank2"])
                P.add("act", lambda e: e.activation(out=vbf, in_=bankf(2), func=AF.Copy),
                      reads=["bank2"], writes=[rv], cost=c_act(512))
                mm_acc(bankf(2), [(hT[:, k, tg_], W[:, k, gc0:gc0 + 512]) for k in range(8)],
                       reads=wg_ + [hres], writes=["bank2"])
                P.add("act", lambda e: e.activation(out=G, in_=bankf(2), func=AF.Silu),
                      reads=["bank2"], writes=[rG], cost=c_act(512), tset="tanh")
                tpk = bankb(3)[:, 0:nblk * 128].rearrange("p (b c) -> p b c", b=nblk)

                def trk(e):
                    for blk in range(nblk):
                        ins = e.transpose(out=tpk[:, blk, :], in_=kT[:, blk, tl], identity=ident)
                    return ins
                P.add("pe", trk, reads=[kres, "ident"], writes=["bank3"], cost=nblk * 0.08)
                if isA:
                    P.add("act", lambda e: e.activation(out=kt, in_=tpk, func=AF.Copy),
                          reads=["bank3"], writes=[rkt], cost=c_act(512))
                else:
                    ktv = kt.rearrange("p (b h) c -> p b h c", h=2)
                    for hh in range(2):
                        cs_ = slice(hh * 64, (hh + 1) * 64)
                        P.add("act", lambda e, hh=hh, cs_=cs_: e.activation(
                            out=ktv[:, :, hh, cs_], in_=tpk[:, :, cs_], func=AF.Copy),
                            reads=["bank3"], writes=[rkt], cost=c_act(128))
                scp = bankf(4).rearrange("p (h c) -> p h c", h=4)

                def scf(e):
                    for h in range(4):
                        blk = h if isA else h // 2
                        ins = e.matmul(scp[:, h, :], lhsT=kT[:, blk, tl], rhs=qT[:, h, tl], start=True, stop=True)
                    return ins
                P.add("pe", scf, reads=[kres, qres], writes=["bank4"], cost=c_pe([128] * 4))
                P.add("dve", lambda e: e.tensor_tensor(out=smk, in0=scp,
                                                       in1=maskbd.unsqueeze(1).to_broadcast([128, 4, 128]),
                                                       op=ALU.mult),
                      reads=["bank4", "maskbd"], writes=[rsm], cost=c_dve(512))
                op_ = bankf(5).rearrange("p (h c) -> p h c", h=4)

                def oin(e):
                    for h in range(4):
                        ins = e.matmul(op_[:, h, :], lhsT=smk[:, h, :], rhs=vbf[:, h * 128:(h + 1) * 128],
                                       start=(h == 0), stop=False, skip_group_check=True)
                    return ins
                P.add("pe", oin, reads=[rsm, rv], writes=["bank5"], cost=c_pe([128] * 4))
                pp = bankf(6)[:, 0:nblk * 128].rearrange("p (b c) -> p b c", b=nblk)
                for c in range(2):
                    cg = 2 * t + c
                    cl8 = 2 * tt + c
                    cl = slice(tt * 128 + c * 64, tt * 128 + (c + 1) * 64)
                    pr = slice(c * 64, (c + 1) * 64)
                    Sb_cur, Sb_nxt = Sbs[cg % 2], Sbs[(cg + 1) % 2]
                    rSc, rSn = "Sb%d" % (cg % 2), "Sb%d" % ((cg + 1) % 2)

                    def ointer(e, cl=cl, pr=pr, c=c, Sb_cur=Sb_cur):
                        for h in range(4):
                            blk = h if isA else h // 2
                            ins = e.matmul(op_[pr, h, :], lhsT=qT[:, h, cl], rhs=Sb_cur[:, blk, :],
                                           start=False, stop=(c == 1 and h == 3), skip_group_check=True)
                        return ins
                    P.add("pe", ointer, reads=[qres, rSc, "bank5"], writes=["bank5"], cost=c_pe([128] * 4))

                    def pst(e, pr=pr):
                        for h in range(4):
                            blk = h if isA else h // 2
                            first = True if isA else (h % 2 == 0)
                            last = True if isA else (h % 2 == 1)
                            ins = e.matmul(pp[:, blk, :], lhsT=kt[pr, h, :],
                                           rhs=vbf[pr, h * 128:(h + 1) * 128], start=first, stop=last)
                        return ins
                    P.add("pe", pst, reads=[rkt, rv], writes=["bank6"], cost=c_pe([128] * 4))
                    if cl8 == 0:
                        ecp, ecpres, pcol = ecs[(s - 1) % 3], "ec%d" % ((s - 1) % 3), 7
                    else:
                        ecp, ecpres, pcol = ec, ecres, cl8 - 1
                    T_prev, T_cur = Tts[(cg + 1) % 2], Tts[cg % 2]
                    rTp, rTc = "Tt%d" % ((cg + 1) % 2), "Tt%d" % (cg % 2)
                    for blk in range(nblk):
                        P.add("dve", lambda e, blk=blk, ecp=ecp, pcol=pcol, T_prev=T_prev, T_cur=T_cur:
                              e.scalar_tensor_tensor(out=T_cur[:, blk, :], in0=T_prev[:, blk, :],
                                                     scalar=ecp[:, blk, pcol:pcol + 1], in1=pp[:, blk, :],
                                                     op0=ALU.mult, op1=ALU.add),
                              reads=[rTp + "_%d" % blk, ecpres, "bank6"], writes=[rTc + "_%d" % blk],
                              cost=c_dve(128) + 0.07)
                    ebc = ec[:, 0:nblk, cl8:cl8 + 1].to_broadcast([128, nblk, 128])
                    P.add("pool", lambda e, ebc=ebc, Sb_nxt=Sb_nxt, T_cur=T_cur: e.tensor_tensor(
                        out=Sb_nxt[:, 0:nblk, :], in0=T_cur[:, 0:nblk, :], in1=ebc, op=ALU.mult),
                        reads=[rTc + "_%d" % b for b in range(nblk)] + [ecres], writes=[rSn],
                        cost=c_pool(128 * nblk))
                for h in range(4):
                    P.add("act", lambda e, h=h: e.activation(out=junk4[:, h, :], in_=op_[:, h, :], func=AF.Square,
                                                             accum_out=st4[:, h:h + 1]),
                          reads=["bank5"], writes=["junk4_%d" % h, rst + "a%d" % h], cost=c_act(128, acc=True))
                P.add("dve", lambda e: e.tensor_scalar(out=st4[:, 4:8], in0=st4[:, 0:4], scalar1=1.0 / 128,
                                                       scalar2=EPS, op0=ALU.mult, op1=ALU.add),
                      reads=[rst + "a%d" % h for h in range(4)], writes=[rst + "b"], cost=0.2)
                P.add("pool", lambda e: e.tensor_tensor(out=st4[:, 8:12], in0=st4[:, 4:8], in1=nhalf[:, 0:4],
                                                        op=ALU.pow),
                      reads=[rst + "b", "nhalf"], writes=[rst + "c"], cost=1.0)
                P.add("dve", lambda e: e.tensor_tensor(out=on1, in0=op_,
                                                       in1=st4[:, 8:12].unsqueeze(2).to_broadcast([128, 4, 128]),
                                                       op=ALU.mult),
                      reads=["bank5", rst + "c"], writes=["on1"], cost=c_dve(512))
                P.add("pool", lambda e: e.tensor_tensor(out=onb, in0=on1.rearrange("p h c -> p (h c)"), in1=G,
                                                        op=ALU.mult),
                      reads=["on1", rG], writes=[ron], cost=c_pool(512))
                tpo = bankb(7)[:, 512:1024].rearrange("p (b c) -> p b c", b=4)

                def tro(e):
                    for h in range(4):
                        ins = e.transpose(out=tpo[:, h, :], in_=onb[:, h * 128:(h + 1) * 128], identity=ident)
                    return ins
                P.add("pe", tro, reads=[ron, "ident"], writes=["bank7"], cost=4 * 0.08)
                hn2 = hna2 if isA else hnb2
                for h in range(4):
                    P.add("act", lambda e, h=h: e.activation(out=oT[:, h, tg_], in_=tpo[:, h, :], func=AF.Copy,
                                                             scale=hn2[:, h:h + 1]),
                          reads=["bank7", "hna2", "hnb2"], writes=[ores + "_%d" % h], cost=c_act(128, 1))

            prep(0)
            for s in range(NS):
                if s == 1 and prefetch is not None:
                    prefetch()
                if s + 1 < NS:
                    prep(s + 1)
                P.mark(br + "_tiles_s%d" % s)
                for tt in range(4):
                    tile(4 * s + tt)

        def prefetch_WB():
            grp = [(0, 512), (512, 512), (1024, 512), (1536, 16)]
            P.add("pool", lambda e: e.memset(dummy2, 0.0), writes=ph0_res + ["ph0_done"], cost=0.3)
            for i, (c0_, w) in enumerate(grp):
                dma(bufY[:, :, c0_:c0_ + w], win_v[:, :, 2048 + c0_:2048 + c0_ + w], reads=["ph0_done"],
                    writes=["bufY_%d" % i], q="pool", nbytes=8 * 2 * w)

        def prefetch_WZ():
            for i in range(4):
                dma(bufX[:, :, i * 512:(i + 1) * 512], win_v[:, :, 3600 + i * 512:3600 + (i + 1) * 512],
                    writes=["bufX_%d" % i], q="pool", nbytes=8 * 1024)

        gla_branch("A", prefetch_WB)
        P.mark("A_done")
        gla_branch("B", prefetch_WZ)
        P.mark("B_done")
        P.barrier()

        if debug:
            ar.reset(work0)
            dbf = ar.alloc([4 * T], F32)
            P.add("dve", lambda e: e.tensor_copy(out=dbf, in_=oTa.rearrange("p a b -> p (a b)")), reads=["oTa_%d" % h for h in range(4)],
                  writes=["dbf"])
            dma(dbg_a, dbf, "dbg", reads=["dbf"], writes=["dbg_a"])
            P.add("dve", lambda e: e.tensor_copy(out=dbf, in_=oTb.rearrange("p a b -> p (a b)")),
                  reads=["oTb_%d" % h for h in range(4)] + ["dbg_a"], writes=["dbf"])
            dma(dbg_b, dbf, "dbg", reads=["dbf"], writes=["dbg_b"])
            P.barrier()

        ar.reset(work0)
        by = ar.sub(bufY_rng)
        Wab = by.alloc([8, D], BF16)
        Wple = by.alloc([2, D], BF16)
        gpost_bc = by.alloc([D], F32)
        ypT = ar.alloc([8, T], BF16)
        Wout = ar.alloc([8, D], BF16)
        gple_bc = ar.alloc([D], F32)
        tas = [ar.alloc([512], F32) for _ in range(2)]
        tbs = [ar.alloc([512], F32) for _ in range(2)]
        Wg = ar.alloc([8, D], BF16)

        dma(Wab[:, 0:4, :], wa_v, writes=["Wab_a"], q="pool", nbytes=8192)
        dma(Wab[:, 4:8, :], wb_v, writes=["Wab_b"], q="pool", nbytes=8192)

        def prefetch_C2():
            for i in range(2):
                dma(Wout[:, :, i * 512:(i + 1) * 512], wout_v[:, :, i * 512:(i + 1) * 512], writes=["Wout%d" % i],
                    q="pool", nbytes=8 * 1024)
            dma(Wple, wple_v, writes=["Wple"], q="pool", nbytes=4096)
            for i in range(2):
                dma(Wg[:, :, i * 512:(i + 1) * 512], wg_v[:, :, i * 512:(i + 1) * 512], writes=["Wg%d" % i],
                    q="pool", nbytes=8 * 1024)
            dma(gpost_bc, gpost_d.partition_broadcast(128).rearrange("p a b -> p (a b)"), writes=["gpost"], nbytes=4096)
            dma(gple_bc, gple_d.partition_broadcast(128).rearrange("p a b -> p (a b)"), writes=["gple"], nbytes=4096)

        for s in range(NS):
            tok = slice(s * 512, (s + 1) * 512)
            hres = "hT%d" % s
            if s == 1:
                prefetch_C2()
            for fb in range(8):
                par = fb % 2
                b0 = 4 * par
                ta, tb = tas[fb % 2], tbs[fb % 2]
                rta, rtb = "ta%d" % (fb % 2), "tb%d" % (fb % 2)
                fcol = slice(fb * 128, (fb + 1) * 128)
                mm_acc(bankf(b0), [(bufX[:, k, fcol], hT[:, k, tok]) for k in range(8)],
                       reads=["bufX_%d" % (fb // 4), hres], writes=["bank%d" % b0])
                mm_acc(bankf(b0 + 1), [(bufX[:, k, 1024 + fb * 128:1024 + (fb + 1) * 128], hT[:, k, tok])
                                       for k in range(8)],
                       reads=["bufX_%d" % (2 + fb // 4), hres], writes=["bank%d" % (b0 + 1)])
                mm_acc(bankf(b0 + 2), [(Wab[:, k, fcol], oTa[:, k, tok]) for k in range(4)],
                       reads=["Wab_a"] + ["oTa_%d" % h for h in range(4)], writes=["bank%d" % (b0 + 2)])
                mm_acc(bankf(b0 + 3), [(Wab[:, 4 + k, fcol], oTb[:, k, tok]) for k in range(4)],
                       reads=["Wab_b"] + ["oTb_%d" % h for h in range(4)], writes=["bank%d" % (b0 + 3)])
                P.add("act", lambda e, ta=ta, b0=b0: e.activation(out=ta, in_=bankf(b0), func=AF.Sigmoid),
                      reads=["bank%d" % b0], writes=[rta], cost=c_act(512), tset="sig")
                P.add("act", lambda e, tb=tb, b0=b0: e.activation(out=tb, in_=bankf(b0 + 1), func=AF.Sigmoid),
                      reads=["bank%d" % (b0 + 1)], writes=[rtb], cost=c_act(512), tset="sig")
                P.add("dve", lambda e, ta=ta, b0=b0: e.tensor_tensor(out=ta, in0=ta, in1=bankf(b0 + 2), op=ALU.mult),
                      reads=[rta, "bank%d" % (b0 + 2)], writes=[rta], cost=c_dve(512))
                P.add("dve", lambda e, tb=tb, b0=b0: e.tensor_tensor(out=tb, in0=tb, in1=bankf(b0 + 3), op=ALU.mult),
                      reads=[rtb, "bank%d" % (b0 + 3)], writes=[rtb], cost=c_dve(512))
                P.add("pool", lambda e, fb=fb, tok=tok, ta=ta, tb=tb: e.tensor_tensor(out=ypT[:, fb, tok], in0=ta,
                                                                                      in1=tb, op=ALU.add),
                      reads=[rta, rtb], writes=["ypT%d" % s], cost=c_pool(512))
        P.mark("C1_done")
        P.barrier()

        ah = ar.sub(hT_rng)
        ax = ar.sub(bufX_rng)
        ao = ar.sub(oTa_rng)
        ab = ar.sub(oTb_rng)
        NBC = 3
        xsC = [ah.alloc([D], F32) for _ in range(NBC)]
        yo = [ah.alloc([D], F32) for _ in range(NBC)]
        tgC = [(ah if i < 2 else ab).alloc([D], F32) for i in range(NBC)]
        en = [ab.alloc([D], F32) for _ in range(NBC)]
        x1b = [ax.alloc([D], BF16) for _ in range(NBC)]
        x1T = [ax.alloc([8, 128], BF16) for _ in range(NBC)]
        psb = [ao.alloc([256], F32) for _ in range(NBC)]
        pb = [ao.alloc([256], BF16) for _ in range(NBC)]
        pT = [ao.alloc([2, 128], BF16) for _ in range(NBC)]
        junkCs = [ao.alloc([512], BF16) for _ in range(4)]
        stC = [ao.alloc([16], F32) for _ in range(NBC)]
        for t in range(NT):
            sl = t % NBC
            tl = slice(t * 128, (t + 1) * 128)
            xr, pr_, yr = "xsC%d" % sl, "psb%d" % sl, "yo%d" % sl
            sC = stC[sl]
            rs = "stC%d" % sl
            dma(xsC[sl], x_d[tl, :], xr, writes=[xr], nbytes=4096)
            dma(psb[sl], p_d[tl, :], pr_, writes=[pr_], nbytes=1024)
            for hf in range(2):
                mm_acc(bankf(4 + hf), [(ypT[:, k, tl], Wout[:, k, hf * 512:(hf + 1) * 512]) for k in range(8)],
                       reads=["ypT%d" % (t // 4), "Wout%d" % hf], writes=["bank%d" % (4 + hf)])
                P.add("act", lambda e, hf=hf, sC=sC: e.activation(out=junkCs[hf], in_=bankf(4 + hf), func=AF.Square,
                                                                  accum_out=sC[:, hf:hf + 1]),
                      reads=["bank%d" % (4 + hf)], writes=["junkC%d" % hf, rs + "a%d" % hf], cost=c_act(512, acc=True))
            P.add("dve", lambda e, sC=sC: e.tensor_tensor(out=sC[:, 2:3], in0=sC[:, 0:1], in1=sC[:, 1:2], op=ALU.add),
                  reads=[rs + "a0", rs + "a1"], writes=[rs + "b"], cost=0.2)
            P.add("dve", lambda e, sC=sC: e.tensor_scalar(out=sC[:, 3:4], in0=sC[:, 2:3], scalar1=1.0 / D, scalar2=EPS,
                                                          op0=ALU.mult, op1=ALU.add),
                  reads=[rs + "b"], writes=[rs + "c"], cost=0.2)
            P.add("pool", lambda e, sC=sC: e.tensor_tensor(out=sC[:, 4:5], in0=sC[:, 3:4], in1=nhalf[:, 0:1], op=ALU.pow),
                  reads=[rs + "c", "nhalf"], writes=[rs + "d"], cost=0.8)
            for hf in range(2):
                hs = slice(hf * 512, (hf + 1) * 512)
                P.add("dve", lambda e, hf=hf, hs=hs, sl=sl, sC=sC: e.scalar_tensor_tensor(
                    out=yo[sl][:, hs], in0=bankf(4 + hf), scalar=sC[:, 4:5], in1=gpost_bc[:, hs],
                    op0=ALU.mult, op1=ALU.mult),
                    reads=["bank%d" % (4 + hf), rs + "d", "gpost"], writes=[yr + "h%d" % hf], cost=c_dve(512))
            P.add("pool", lambda e, sl=sl: e.tensor_tensor(out=xsC[sl], in0=xsC[sl], in1=yo[sl], op=ALU.add),
                  reads=[xr, yr + "h0", yr + "h1"], writes=[xr], cost=c_pool(D))
            P.add("act", lambda e, sl=sl: e.activation(out=x1b[sl], in_=xsC[sl], func=AF.Copy),
                  reads=[xr], writes=["x1b%d" % sl], cost=c_act(D))
            tpx = bankb(7).rearrange("p (k c) -> p k c", k=8)

            def trx(e, tpx=tpx, sl=sl):
                for k in range(8):
                    ins = e.transpose(out=tpx[:, k, :], in_=x1b[sl][:, k * 128:(k + 1) * 128], identity=ident)
                return ins
            P.add("pe", trx, reads=["x1b%d" % sl, "ident"], writes=["bank7"], cost=8 * 0.08)
            P.add("act", lambda e, tpx=tpx, sl=sl: e.activation(out=x1T[sl], in_=tpx, func=AF.Copy),
                  reads=["bank7"], writes=["x1T%d" % sl], cost=c_act(D))
            P.add("pool", lambda e, sl=sl: e.tensor_copy(out=pb[sl], in_=psb[sl]), reads=[pr_], writes=["pb%d" % sl],
                  cost=c_pool(256))
            tpp = bankb(6)[:, 0:256].rearrange("p (k c) -> p k c", k=2)

            def trp(e, tpp=tpp, sl=sl):
                for k in range(2):
                    ins = e.transpose(out=tpp[:, k, :], in_=pb[sl][:, k * 128:(k + 1) * 128], identity=ident)
                return ins
            P.add("pe", trp, reads=["pb%d" % sl, "ident"], writes=["bank6"], cost=2 * 0.08)
            P.add("act", lambda e, tpp=tpp, sl=sl: e.activation(out=pT[sl], in_=tpp, func=AF.Copy),
                  reads=["bank6"], writes=["pT%d" % sl], cost=c_act(256))
            for hf in range(2):
                bk = 2 + hf
                mm_acc(bankf(bk), [(pT[sl][:, k, :], Wple[:, k, hf * 512:(hf + 1) * 512]) for k in range(2)],
                       reads=["pT%d" % sl, "Wple"], writes=["bank%d" % bk])
                P.add("act", lambda e, hf=hf, bk=bk, sC=sC: e.activation(out=junkCs[2 + hf], in_=bankf(bk), func=AF.Square,
                                                                         accum_out=sC[:, 8 + hf:9 + hf]),
                      reads=["bank%d" % bk], writes=["junkC%d" % (2 + hf), rs + "e%d" % hf], cost=c_act(512, acc=True))
            P.add("dve", lambda e, sC=sC: e.tensor_tensor(out=sC[:, 10:11], in0=sC[:, 8:9], in1=sC[:, 9:10], op=ALU.add),
                  reads=[rs + "e0", rs + "e1"], writes=[rs + "f"], cost=0.2)
            P.add("dve", lambda e, sC=sC: e.tensor_scalar(out=sC[:, 11:12], in0=sC[:, 10:11], scalar1=1.0 / D,
                                                          scalar2=EPS, op0=ALU.mult, op1=ALU.add),
                  reads=[rs + "f"], writes=[rs + "g"], cost=0.2)
            P.add("pool", lambda e, sC=sC: e.tensor_tensor(out=sC[:, 12:13], in0=sC[:, 11:12], in1=nhalf[:, 0:1],
                                                           op=ALU.pow),
                  reads=[rs + "g", "nhalf"], writes=[rs + "h"], cost=0.8)
            for hf in range(2):
                hs = slice(hf * 512, (hf + 1) * 512)
                bk = 2 + hf
                P.add("dve", lambda e, hs=hs, bk=bk, sl=sl, sC=sC: e.scalar_tensor_tensor(
                    out=en[sl][:, hs], in0=bankf(bk), scalar=sC[:, 12:13], in1=gple_bc[:, hs],
                    op0=ALU.mult, op1=ALU.mult),
                    reads=["bank%d" % bk, rs + "h", "gple"], writes=["en%d_%d" % (sl, hf)], cost=c_dve(512))
            for hf in range(2):
                hs = slice(hf * 512, (hf + 1) * 512)
                mm_acc(bankf(hf), [(x1T[sl][:, k, :], Wg[:, k, hs]) for k in range(8)],
                       reads=["x1T%d" % sl, "Wg%d" % hf], writes=["bank%d" % hf])
                P.add("act", lambda e, hf=hf, hs=hs, sl=sl: e.activation(out=tgC[sl][:, hs], in_=bankf(hf),
                                                                         func=AF.Sigmoid),
                      reads=["bank%d" % hf], writes=["tgC%d_%d" % (sl, hf)], cost=c_act(512), tset="sig")
                P.add("dve", lambda e, hs=hs, sl=sl: e.tensor_tensor(out=en[sl][:, hs], in0=tgC[sl][:, hs],
                                                                     in1=en[sl][:, hs], op=ALU.mult),
                      reads=["tgC%d_%d" % (sl, hf), "en%d_%d" % (sl, hf)], writes=["en%d_%d" % (sl, hf)],
                      cost=c_dve(512))
                P.add("dve", lambda e, hs=hs, sl=sl: e.tensor_tensor(out=yo[sl][:, hs], in0=en[sl][:, hs],
                                                                     in1=xsC[sl][:, hs], op=ALU.add),
                      reads=["en%d_%d" % (sl, hf), xr, yr + "h%d" % hf], writes=[yr + "h%d" % hf],
                      cost=c_dve(512))
            dma(out_d[tl, :], yo[sl], "out%d" % sl, reads=[yr + "h0", yr + "h1"],
                writes=[yr + "h0", yr + "h1", "OUT%d" % sl], nbytes=4096)
        P.barrier()
        P.add("sp", None, reads=["OUT0", "OUT1", "OUT2"] + (["dbg_a", "dbg_b"] if debug else []), force=True)
        build_nc.n_ops = len(P.ops)
        build_nc.marks = P.marks
        P.emit(nc, st)
        build_nc.est_time = getattr(P, "est_time", None)
        fin = getattr(P, "finish", None)
        if fin:
            build_nc.mark_times = [(nm, max(fin[:i]) if i else 0.0) for nm, i in P.marks]
            build_nc.busy = P.busy
            build_nc.P = P
    return nc


def _consts():
    ident = np.eye(128, dtype=np.float32)
    j = np.arange(128)[:, None]
    i = np.arange(128)[None, :]
    maskbd = ((j <= i) & ((j // 64) == (i // 64))).astype(np.float32)
    m01 = np.ones((128, 512), np.float32)
    m01[:, ::64] = 0.0
    return ident, maskbd, m01


def make_in_maps(x, p, w_in, lb_logits, w_gla_up, b_gla, norm_pre, norm_post, head_norm_a, head_norm_b,
                 w_branch_a, w_branch_b, w_out, w_ple, w_ple_gate, norm_ple):
    f = lambda a: np.ascontiguousarray(np.asarray(a, dtype=np.float32))
    ident, maskbd, m01 = _consts()
    lbl = f(np.asarray(lb_logits).reshape(2, 4, 128).transpose(2, 0, 1).reshape(128, 8))
    shared = {
        "w_in": f(np.asarray(w_in)[0]),
        "w_a": f(np.asarray(w_branch_a)[0]),
        "w_b": f(np.asarray(w_branch_b)[0]),
        "w_out": f(np.asarray(w_out)[0]),
        "w_ple": f(np.asarray(w_ple)[0]),
        "w_g": f(np.asarray(w_ple_gate)[0]),
        "w_up": f(np.asarray(w_gla_up)[0]),
        "lbl": lbl,
        "gpre": f(np.asarray(norm_pre)[0].reshape(1, D)),
        "hna": f(np.asarray(head_norm_a)[0].reshape(4, 128).T),
        "hnb": f(np.asarray(head_norm_b)[0].reshape(4, 128).T),
        "bgl": f(np.asarray(b_gla)[0].reshape(2, 128).T),
        "gpost": f(np.asarray(norm_post)[0].reshape(1, D)),
        "gple": f(np.asarray(norm_ple)[0].reshape(1, D)),
        "ident": ident,
        "maskbd": maskbd,
        "mask01": m01,
    }
    xs = np.asarray(x)
    ps = np.asarray(p)
    maps = []
    for b in range(8):
        m = dict(shared)
        m["x"] = f(xs[b])
        m["p"] = f(ps[0, b])
        maps.append(m)
    return maps


_NC_CACHE = {}


def kernel(x, p, w_in, lb_logits, w_gla_up, b_gla, norm_pre, norm_post, head_norm_a, head_norm_b,
           w_branch_a, w_branch_b, w_out, w_ple, w_ple_gate, norm_ple):
    in_maps = make_in_maps(x, p, w_in, lb_logits, w_gla_up, b_gla, norm_pre, norm_post, head_norm_a, head_norm_b,
                           w_branch_a, w_branch_b, w_out, w_ple, w_ple_gate, norm_ple)
    if "nc" not in _NC_CACHE:
        _NC_CACHE["nc"] = build_nc(False)
    nc = _NC_CACHE["nc"]
    res = run_bass_kernel_spmd(nc, in_maps, core_ids=list(range(8)))
    out = np.stack([np.asarray(r["out"]) for r in res.results], axis=0).astype(np.float32)
    return out
```

```python
import contextlib
import sys
import numpy as np
import concourse.bass as bass
import concourse.mybir as mybir
from concourse.alu_op_type import AluOpType as ALU
from concourse.bass_utils import run_bass_kernel_spmd

F32 = mybir.dt.float32
BF16 = mybir.dt.bfloat16
AF = mybir.ActivationFunctionType
AX = mybir.AxisListType

ENGS = ("sp", "act", "pool", "dve", "pe")

T = 2048
D = 1024
NT = 16
NS = 4
DIN = 5648
EPS = 1e-6
TSWITCH = 1.3


class Op:
    __slots__ = ("idx", "eng", "fn", "reads", "writes", "raw", "oth", "need_inc",
                 "dma_key", "dma_cnt", "inc_cnt", "cost", "lat", "tset", "deps", "prio", "line", "start", "bind", "dbytes")


def c_pe(cols):
    return sum(n / 2400.0 + 0.008 for n in cols)


def c_act(n, naps=0, acc=False):
    return 0.1 + n / 1100.0 + 0.06 * naps + (0.09 if acc else 0.0)


def c_dve(n, scan=False):
    return 0.15 + (2 * n if scan else n) / 960.0


def c_pool(n):
    return 0.3 + n / 520.0


class Prog:
    def __init__(self):
        self.ops = []
        self.last_writer = {}
        self.readers = {}
        self.dma_counts = {}
        self.last_dma = {}
        self.last_compute = {}
        self.pending = {e: set() for e in ENGS}
        self.max_ops = None
        self.marks = []
        self.do_schedule = True
        self.has_succ = set()
        self.bar_start = 0
        self.join = None
        self.join_fn = None

    def mark(self, name):
        self.marks.append((name, len(self.ops)))

    def add(self, eng, fn, reads=(), writes=(), dma_key=None, force=False, cost=0.3, lat=None, tset=None):
        if self.max_ops is not None and len(self.ops) >= self.max_ops and not force:
            return None
        op = Op()
        op.idx = len(self.ops)
        op.line = sys._getframe(1).f_lineno if sys._getframe(1).f_code.co_name not in ("dma", "mm_acc") \
            else sys._getframe(2).f_lineno
        op.eng = eng
        op.fn = fn
        op.reads = tuple(reads)
        op.writes = tuple(writes)
        op.need_inc = False
        op.dma_key = dma_key
        op.dma_cnt = 0
        op.inc_cnt = 0
        op.cost = cost
        op.lat = cost if lat is None else lat
        op.tset = tset
        op.dbytes = 0
        raw, oth = set(), set()
        for r in op.reads:
            if r in self.last_writer:
                raw.add(self.last_writer[r])
        for w in op.writes:
            if w in self.last_writer:
                oth.add(self.last_writer[w])
            for rd in self.readers.get(w, ()):
                oth.add(rd)
        for w in op.writes:
            self.last_writer[w] = op.idx
            self.readers[w] = []
        for r in op.reads:
            if r not in op.writes:
                self.readers.setdefault(r, []).append(op.idx)
        if self.pending[eng]:
            raw |= self.pending[eng]
            self.pending[eng] = set()
        if self.join is not None:
            raw.add(self.join)
        raw.discard(op.idx)
        oth.discard(op.idx)
        op.raw = raw
        op.oth = oth - raw
        self.has_succ |= raw
        self.has_succ |= oth
        if dma_key is not None:
            self.dma_counts[dma_key] = self.dma_counts.get(dma_key, 0) + 1
            op.dma_cnt = self.dma_counts[dma_key]
            self.last_dma[dma_key] = op.idx
        elif fn is not None:
            self.last_compute[eng] = op.idx
        self.ops.append(op)
        return op

    def barrier(self):
        sinks = set(i for i in range(self.bar_start, len(self.ops)) if i not in self.has_succ)
        self.pending["pool"] |= sinks
        j = self.add("pool", self.join_fn, cost=0.3, force=True)
        self.join = j.idx
        self.bar_start = j.idx

    def _sync_deps(self, op):
        out = []
        for d in sorted(op.raw | op.oth):
            dop = self.ops[d]
            if dop.dma_key is None and dop.eng == op.eng and op.dma_key is None and op.eng == "pe":
                continue
            out.append(dop)
        return out

    def schedule(self):
        ops = self.ops
        n = len(ops)
        succ = [[] for _ in range(n)]
        ndeps = [0] * n
        for op in ops:
            op.deps = sorted(op.raw | op.oth)
            ndeps[op.idx] = len(op.deps)
            for d in op.deps:
                succ[d].append(op.idx)
        for op in reversed(ops):
            best = 0.0
            for s in succ[op.idx]:
                best = max(best, ops[s].prio)
            op.prio = best + op.lat
        order = {e: [] for e in ENGS}
        if not self.do_schedule:
            for op in ops:
                order[op.eng].append(op)
            return order
        finish = [0.0] * n
        free = {e: 0.0 for e in ENGS}
        dma_busy = [0.0]
        cur_tset = [None]
        ready = {e: [] for e in ENGS}
        for op in ops:
            if ndeps[op.idx] == 0:
                ready[op.eng].append(op)
        left = n
        while left:
            best = None
            for e in ENGS:
                for op in ready[e]:
                    r = 0.0
                    rb = None
                    for d in op.deps:
                        f = finish[d] + (0.0 if (ops[d].eng == e and ops[d].dma_key is None) else 0.15)
                        if f > r:
                            r = f
                            rb = d
                    s = max(free[e], r)
                    op.bind = rb if r >= free[e] else -1
                    if e == "act" and op.tset is not None and cur_tset[0] is not None and cur_tset[0] != op.tset:
                        s += TSWITCH
                    key = (s, -op.prio, op.idx)
                    if best is None or key < best[0]:
                        best = (key, op)
            (s, _, _), op = best
            e = op.eng
            ready[e].remove(op)
            if e == "act" and op.tset is not None:
                cur_tset[0] = op.tset
            if op.bind == -1:
                op.bind = ("eng", order[e][-1].idx if order[e] else None)
            else:
                op.bind = ("dep", op.bind)
            op.start = s
            free[e] = s + op.cost
            if op.dma_key is not None and op.dbytes:
                t0 = max(s + 1.5, dma_busy[0])
                dma_busy[0] = t0 + op.dbytes / 260e3
                finish[op.idx] = dma_busy[0] + 0.5
            else:
                finish[op.idx] = s + op.lat
            order[e].append(op)
            left -= 1
            for sidx in succ[op.idx]:
                ndeps[sidx] -= 1
                if ndeps[sidx] == 0:
                    ready[ops[sidx].eng].append(ops[sidx])
        self.est_time = max(finish) if n else 0.0
        self.finish = finish
        self.busy = {e: sum(op.cost for op in order[e]) for e in ENGS}
        return order

    def emit(self, nc, stack):
        order = self.schedule()
        for op in self.ops:
            for d in self._sync_deps(op):
                if d.dma_key is None:
                    d.need_inc = True
        eng_sem = {e: stack.enter_context(nc.semaphore("s_" + e)) for e in ENGS}
        key_sem = {k: stack.enter_context(nc.semaphore("d_" + k)) for k in self.dma_counts}
        for e in ENGS:
            c = 0
            for op in order[e]:
                if op.dma_key is None and op.need_inc:
                    c += 1
                    op.inc_cnt = c
        block = stack.enter_context(nc.Block())
        prog = self

        def run(engname, eng):
            waited = {}
            for op in order[engname]:
                for d in prog._sync_deps(op):
                    if d.dma_key is not None:
                        sem, val, k = key_sem[d.dma_key], 16 * d.dma_cnt, "d_" + d.dma_key
                    else:
                        sem, val, k = eng_sem[d.eng], d.inc_cnt, "s_" + d.eng
                    if waited.get(k, 0) >= val:
                        continue
                    waited[k] = val
                    eng.wait_ge(sem, val)
                if op.fn is None:
                    continue
                ins = op.fn(eng)
                if op.dma_key is not None:
                    ins.then_inc(key_sem[op.dma_key], 16)
                elif op.need_inc:
                    ins.then_inc(eng_sem[op.eng], 1)

        @block.sync
        def _(e):
            run("sp", e)

        @block.scalar
        def _(e):
            run("act", e)

        @block.gpsimd
        def _(e):
            run("pool", e)

        @block.vector
        def _(e):
            run("dve", e)

        @block.tensor
        def _(e):
            run("pe", e)


class Arena:
    def __init__(self, nc, name, nbytes, ap=None):
        self.words = nbytes // 4
        if ap is None:
            self.t = nc.alloc_sbuf_tensor(name, [128, self.words], F32)
            self.ap = self.t.ap()
        else:
            self.ap = ap
        self.off = 0
        self.hi = 0
        self.last = (0, 0)

    def sub(self, rng):
        off, words = rng
        return Arena(None, None, words * 4, ap=self.ap[:, off:off + words])

    def mark(self):
        return self.off

    def reset(self, m):
        self.off = m

    def alloc(self, free_shape, dt):
        n = 1
        for s in free_shape:
            n *= s
        words = n if dt == F32 else (n + 1) // 2
        words = (words + 7) // 8 * 8
        assert self.off + words <= self.words, ("arena overflow", self.off, words, self.words)
        v = self.ap[:, self.off:self.off + words]
        self.last = (self.off, words)
        self.off += words
        self.hi = max(self.hi, self.off)
        if dt != F32:
            v = v.bitcast(dt)
        v = v[:, 0:n]
        if len(free_shape) == 2:
            v = v.rearrange("p (a b) -> p a b", a=free_shape[0])
        elif len(free_shape) == 3:
            v = v.rearrange("p (a b c) -> p a b c", a=free_shape[0], b=free_shape[1])
        return v


def build_nc(debug=False, max_ops=None, do_schedule=True):
    nc = bass.Bass("TRN2", target_bir_lowering=False)

    def din(name, shape):
        return nc.dram_tensor(name, list(shape), F32, kind="ExternalInput").ap()

    x_d = din("x", [T, D])
    p_d = din("p", [T, 256])
    win_d = din("w_in", [D, DIN])
    wa_d = din("w_a", [512, D])
    wb_d = din("w_b", [512, D])
    wout_d = din("w_out", [D, D])
    wple_d = din("w_ple", [256, D])
    wg_d = din("w_g", [D, D])
    wup_d = din("w_up", [16, 256])
    lbl_d = din("lbl", [128, 8])
    gpre_d = din("gpre", [1, D])
    hna_d = din("hna", [128, 4])
    hnb_d = din("hnb", [128, 4])
    bgl_d = din("bgl", [128, 2])
    gpost_d = din("gpost", [1, D])
    gple_d = din("gple", [1, D])
    ident_d = din("ident", [128, 128])
    maskbd_d = din("maskbd", [128, 128])
    mask01_d = din("mask01", [128, 512])
    out_d = nc.dram_tensor("out", [T, D], F32, kind="ExternalOutput").ap()
    if debug:
        dbg_a = nc.dram_tensor("dbg_a", [128, 4 * T], F32, kind="ExternalOutput").ap()
        dbg_b = nc.dram_tensor("dbg_b", [128, 4 * T], F32, kind="ExternalOutput").ap()

    win_v = win_d.rearrange("(k p) n -> p k n", p=128)
    wa_v = wa_d.rearrange("(k p) n -> p k n", p=128)
    wb_v = wb_d.rearrange("(k p) n -> p k n", p=128)
    wout_v = wout_d.rearrange("(k p) n -> p k n", p=128)
    wple_v = wple_d.rearrange("(k p) n -> p k n", p=128)
    wg_v = wg_d.rearrange("(k p) n -> p k n", p=128)

    P = Prog()
    P.max_ops = max_ops
    P.do_schedule = do_schedule
    with contextlib.ExitStack() as st:
        ar = Arena(nc, "arena", 207 * 1024 + 768)
        banks = [st.enter_context(nc.psum_tensor("bank%d" % i, [128, 512], F32)) for i in range(8)]

        def bankf(i):
            return banks[i][:]

        def bankb(i):
            return banks[i][:].bitcast(BF16)

        hT = ar.alloc([8, T], BF16)
        hT_rng = ar.last
        oTa = ar.alloc([4, T], BF16)
        oTa_rng = ar.last
        oTb = ar.alloc([4, T], BF16)
        oTb_rng = ar.last
        ident = ar.alloc([128], BF16)
        maskbd = ar.alloc([128], F32)
        mask01 = ar.alloc([512], F32)
        sm_lbl = ar.alloc([8], F32)
        hna = ar.alloc([4], F32)
        hnb = ar.alloc([4], F32)
        hna2 = ar.alloc([4], F32)
        hnb2 = ar.alloc([4], F32)
        bgl = ar.alloc([2], F32)
        nbgl = ar.alloc([2], F32)
        c1 = ar.alloc([4], F32)
        nc1 = ar.alloc([4], F32)
        lnc1 = ar.alloc([4], F32)
        c0 = ar.alloc([4], F32)
        lbd = ar.alloc([4], F32)
        lbt = ar.alloc([4], F32)
        epsc = ar.alloc([1], F32)
        onec = ar.alloc([1], F32)
        nhalf = ar.alloc([8], F32)
        wup = ar.alloc([256], BF16)
        stage = None
        bufX = ar.alloc([8, 2048], BF16)
        bufX_rng = ar.last
        bufY = ar.alloc([8, 1552], BF16)
        bufY_rng = ar.last
        dummy = ar.alloc([8], F32)
        dummy2 = ar.alloc([8], F32)
        P.join_fn = lambda e: e.memset(dummy, 0.0)
        work0 = ar.mark()

        ukey = [0]

        def dma(dst, src, key=None, reads=(), writes=(), q="sp", nbytes=65536):
            if key is None:
                ukey[0] += 1
                key = "u%d" % ukey[0]
            lat = 2.0 + nbytes * 128 / 200e3
            occ = 0.45 if q == "sp" else 1.1
            o_ = P.add(q, lambda e: e.dma_start(out=dst, in_=src), reads=reads, writes=writes, dma_key=key,
                       cost=occ, lat=lat)
            if o_ is not None:
                nel = 1
                for s_ in src.shape:
                    nel *= s_
                o_.dbytes = max(nel * 4, 16384)

        stage_ctr = [0]

        def load_cast(dst, src, ncols, scale, dst_res, eng="pool", extra_reads=()):
            c = 0
            while c < ncols:
                w = min(1024, ncols - c)
                sl = stage_ctr[0] % 2
                stage_ctr[0] += 1
                sres = "stg%d" % sl
                sv = stage[sl][:, 0:w]
                dma(sv, src[:, c:c + w], sres, writes=[sres], nbytes=4 * w)
                dv = dst[:, c:c + w]
                if eng == "pool":
                    P.add("pool", lambda e, dv=dv, sv=sv: e.tensor_scalar(out=dv, in0=sv, scalar1=scale, scalar2=0.0,
                                                                          op0=ALU.mult, op1=ALU.add),
                          reads=[sres] + list(extra_reads), writes=[dst_res], cost=c_pool(w) * 0.6)
                else:
                    P.add("dve", lambda e, dv=dv, sv=sv: e.tensor_scalar(out=dv, in0=sv, scalar1=scale, scalar2=None,
                                                                         op0=ALU.mult),
                          reads=[sres] + list(extra_reads), writes=[dst_res], cost=c_dve(w))
                c += w

        def mm_acc(out, pairs, reads, writes):
            def fn(e):
                n = len(pairs)
                for i, (l, r) in enumerate(pairs):
                    ins = e.matmul(out, lhsT=l, rhs=r, start=(i == 0), stop=(i == n - 1))
                return ins
            cols = [r.shape[-1] for (_, r) in pairs]
            P.add("pe", fn, reads=reads, writes=writes, cost=c_pe(cols))

        dma(ident, ident_d, reads=(), writes=["ident"], q="pool", nbytes=256)
        dma(wup[0:16, :], wup_d, writes=["wup"], q="pool", nbytes=512)
        dma(maskbd, maskbd_d, writes=["maskbd"], nbytes=512)
        dma(mask01, mask01_d, writes=["mask01"], nbytes=2048)
        dma(sm_lbl, lbl_d, writes=["lbl"], nbytes=32)
        dma(hna, hna_d, writes=["hna"], nbytes=16)
        dma(hnb, hnb_d, writes=["hnb"], nbytes=16)
        dma(bgl, bgl_d, writes=["bgl"], nbytes=8)
        P.add("pool", lambda e: e.memset(epsc, EPS), writes=["epsc"], cost=0.3)
        P.add("pool", lambda e: e.memset(onec, 1.0), writes=["onec"], cost=0.3)
        P.add("pool", lambda e: e.memset(nhalf, -0.5), writes=["nhalf"], cost=0.3)
        P.add("dve", lambda e: e.tensor_tensor(out=lbd, in0=sm_lbl[:, 0:4], in1=sm_lbl[:, 4:8], op=ALU.subtract),
              reads=["lbl"], writes=["lbd"], cost=0.2)
        P.add("act", lambda e: e.activation(out=lbt, in_=lbd, func=AF.Tanh, scale=0.5), reads=["lbd"], writes=["lbt"],
              cost=0.3, tset="tanh")
        P.add("dve", lambda e: e.tensor_scalar(out=c1, in0=lbt, scalar1=-0.25, scalar2=0.25, op0=ALU.mult, op1=ALU.add),
              reads=["lbt"], writes=["c1"], cost=0.2)
        P.add("dve", lambda e: e.tensor_scalar(out=c0, in0=lbt, scalar1=0.25, scalar2=0.75, op0=ALU.mult, op1=ALU.add),
              reads=["lbt"], writes=["c0"], cost=0.2)
        P.add("dve", lambda e: e.tensor_scalar(out=nc1, in0=lbt, scalar1=0.25, scalar2=-0.25, op0=ALU.mult, op1=ALU.add),
              reads=["lbt"], writes=["nc1"], cost=0.2)
        P.add("act", lambda e: e.activation(out=lnc1, in_=c1, func=AF.Ln), reads=["c1"], writes=["lnc1"], cost=0.3,
              tset="ln")
        P.add("dve", lambda e: e.tensor_scalar(out=nbgl, in0=bgl, scalar1=-1.0, scalar2=None, op0=ALU.mult),
              reads=["bgl"], writes=["nbgl"], cost=0.2)
        P.add("dve", lambda e: e.tensor_scalar(out=hna2, in0=hna, scalar1=1.0, scalar2=None, op0=ALU.mult),
              reads=["hna"], writes=["hna2"], cost=0.2)
        P.add("dve", lambda e: e.tensor_scalar(out=hnb2, in0=hnb, scalar1=1.0, scalar2=None, op0=ALU.mult),
              reads=["hnb"], writes=["hnb2"], cost=0.2)
        P.mark("consts_done")

        a0 = ar.sub(bufY_rng)
        gpre_bc = a0.alloc([D], F32)
        xs = [a0.alloc([D], F32) for _ in range(3)]
        hb = [a0.alloc([D], BF16) for _ in range(2)]
        junk0 = a0.alloc([D], BF16)
        st0 = [a0.alloc([4], F32) for _ in range(2)]
        dma(gpre_bc, gpre_d.partition_broadcast(128).rearrange("p a b -> p (a b)"), writes=["gpre_bc"], nbytes=4096)
        ph0_res = ["gpre_bc", "junk0"]

        def wa_load(grp, reads=()):
            c0_ = grp * 512
            dma(bufX[:, :, c0_:c0_ + 512], win_v[:, :, c0_:c0_ + 512], reads=reads, writes=["bufX_%d" % grp],
                q="pool", nbytes=8 * 1024)


        for t in range(NT):
            sl3 = t % 3
            sl = t % 2
            xr, hr, sr = "xs%d" % sl3, "hb%d" % sl, "st0_%d" % sl
            ph0_res += [xr, hr, sr + "a", sr + "b", sr + "c"]
            tpb = bankb(6 + sl).rearrange("p (k c) -> p k c", k=8)
            dma(xs[sl3], x_d[t * 128:(t + 1) * 128, :], xr, writes=[xr], nbytes=4096)
            if t < 4:
                wa_load(t)
            P.add("act", lambda e, sl=sl, sl3=sl3: e.activation(out=junk0, in_=xs[sl3], func=AF.Square,
                                                               accum_out=st0[sl][:, 0:1]),
                  reads=[xr], writes=["junk0", sr + "a"], cost=c_act(D, acc=True))
            P.add("act", lambda e, sl=sl: e.activation(out=st0[sl][:, 1:2], in_=st0[sl][:, 0:1], func=AF.Ln,
                                                       scale=1.0 / D, bias=epsc[:, 0:1]),
                  reads=[sr + "a", "epsc"], writes=[sr + "b"], cost=0.3, tset="ln")
            P.add("act", lambda e, sl=sl: e.activation(out=st0[sl][:, 2:3], in_=st0[sl][:, 1:2], func=AF.Exp,
                                                       scale=-0.5),
                  reads=[sr + "b"], writes=[sr + "c"], cost=0.3)
            P.add("dve", lambda e, sl=sl, sl3=sl3: e.scalar_tensor_tensor(out=hb[sl], in0=xs[sl3],
                                                                         scalar=st0[sl][:, 2:3], in1=gpre_bc,
                                                                         op0=ALU.mult, op1=ALU.mult),
                  reads=[xr, sr + "c", "gpre_bc"], writes=[hr], cost=c_dve(D))

            def tr(e, sl=sl, tpb=tpb):
                for k in range(8):
                    ins = e.transpose(out=tpb[:, k, :], in_=hb[sl][:, k * 128:(k + 1) * 128], identity=ident)
                return ins
            P.add("pe", tr, reads=[hr, "ident"], writes=["bank%d" % (6 + sl)], cost=8 * 0.08)
            P.add("dve", lambda e, t=t, tpb=tpb: e.tensor_copy(out=hT[:, :, t * 128:(t + 1) * 128], in_=tpb),
                  reads=["bank%d" % (6 + sl)], writes=["hT%d" % (t // 4)], cost=c_dve(D) * 0.7)
        P.mark("phase0_done")

        ar.reset(work0)
        NB = 4
        tmp = [[ar.alloc([512], F32) for _ in range(6)] for _ in range(2)]
        qTs = [ar.alloc([4, 512], BF16) for _ in range(2)]
        kTs = [ar.alloc([NB, 512], BF16) for _ in range(2)]
        ecs = [ar.alloc([NB, 8], F32) for _ in range(3)]
        NBT = 4
        vbfs = [ar.alloc([512], BF16) for _ in range(NBT)]
        Gs = [ar.alloc([512], F32) for _ in range(NBT)]
        tgs = Gs
        kts = [ar.alloc([4, 128], BF16) for _ in range(NBT)]
        smks = [ar.alloc([4, 128], BF16) for _ in range(NBT)]
        onbs = [ar.alloc([512], BF16) for _ in range(NBT)]
        on1 = ar.alloc([4, 128], F32)
        Tts = [ar.alloc([NB, 128], F32) for _ in range(2)]
        Sbs = [ar.alloc([NB, 128], BF16) for _ in range(2)]
        st4s = [ar.alloc([16], F32) for _ in range(4)]
        aTs = [ar.alloc([512], BF16) for _ in range(2)]
        junk4 = ar.alloc([4, 128], BF16)

        def gla_branch(br, prefetch=None):
            isA = br == "A"
            nblk = 4 if isA else 2
            dk = 128 if isA else 64
            W = bufX if isA else bufY
            oT = oTa if isA else oTb
            ores = "oTa" if isA else "oTb"
            qc0, fc0, vc0, gc0 = (0, 512, 1024, 1536) if isA else (0, 256, 512, 1024)
            if isA:
                wq, wf, wv, wg_ = ["bufX_0"], ["bufX_1"], ["bufX_2"], ["bufX_3"]
            else:
                wq, wf, wv, wg_ = ["bufY_0"], ["bufY_0"], ["bufY_1"], ["bufY_2"]

            P.add("pool", lambda e: e.memset(Tts[1], 0.0), writes=["Tt1_%d" % b for b in range(NB)], cost=c_pool(512))
            P.add("pool", lambda e: e.memset(Sbs[0], 0.0), writes=["Sb0"], cost=c_pool(256))
            for i in range(3):
                P.add("pool", lambda e, i=i: e.memset(ecs[i], 0.0), writes=["ec%d" % i], cost=0.3)
            if not isA:
                for i in range(2):
                    P.add("pool", lambda e, i=i: e.memset(qTs[i], 0.0), writes=["qT%d" % i], cost=c_pool(1024))
                for i in range(NBT):
                    P.add("pool", lambda e, i=i: e.memset(kts[i], 0.0), writes=["kt%d" % i], cost=c_pool(256))

            def prep(s):
                ss = s % 2
                qT, kT, ec = qTs[ss], kTs[ss], ecs[s % 3]
                qres, kres, ecres = "qT%d" % ss, "kT%d" % ss, "ec%d" % (s % 3)
                tok = slice(s * 512, (s + 1) * 512)
                hres = "hT%d" % s
                aT = aTs[ss]
                if not isA:
                    mm_acc(bankf(4)[0:16, :], [(W[:, k, 1536:1552], hT[:, k, tok]) for k in range(8)],
                           reads=["bufY_3", hres], writes=["bank4"])
                    P.add("act", lambda e, aT=aT: e.activation(out=aT[0:16, :], in_=bankf(4)[0:16, :], func=AF.Copy),
                          reads=["bank4"], writes=["aT%d" % ss], cost=c_act(512))
                for blk in range(nblk):
                    ts_ = (s * nblk + blk) % 2
                    t1, t2, t3, t4, Ebt, Enb = tmp[ts_]
                    r1, r2, r3, r4, rE, rN = ["tmp%d_%d" % (ts_, i) for i in range(6)]
                    qcol = slice(qc0 + blk * 128, qc0 + (blk + 1) * 128)
                    fcol = slice(fc0 + blk * 128, fc0 + (blk + 1) * 128)
                    mm_acc(bankf(0), [(W[:, k, qcol], hT[:, k, tok]) for k in range(8)],
                           reads=wq + [hres], writes=["bank0"])
                    mm_acc(bankf(1), [(W[:, k, fcol], hT[:, k, tok]) for k in range(8)],
                           reads=wf + [hres], writes=["bank1"])
                    if isA:
                        P.add("act", lambda e, t1=t1: e.activation(out=t1, in_=bankf(0), func=AF.Silu),
                              reads=["bank0"], writes=[r1], cost=c_act(512), tset="tanh")
                        P.add("act", lambda e, t2=t2: e.activation(out=t2, in_=bankf(1), func=AF.Tanh, scale=0.5),
                              reads=["bank1"], writes=[r2], cost=c_act(512), tset="tanh")
                        P.add("act", lambda e, blk=blk, t2=t2, t3=t3: e.activation(
                            out=t3, in_=t2, func=AF.Ln, scale=c1[:, blk:blk + 1], bias=c0[:, blk:blk + 1]),
                            reads=[r2, "c1", "c0"], writes=[r3], cost=c_act(512, 2), tset="ln")
                    else:
                        mm_acc(bankf(4), [(wup[0:16, blk * 128:(blk + 1) * 128], aT[0:16, :])],
                               reads=["wup", "aT%d" % ss], writes=["bank4"])
                        P.add("act", lambda e, blk=blk, t1=t1: e.activation(out=t1, in_=bankf(4), func=AF.Exp, scale=-1.0,
                                                                            bias=nbgl[:, blk:blk + 1]),
                              reads=["bank4", "nbgl"], writes=[r1], cost=c_act(512, 1))
                        P.add("act", lambda e, t1=t1, t3=t3: e.activation(out=t3, in_=t1, func=AF.Ln, bias=onec[:, 0:1]),
                              reads=[r1, "onec"], writes=[r3], cost=c_act(512, 1), tset="ln")
                    P.add("dve", lambda e, t3=t3, t4=t4: e.tensor_tensor_scan(out=t4, data0=mask01, data1=t3, initial=0.0,
                                                                              op0=ALU.mult, op1=ALU.add),
                          reads=[r3, "mask01"], writes=[r4], cost=c_dve(512, scan=True))
                    sE, sN = (1.0, -1.0) if isA else (-1.0 / 16, 1.0 / 16)
                    P.add("act", lambda e, t4=t4, Ebt=Ebt, sE=sE: e.activation(out=Ebt, in_=t4, func=AF.Exp, scale=sE),
                          reads=[r4], writes=[rE], cost=c_act(512))
                    if isA:
                        P.add("act", lambda e, t4=t4, Enb=Enb, blk=blk: e.activation(out=Enb, in_=t4, func=AF.Exp,
                                                                                     scale=-1.0, bias=lnc1[:, blk:blk + 1]),
                              reads=[r4, "lnc1"], writes=[rN], cost=c_act(512, 1))
                    else:
                        P.add("act", lambda e, t4=t4, Enb=Enb, sN=sN: e.activation(out=Enb, in_=t4, func=AF.Exp, scale=sN),
                              reads=[r4], writes=[rN], cost=c_act(512))
                    P.add("dve", lambda e, Ebt=Ebt, ec=ec, blk=blk: e.tensor_copy(
                        out=ec[:, blk, :], in_=Ebt.rearrange("p (c j) -> p c j", j=64)[:, :, 63]),
                        reads=[rE], writes=[ecres], cost=0.2)
                    if isA:
                        P.add("dve", lambda e, blk=blk, t1=t1, Ebt=Ebt, qT=qT: e.scalar_tensor_tensor(
                            out=qT[:, blk, :], in0=t1, scalar=-(dk ** -0.5), in1=Ebt, op0=ALU.mult, op1=ALU.mult),
                            reads=[r1, rE], writes=[qres], cost=c_dve(512))
                        P.add("dve", lambda e, blk=blk, t2=t2, Enb=Enb, kT=kT: e.scalar_tensor_tensor(
                            out=kT[:, blk, :], in0=t2, scalar=-1.0, in1=Enb, op0=ALU.add, op1=ALU.mult),
                            reads=[r2, rN, r3], writes=[kres], cost=c_dve(512))
                    else:
                        for hh in range(2):
                            pq = slice(hh * 64, (hh + 1) * 64)
                            P.add("dve", lambda e, blk=blk, hh=hh, pq=pq, Ebt=Ebt, qT=qT: e.scalar_tensor_tensor(
                                out=qT[pq, 2 * blk + hh, :], in0=bankf(0)[pq, :], scalar=dk ** -0.5,
                                in1=Ebt[pq, :], op0=ALU.mult, op1=ALU.mult),
                                reads=["bank0", rE], writes=[qres], cost=c_dve(512))
                        P.add("dve", lambda e, blk=blk, Enb=Enb, kT=kT: e.tensor_tensor(out=kT[:, blk, :], in0=bankf(1),
                                                                                       in1=Enb, op=ALU.mult),
                              reads=["bank1", rN], writes=[kres], cost=c_dve(512))

            def tile(t):
                s, tt = divmod(t, 4)
                ss = s % 2
                tp_ = t % NBT
                qT, kT, ec = qTs[ss], kTs[ss], ecs[s % 3]
                qres, kres, ecres = "qT%d" % ss, "kT%d" % ss, "ec%d" % (s % 3)
                vbf, tg, G, kt, smk, onb, st4 = vbfs[tp_], tgs[tp_], Gs[tp_], kts[tp_], smks[tp_], onbs[tp_], st4s[tp_]
                rv, rtg, rG, rkt, rsm, ron, rst = ["%s%d" % (n_, tp_) for n_ in ("vbf", "tg", "G", "kt", "smk", "onb", "st4")]
                hres = "hT%d" % s
                tl = slice(tt * 128, (tt + 1) * 128)
                tg_ = slice(t * 128, (t + 1) * 128)
                mm_acc(bankf(2), [(hT[:, k, tg_], W[:, k, vc0:vc0 + 512]) for k in range(8)],
                       reads=wv + [hres], writes=["bank2"])
                P.add("dve", lambda e: e.tensor_copy(out=vbf, in_=bankf(2)),
                      reads=["bank2"], writes=[rv], cost=c_dve(512))
                mm_acc(bankf(2), [(hT[:, k, tg_], W[:, k, gc0:gc0 + 512]) for k in range(8)],
                       reads=wg_ + [hres], writes=["bank2"])
                P.add("act", lambda e: e.activation(out=G, in_=bankf(2), func=AF.Silu),
                      reads=["bank2"], writes=[rG], cost=c_act(512), tset="tanh")
                tpk = bankb(3)[:, 0:nblk * 128].rearrange("p (b c) -> p b c", b=nblk)

                def trk(e):
                    for blk in range(nblk):
                        ins = e.transpose(out=tpk[:, blk, :], in_=kT[:, blk, tl], identity=ident)
                    return ins
                P.add("pe", trk, reads=[kres, "ident"], writes=["bank3"], cost=nblk * 0.08)
                if isA:
                    P.add("act", lambda e: e.activation(out=kt, in_=tpk, func=AF.Copy),
                          reads=["bank3"], writes=[rkt], cost=c_act(512))
                else:
                    ktv = kt.rearrange("p (b h) c -> p b h c", h=2)
                    for hh in range(2):
                        cs_ = slice(hh * 64, (hh + 1) * 64)
                        P.add("act", lambda e, hh=hh, cs_=cs_: e.activation(
                            out=ktv[:, :, hh, cs_], in_=tpk[:, :, cs_], func=AF.Copy),
                            reads=["bank3"], writes=[rkt], cost=c_act(128))
                scp = bankf(4).rearrange("p (h c) -> p h c", h=4)

                def scf(e):
                    for h in range(4):
                        blk = h if isA else h // 2
                        ins = e.matmul(scp[:, h, :], lhsT=kT[:, blk, tl], rhs=qT[:, h, tl], start=True, stop=True)
                    return ins
                P.add("pe", scf, reads=[kres, qres], writes=["bank4"], cost=c_pe([128] * 4))
                P.add("dve", lambda e: e.tensor_tensor(out=smk, in0=scp,
                                                       in1=maskbd.unsqueeze(1).to_broadcast([128, 4, 128]),
                                                       op=ALU.mult),
                      reads=["bank4", "maskbd"], writes=[rsm], cost=c_dve(512))
                op_ = bankf(5).rearrange("p (h c) -> p h c", h=4)

                def oin(e):
                    for h in range(4):
                        ins = e.matmul(op_[:, h, :], lhsT=smk[:, h, :], rhs=vbf[:, h * 128:(h + 1) * 128],
                                       start=(h == 0), stop=False, skip_group_check=True)
                    return ins
                P.add("pe", oin, reads=[rsm, rv], writes=["bank5"], cost=c_pe([128] * 4))
                pp = bankf(6)[:, 0:nblk * 128].rearrange("p (b c) -> p b c", b=nblk)
                for c in range(2):
                    cg = 2 * t + c
                    cl8 = 2 * tt + c
                    cl = slice(tt * 128 + c * 64, tt * 128 + (c + 1) * 64)
                    pr = slice(c * 64, (c + 1) * 64)
                    Sb_cur, Sb_nxt = Sbs[cg % 2], Sbs[(cg + 1) % 2]
                    rSc, rSn = "Sb%d" % (cg % 2), "Sb%d" % ((cg + 1) % 2)

                    def ointer(e, cl=cl, pr=pr, c=c, Sb_cur=Sb_cur):
                        for h in range(4):
                            blk = h if isA else h // 2
                            ins = e.matmul(op_[pr, h, :], lhsT=qT[:, h, cl], rhs=Sb_cur[:, blk, :],
                                           start=False, stop=(c == 1 and h == 3), skip_group_check=True)
                        return ins
                    P.add("pe", ointer, reads=[qres, rSc, "bank5"], writes=["bank5"], cost=c_pe([128] * 4))

                    def pst(e, pr=pr):
                        for h in range(4):
                            blk = h if isA else h // 2
                            first = True if isA else (h % 2 == 0)
                            last = True if isA else (h % 2 == 1)
                            ins = e.matmul(pp[:, blk, :], lhsT=kt[pr, h, :],
                                           rhs=vbf[pr, h * 128:(h + 1) * 128], start=first, stop=last)
                        return ins
                    P.add("pe", pst, reads=[rkt, rv], writes=["bank6"], cost=c_pe([128] * 4))
                    if cl8 == 0:
                        ecp, ecpres, pcol = ecs[(s - 1) % 3], "ec%d" % ((s - 1) % 3), 7
                    else:
                        ecp, ecpres, pcol = ec, ecres, cl8 - 1
                    T_prev, T_cur = Tts[(cg + 1) % 2], Tts[cg % 2]
                    rTp, rTc = "Tt%d" % ((cg + 1) % 2), "Tt%d" % (cg % 2)
                    for blk in range(nblk):
                        P.add("dve", lambda e, blk=blk, ecp=ecp, pcol=pcol, T_prev=T_prev, T_cur=T_cur:
                              e.scalar_tensor_tensor(out=T_cur[:, blk, :], in0=T_prev[:, blk, :],
                                                     scalar=ecp[:, blk, pcol:pcol + 1], in1=pp[:, blk, :],
                                                     op0=ALU.mult, op1=ALU.add),
                              reads=[rTp + "_%d" % blk, ecpres, "bank6"], writes=[rTc + "_%d" % blk],
                              cost=c_dve(128) + 0.07)
                    ebc = ec[:, 0:nblk, cl8:cl8 + 1].to_broadcast([128, nblk, 128])
                    P.add("pool", lambda e, ebc=ebc, Sb_nxt=Sb_nxt, T_cur=T_cur: e.tensor_tensor(
                        out=Sb_nxt[:, 0:nblk, :], in0=T_cur[:, 0:nblk, :], in1=ebc, op=ALU.mult),
                        reads=[rTc + "_%d" % b for b in range(nblk)] + [ecres], writes=[rSn],
                        cost=c_pool(128 * nblk))
                for h in range(4):
                    P.add("act", lambda e, h=h: e.activation(out=junk4[:, h, :], in_=op_[:, h, :], func=AF.Square,
                                                             accum_out=st4[:, h:h + 1]),
                          reads=["bank5"], writes=["junk4_%d" % h, rst + "a%d" % h], cost=c_act(128, acc=True))
                P.add("dve", lambda e: e.tensor_scalar(out=st4[:, 4:8], in0=st4[:, 0:4], scalar1=1.0 / 128,
                                                       scalar2=EPS, op0=ALU.mult, op1=ALU.add),
                      reads=[rst + "a%d" % h for h in range(4)], writes=[rst + "b"], cost=0.2)
                P.add("pool", lambda e: e.tensor_tensor(out=st4[:, 8:12], in0=st4[:, 4:8], in1=nhalf[:, 0:4],
                                                        op=ALU.pow),
                      reads=[rst + "b", "nhalf"], writes=[rst + "c"], cost=1.0)
                P.add("dve", lambda e: e.tensor_tensor(out=on1, in0=op_,
                                                       in1=st4[:, 8:12].unsqueeze(2).to_broadcast([128, 4, 128]),
                                                       op=ALU.mult),
                      reads=["bank5", rst + "c"], writes=["on1"], cost=c_dve(512))
                P.add("pool", lambda e: e.tensor_tensor(out=onb, in0=on1.rearrange("p h c -> p (h c)"), in1=G,
                                                        op=ALU.mult),
                      reads=["on1", rG], writes=[ron], cost=c_pool(512))
                tpo = bankb(7)[:, 512:1024].rearrange("p (b c) -> p b c", b=4)

                def tro(e):
                    for h in range(4):
                        ins = e.transpose(out=tpo[:, h, :], in_=onb[:, h * 128:(h + 1) * 128], identity=ident)
                    return ins
                P.add("pe", tro, reads=[ron, "ident"], writes=["bank7"], cost=4 * 0.08)
                hn2 = hna2 if isA else hnb2
                for h in range(4):
                    P.add("act", lambda e, h=h: e.activation(out=oT[:, h, tg_], in_=tpo[:, h, :], func=AF.Copy,
                                                             scale=hn2[:, h:h + 1]),
                          reads=["bank7", "hna2", "hnb2"], writes=[ores + "_%d" % h], cost=c_act(128, 1))

            prep(0)
            for s in range(NS):
                if s == 1 and prefetch is not None:
                    prefetch()
                if s + 1 < NS:
                    prep(s + 1)
                P.mark(br + "_tiles_s%d" % s)
                for tt in range(4):
                    tile(4 * s + tt)

        def prefetch_WB():
            grp = [(0, 512), (512, 512), (1024, 512), (1536, 16)]
            P.add("pool", lambda e: e.memset(dummy2, 0.0), writes=ph0_res + ["ph0_done"], cost=0.3)
            for i, (c0_, w) in enumerate(grp):
                dma(bufY[:, :, c0_:c0_ + w], win_v[:, :, 2048 + c0_:2048 + c0_ + w], reads=["ph0_done"],
                    writes=["bufY_%d" % i], q="pool", nbytes=8 * 2 * w)

        def prefetch_WZ():
            for i in range(4):
                dma(bufX[:, :, i * 512:(i + 1) * 512], win_v[:, :, 3600 + i * 512:3600 + (i + 1) * 512],
                    writes=["bufX_%d" % i], q="pool", nbytes=8 * 1024)

        gla_branch("A", prefetch_WB)
        P.mark("A_done")
        gla_branch("B", prefetch_WZ)
        P.mark("B_done")
        P.barrier()

        if debug:
            ar.reset(work0)
            dbf = ar.alloc([4 * T], F32)
            P.add("dve", lambda e: e.tensor_copy(out=dbf, in_=oTa.rearrange("p a b -> p (a b)")), reads=["oTa_%d" % h for h in range(4)],
                  writes=["dbf"])
            dma(dbg_a, dbf, "dbg", reads=["dbf"], writes=["dbg_a"])
            P.add("dve", lambda e: e.tensor_copy(out=dbf, in_=oTb.rearrange("p a b -> p (a b)")),
                  reads=["oTb_%d" % h for h in range(4)] + ["dbg_a"], writes=["dbf"])
            dma(dbg_b, dbf, "dbg", reads=["dbf"], writes=["dbg_b"])
            P.barrier()

        ar.reset(work0)
        by = ar.sub(bufY_rng)
        Wab = by.alloc([8, D], BF16)
        Wple = by.alloc([2, D], BF16)
        gpost_bc = by.alloc([D], F32)
        ypT = ar.alloc([8, T], BF16)
        Wout = ar.alloc([8, D], BF16)
        gple_bc = ar.alloc([D], F32)
        tas = [ar.alloc([512], F32) for _ in range(2)]
        tbs = [ar.alloc([512], F32) for _ in range(2)]
        Wg = ar.alloc([8, D], BF16)

        dma(Wab[:, 0:4, :], wa_v, writes=["Wab_a"], q="pool", nbytes=8192)
        dma(Wab[:, 4:8, :], wb_v, writes=["Wab_b"], q="pool", nbytes=8192)

        def prefetch_C2():
            for i in range(2):
                dma(Wout[:, :, i * 512:(i + 1) * 512], wout_v[:, :, i * 512:(i + 1) * 512], writes=["Wout%d" % i],
                    q="pool", nbytes=8 * 1024)
            dma(Wple, wple_v, writes=["Wple"], q="pool", nbytes=4096)
            for i in range(2):
                dma(Wg[:, :, i * 512:(i + 1) * 512], wg_v[:, :, i * 512:(i + 1) * 512], writes=["Wg%d" % i],
                    q="pool", nbytes=8 * 1024)
            dma(gpost_bc, gpost_d.partition_broadcast(128).rearrange("p a b -> p (a b)"), writes=["gpost"], nbytes=4096)
            dma(gple_bc, gple_d.partition_broadcast(128).rearrange("p a b -> p (a b)"), writes=["gple"], nbytes=4096)

        for s in range(NS):
            tok = slice(s * 512, (s + 1) * 512)
            hres = "hT%d" % s
            if s == 1:
                prefetch_C2()
            for fb in range(8):
                par = fb % 2
                b0 = 4 * par
                ta, tb = tas[fb % 2], tbs[fb % 2]
                rta, rtb = "ta%d" % (fb % 2), "tb%d" % (fb % 2)
                fcol = slice(fb * 128, (fb + 1) * 128)
                mm_acc(bankf(b0), [(bufX[:, k, fcol], hT[:, k, tok]) for k in range(8)],
                       reads=["bufX_%d" % (fb // 4), hres], writes=["bank%d" % b0])
                mm_acc(bankf(b0 + 1), [(bufX[:, k, 1024 + fb * 128:1024 + (fb + 1) * 128], hT[:, k, tok])
                                       for k in range(8)],
                       reads=["bufX_%d" % (2 + fb // 4), hres], writes=["bank%d" % (b0 + 1)])
                mm_acc(bankf(b0 + 2), [(Wab[:, k, fcol], oTa[:, k, tok]) for k in range(4)],
                       reads=["Wab_a"] + ["oTa_%d" % h for h in range(4)], writes=["bank%d" % (b0 + 2)])
                mm_acc(bankf(b0 + 3), [(Wab[:, 4 + k, fcol], oTb[:, k, tok]) for k in range(4)],
                       reads=["Wab_b"] + ["oTb_%d" % h for h in range(4)], writes=["bank%d" % (b0 + 3)])
                P.add("act", lambda e, ta=ta, b0=b0: e.activation(out=ta, in_=bankf(b0), func=AF.Sigmoid),
                      reads=["bank%d" % b0], writes=[rta], cost=c_act(512), tset="sig")
                P.add("act", lambda e, tb=tb, b0=b0: e.activation(out=tb, in_=bankf(b0 + 1), func=AF.Sigmoid),
                      reads=["bank%d" % (b0 + 1)], writes=[rtb], cost=c_act(512), tset="sig")
                P.add("dve", lambda e, ta=ta, b0=b0: e.tensor_tensor(out=ta, in0=ta, in1=bankf(b0 + 2), op=ALU.mult),
                      reads=[rta, "bank%d" % (b0 + 2)], writes=[rta], cost=c_dve(512))
                P.add("dve", lambda e, tb=tb, b0=b0: e.tensor_tensor(out=tb, in0=tb, in1=bankf(b0 + 3), op=ALU.mult),
                      reads=[rtb, "bank%d" % (b0 + 3)], writes=[rtb], cost=c_dve(512))
                P.add("pool", lambda e, fb=fb, tok=tok, ta=ta, tb=tb: e.tensor_tensor(out=ypT[:, fb, tok], in0=ta,
                                                                                      in1=tb, op=ALU.add),
                      reads=[rta, rtb], writes=["ypT%d" % s], cost=c_pool(512))
        P.mark("C1_done")
        P.barrier()

        ah = ar.sub(hT_rng)
        ax = ar.sub(bufX_rng)
        ao = ar.sub(oTa_rng)
        ab = ar.sub(oTb_rng)
        NBC = 3
        xsC = [ah.alloc([D], F32) for _ in range(NBC)]
        yo = [ah.alloc([D], F32) for _ in range(NBC)]
        tgC = [(ah if i < 2 else ab).alloc([D], F32) for i in range(NBC)]
        en = [ab.alloc([D], F32) for _ in range(NBC)]
        x1b = [ax.alloc([D], BF16) for _ in range(NBC)]
        x1T = [ax.alloc([8, 128], BF16) for _ in range(NBC)]
        psb = [ao.alloc([256], F32) for _ in range(NBC)]
        pb = [ao.alloc([256], BF16) for _ in range(NBC)]
        pT = [ao.alloc([2, 128], BF16) for _ in range(NBC)]
        junkCs = [ao.alloc([512], BF16) for _ in range(4)]
        stC = [ao.alloc([16], F32) for _ in range(NBC)]
        for t in range(NT):
            sl = t % NBC
            tl = slice(t * 128, (t + 1) * 128)
            xr, pr_, yr = "xsC%d" % sl, "psb%d" % sl, "yo%d" % sl
            sC = stC[sl]
            rs = "stC%d" % sl
            dma(xsC[sl], x_d[tl, :], xr, writes=[xr], nbytes=4096)
            dma(psb[sl], p_d[tl, :], pr_, writes=[pr_], nbytes=1024)
            for hf in range(2):
                mm_acc(bankf(4 + hf), [(ypT[:, k, tl], Wout[:, k, hf * 512:(hf + 1) * 512]) for k in range(8)],
                       reads=["ypT%d" % (t // 4), "Wout%d" % hf], writes=["bank%d" % (4 + hf)])
                P.add("act", lambda e, hf=hf, sC=sC: e.activation(out=junkCs[hf], in_=bankf(4 + hf), func=AF.Square,
                                                                  accum_out=sC[:, hf:hf + 1]),
                      reads=["bank%d" % (4 + hf)], writes=["junkC%d" % hf, rs + "a%d" % hf], cost=c_act(512, acc=True))
            P.add("dve", lambda e, sC=sC: e.tensor_tensor(out=sC[:, 2:3], in0=sC[:, 0:1], in1=sC[:, 1:2], op=ALU.add),
                  reads=[rs + "a0", rs + "a1"], writes=[rs + "b"], cost=0.2)
            P.add("dve", lambda e, sC=sC: e.tensor_scalar(out=sC[:, 3:4], in0=sC[:, 2:3], scalar1=1.0 / D, scalar2=EPS,
                                                          op0=ALU.mult, op1=ALU.add),
                  reads=[rs + "b"], writes=[rs + "c"], cost=0.2)
            P.add("pool", lambda e, sC=sC: e.tensor_tensor(out=sC[:, 4:5], in0=sC[:, 3:4], in1=nhalf[:, 0:1], op=ALU.pow),
                  reads=[rs + "c", "nhalf"], writes=[rs + "d"], cost=0.8)
            for hf in range(2):
                hs = slice(hf * 512, (hf + 1) * 512)
                P.add("dve", lambda e, hf=hf, hs=hs, sl=sl, sC=sC: e.scalar_tensor_tensor(
                    out=yo[sl][:, hs], in0=bankf(4 + hf), scalar=sC[:, 4:5], in1=gpost_bc[:, hs],
                    op0=ALU.mult, op1=ALU.mult),
                    reads=["bank%d" % (4 + hf), rs + "d", "gpost"], writes=[yr + "h%d" % hf], cost=c_dve(512))
            P.add("pool", lambda e, sl=sl: e.tensor_tensor(out=xsC[sl], in0=xsC[sl], in1=yo[sl], op=ALU.add),
                  reads=[xr, yr + "h0", yr + "h1"], writes=[xr], cost=c_pool(D))
            P.add("act", lambda e, sl=sl: e.activation(out=x1b[sl], in_=xsC[sl], func=AF.Copy),
                  reads=[xr], writes=["x1b%d" % sl], cost=c_act(D))
            tpx = bankb(7).rearrange("p (k c) -> p k c", k=8)

            def trx(e, tpx=tpx, sl=sl):
                for k in range(8):
                    ins = e.transpose(out=tpx[:, k, :], in_=x1b[sl][:, k * 128:(k + 1) * 128], identity=ident)
                return ins
            P.add("pe", trx, reads=["x1b%d" % sl, "ident"], writes=["bank7"], cost=8 * 0.08)
            P.add("act", lambda e, tpx=tpx, sl=sl: e.activation(out=x1T[sl], in_=tpx, func=AF.Copy),
                  reads=["bank7"], writes=["x1T%d" % sl], cost=c_act(D))
            P.add("pool", lambda e, sl=sl: e.tensor_copy(out=pb[sl], in_=psb[sl]), reads=[pr_], writes=["pb%d" % sl],
                  cost=c_pool(256))
            tpp = bankb(6)[:, 0:256].rearrange("p (k c) -> p k c", k=2)

            def trp(e, tpp=tpp, sl=sl):
                for k in range(2):
                    ins = e.transpose(out=tpp[:, k, :], in_=pb[sl][:, k * 128:(k + 1) * 128], identity=ident)
                return ins
            P.add("pe", trp, reads=["pb%d" % sl, "ident"], writes=["bank6"], cost=2 * 0.08)
            P.add("act", lambda e, tpp=tpp, sl=sl: e.activation(out=pT[sl], in_=tpp, func=AF.Copy),
                  reads=["bank6"], writes=["pT%d" % sl], cost=c_act(256))
            for hf in range(2):
                bk = 2 + hf
                mm_acc(bankf(bk), [(pT[sl][:, k, :], Wple[:, k, hf * 512:(hf + 1) * 512]) for k in range(2)],
                       reads=["pT%d" % sl, "Wple"], writes=["bank%d" % bk])
                P.add("act", lambda e, hf=hf, bk=bk, sC=sC: e.activation(out=junkCs[2 + hf], in_=bankf(bk), func=AF.Square,
                                                                         accum_out=sC[:, 8 + hf:9 + hf]),
                      reads=["bank%d" % bk], writes=["junkC%d" % (2 + hf), rs + "e%d" % hf], cost=c_act(512, acc=True))
            P.add("dve", lambda e, sC=sC: e.tensor_tensor(out=sC[:, 10:11], in0=sC[:, 8:9], in1=sC[:, 9:10], op=ALU.add),
                  reads=[rs + "e0", rs + "e1"], writes=[rs + "f"], cost=0.2)
            P.add("dve", lambda e, sC=sC: e.tensor_scalar(out=sC[:, 11:12], in0=sC[:, 10:11], scalar1=1.0 / D,
                                                          scalar2=EPS, op0=ALU.mult, op1=ALU.add),
                  reads=[rs + "f"], writes=[rs + "g"], cost=0.2)
            P.add("pool", lambda e, sC=sC: e.tensor_tensor(out=sC[:, 12:13], in0=sC[:, 11:12], in1=nhalf[:, 0:1],
                                                           op=ALU.pow),
                  reads=[rs + "g", "nhalf"], writes=[rs + "h"], cost=0.8)
            for hf in range(2):
                hs = slice(hf * 512, (hf + 1) * 512)
                bk = 2 + hf
                P.add("dve", lambda e, hs=hs, bk=bk, sl=sl, sC=sC: e.scalar_tensor_tensor(
                    out=en[sl][:, hs], in0=bankf(bk), scalar=sC[:, 12:13], in1=gple_bc[:, hs],
                    op0=ALU.mult, op1=ALU.mult),
                    reads=["bank%d" % bk, rs + "h", "gple"], writes=["en%d_%d" % (sl, hf)], cost=c_dve(512))
            for hf in range(2):
                hs = slice(hf * 512, (hf + 1) * 512)
                mm_acc(bankf(hf), [(x1T[sl][:, k, :], Wg[:, k, hs]) for k in range(8)],
                       reads=["x1T%d" % sl, "Wg%d" % hf], writes=["bank%d" % hf])
                P.add("act", lambda e, hf=hf, hs=hs, sl=sl: e.activation(out=tgC[sl][:, hs], in_=bankf(hf),
                                                                         func=AF.Sigmoid),
                      reads=["bank%d" % hf], writes=["tgC%d_%d" % (sl, hf)], cost=c_act(512), tset="sig")
                P.add("dve", lambda e, hs=hs, sl=sl: e.tensor_tensor(out=en[sl][:, hs], in0=tgC[sl][:, hs],
                                                                     in1=en[sl][:, hs], op=ALU.mult),
                      reads=["tgC%d_%d" % (sl, hf), "en%d_%d" % (sl, hf)], writes=["en%d_%d" % (sl, hf)],
                      cost=c_dve(512))
                P.add("dve", lambda e, hs=hs, sl=sl: e.tensor_tensor(out=yo[sl][:, hs], in0=en[sl][:, hs],
                                                                     in1=xsC[sl][:, hs], op=ALU.add),
                      reads=["en%d_%d" % (sl, hf), xr, yr + "h%d" % hf], writes=[yr + "h%d" % hf],
                      cost=c_dve(512))
            dma(out_d[tl, :], yo[sl], "out%d" % sl, reads=[yr + "h0", yr + "h1"],
                writes=[yr + "h0", yr + "h1", "OUT%d" % sl], nbytes=4096)
        P.barrier()
        P.add("sp", None, reads=["OUT0", "OUT1", "OUT2"] + (["dbg_a", "dbg_b"] if debug else []), force=True)
        build_nc.n_ops = len(P.ops)
        build_nc.marks = P.marks
        P.emit(nc, st)
        build_nc.est_time = getattr(P, "est_time", None)
        fin = getattr(P, "finish", None)
        if fin:
            build_nc.mark_times = [(nm, max(fin[:i]) if i else 0.0) for nm, i in P.marks]
            build_nc.busy = P.busy
            build_nc.P = P
    return nc


def _consts():
    ident = np.eye(128, dtype=np.float32)
    j = np.arange(128)[:, None]
    i = np.arange(128)[None, :]
    maskbd = ((j <= i) & ((j // 64) == (i // 64))).astype(np.float32)
    m01 = np.ones((128, 512), np.float32)
    m01[:, ::64] = 0.0
    return ident, maskbd, m01


def make_in_maps(x, p, w_in, lb_logits, w_gla_up, b_gla, norm_pre, norm_post, head_norm_a, head_norm_b,
                 w_branch_a, w_branch_b, w_out, w_ple, w_ple_gate, norm_ple):
    f = lambda a: np.ascontiguousarray(np.asarray(a, dtype=np.float32))
    ident, maskbd, m01 = _consts()
    lbl = f(np.asarray(lb_logits).reshape(2, 4, 128).transpose(2, 0, 1).reshape(128, 8))
    shared = {
        "w_in": f(np.asarray(w_in)[0]),
        "w_a": f(np.asarray(w_branch_a)[0]),
        "w_b": f(np.asarray(w_branch_b)[0]),
        "w_out": f(np.asarray(w_out)[0]),
        "w_ple": f(np.asarray(w_ple)[0]),
        "w_g": f(np.asarray(w_ple_gate)[0]),
        "w_up": f(np.asarray(w_gla_up)[0]),
        "lbl": lbl,
        "gpre": f(np.asarray(norm_pre)[0].reshape(1, D)),
        "hna": f(np.asarray(head_norm_a)[0].reshape(4, 128).T),
        "hnb": f(np.asarray(head_norm_b)[0].reshape(4, 128).T),
        "bgl": f(np.asarray(b_gla)[0].reshape(2, 128).T),
        "gpost": f(np.asarray(norm_post)[0].reshape(1, D)),
        "gple": f(np.asarray(norm_ple)[0].reshape(1, D)),
        "ident": ident,
        "maskbd": maskbd,
        "mask01": m01,
    }
    xs = np.asarray(x)
    ps = np.asarray(p)
    maps = []
    for b in range(8):
        m = dict(shared)
        m["x"] = f(xs[b])
        m["p"] = f(ps[0, b])
        maps.append(m)
    return maps


_NC_CACHE = {}


def kernel(x, p, w_in, lb_logits, w_gla_up, b_gla, norm_pre, norm_post, head_norm_a, head_norm_b,
           w_branch_a, w_branch_b, w_out, w_ple, w_ple_gate, norm_ple):
    in_maps = make_in_maps(x, p, w_in, lb_logits, w_gla_up, b_gla, norm_pre, norm_post, head_norm_a, head_norm_b,
                           w_branch_a, w_branch_b, w_out, w_ple, w_ple_gate, norm_ple)
    if "nc" not in _NC_CACHE:
        _NC_CACHE["nc"] = build_nc(False)
    nc = _NC_CACHE["nc"]
    res = run_bass_kernel_spmd(nc, in_maps, core_ids=list(range(8)))
    out = np.stack([np.asarray(r["out"]) for r in res.results], axis=0).astype(np.float32)
    return out
```

```python
import contextlib
import sys
import numpy as np
import concourse.bass as bass
import concourse.mybir as mybir
from concourse.alu_op_type import AluOpType as ALU
from concourse.bass_utils import run_bass_kernel_spmd

F32 = mybir.dt.float32
BF16 = mybir.dt.bfloat16
AF = mybir.ActivationFunctionType
AX = mybir.AxisListType

ENGS = ("sp", "act", "pool", "dve", "pe")

T = 2048
D = 1024
NT = 16
NS = 4
DIN = 5648
EPS = 1e-6
TSWITCH = 1.3


class Op:
    __slots__ = ("idx", "eng", "fn", "reads", "writes", "raw", "oth", "need_inc",
                 "dma_key", "dma_cnt", "inc_cnt", "cost", "lat", "tset", "deps", "prio", "line", "start", "bind", "dbytes")


def c_pe(cols):
    return sum(n / 2400.0 + 0.008 for n in cols)


def c_act(n, naps=0, acc=False):
    return 0.1 + n / 1100.0 + 0.06 * naps + (0.09 if acc else 0.0)


def c_dve(n, scan=False):
    return 0.15 + (2 * n if scan else n) / 960.0


def c_pool(n):
    return 0.3 + n / 520.0


class Prog:
    def __init__(self):
        self.ops = []
        self.last_writer = {}
        self.readers = {}
        self.dma_counts = {}
        self.last_dma = {}
        self.last_compute = {}
        self.pending = {e: set() for e in ENGS}
        self.max_ops = None
        self.marks = []
        self.do_schedule = True
        self.has_succ = set()
        self.bar_start = 0
        self.join = None
        self.join_fn = None

    def mark(self, name):
        self.marks.append((name, len(self.ops)))

    def add(self, eng, fn, reads=(), writes=(), dma_key=None, force=False, cost=0.3, lat=None, tset=None):
        if self.max_ops is not None and len(self.ops) >= self.max_ops and not force:
            return None
        op = Op()
        op.idx = len(self.ops)
        op.line = sys._getframe(1).f_lineno if sys._getframe(1).f_code.co_name not in ("dma", "mm_acc") \
            else sys._getframe(2).f_lineno
        op.eng = eng
        op.fn = fn
        op.reads = tuple(reads)
        op.writes = tuple(writes)
        op.need_inc = False
        op.dma_key = dma_key
        op.dma_cnt = 0
        op.inc_cnt = 0
        op.cost = cost
        op.lat = cost if lat is None else lat
        op.tset = tset
        op.dbytes = 0
        raw, oth = set(), set()
        for r in op.reads:
            if r in self.last_writer:
                raw.add(self.last_writer[r])
        for w in op.writes:
            if w in self.last_writer:
                oth.add(self.last_writer[w])
            for rd in self.readers.get(w, ()):
                oth.add(rd)
        for w in op.writes:
            self.last_writer[w] = op.idx
            self.readers[w] = []
        for r in op.reads:
            if r not in op.writes:
                self.readers.setdefault(r, []).append(op.idx)
        if self.pending[eng]:
            raw |= self.pending[eng]
            self.pending[eng] = set()
        if self.join is not None:
            raw.add(self.join)
        raw.discard(op.idx)
        oth.discard(op.idx)
        op.raw = raw
        op.oth = oth - raw
        self.has_succ |= raw
        self.has_succ |= oth
        if dma_key is not None:
            self.dma_counts[dma_key] = self.dma_counts.get(dma_key, 0) + 1
            op.dma_cnt = self.dma_counts[dma_key]
            self.last_dma[dma_key] = op.idx
        elif fn is not None:
            self.last_compute[eng] = op.idx
        self.ops.append(op)
        return op

    def barrier(self):
        sinks = set(i for i in range(self.bar_start, len(self.ops)) if i not in self.has_succ)
        self.pending["pool"] |= sinks
        j = self.add("pool", self.join_fn, cost=0.3, force=True)
        self.join = j.idx
        self.bar_start = j.idx

    def _sync_deps(self, op):
        out = []
        for d in sorted(op.raw | op.oth):
            dop = self.ops[d]
            if dop.dma_key is None and dop.eng == op.eng and op.dma_key is None and op.eng == "pe":
                continue
            out.append(dop)
        return out

    def schedule(self):
        ops = self.ops
        n = len(ops)
        succ = [[] for _ in range(n)]
        ndeps = [0] * n
        for op in ops:
            op.deps = sorted(op.raw | op.oth)
            ndeps[op.idx] = len(op.deps)
            for d in op.deps:
                succ[d].append(op.idx)
        for op in reversed(ops):
            best = 0.0
            for s in succ[op.idx]:
                best = max(best, ops[s].prio)
            op.prio = best + op.lat
        order = {e: [] for e in ENGS}
        if not self.do_schedule:
            for op in ops:
                order[op.eng].append(op)
            return order
        finish = [0.0] * n
        free = {e: 0.0 for e in ENGS}
        dma_busy = [0.0]
        cur_tset = [None]
        ready = {e: [] for e in ENGS}
        for op in ops:
            if ndeps[op.idx] == 0:
                ready[op.eng].append(op)
        left = n
        while left:
            best = None
            for e in ENGS:
                for op in ready[e]:
                    r = 0.0
                    rb = None
                    for d in op.deps:
                        f = finish[d] + (0.0 if (ops[d].eng == e and ops[d].dma_key is None) else 0.15)
                        if f > r:
                            r = f
                            rb = d
                    s = max(free[e], r)
                    op.bind = rb if r >= free[e] else -1
                    if e == "act" and op.tset is not None and cur_tset[0] is not None and cur_tset[0] != op.tset:
                        s += TSWITCH
                    key = (s, -op.prio, op.idx)
                    if best is None or key < best[0]:
                        best = (key, op)
            (s, _, _), op = best
            e = op.eng
            ready[e].remove(op)
            if e == "act" and op.tset is not None:
                cur_tset[0] = op.tset
            if op.bind == -1:
                op.bind = ("eng", order[e][-1].idx if order[e] else None)
            else:
                op.bind = ("dep", op.bind)
            op.start = s
            free[e] = s + op.cost
            if op.dma_key is not None and op.dbytes:
                t0 = max(s + 1.5, dma_busy[0])
                dma_busy[0] = t0 + op.dbytes / 260e3
                finish[op.idx] = dma_busy[0] + 0.5
            else:
                finish[op.idx] = s + op.lat
            order[e].append(op)
            left -= 1
            for sidx in succ[op.idx]:
                ndeps[sidx] -= 1
                if ndeps[sidx] == 0:
                    ready[ops[sidx].eng].append(ops[sidx])
        self.est_time = max(finish) if n else 0.0
        self.finish = finish
        self.busy = {e: sum(op.cost for op in order[e]) for e in ENGS}
        return order

    def emit(self, nc, stack):
        order = self.schedule()
        for op in self.ops:
            for d in self._sync_deps(op):
                if d.dma_key is None:
                    d.need_inc = True
        eng_sem = {e: stack.enter_context(nc.semaphore("s_" + e)) for e in ENGS}
        key_sem = {k: stack.enter_context(nc.semaphore("d_" + k)) for k in self.dma_counts}
        for e in ENGS:
            c = 0
            for op in order[e]:
                if op.dma_key is None and op.need_inc:
                    c += 1
                    op.inc_cnt = c
        block = stack.enter_context(nc.Block())
        prog = self

        def run(engname, eng):
            waited = {}
            for op in order[engname]:
                for d in prog._sync_deps(op):
                    if d.dma_key is not None:
                        sem, val, k = key_sem[d.dma_key], 16 * d.dma_cnt, "d_" + d.dma_key
                    else:
                        sem, val, k = eng_sem[d.eng], d.inc_cnt, "s_" + d.eng
                    if waited.get(k, 0) >= val:
                        continue
                    waited[k] = val
                    eng.wait_ge(sem, val)
                if op.fn is None:
                    continue
                ins = op.fn(eng)
                if op.dma_key is not None:
                    ins.then_inc(key_sem[op.dma_key], 16)
                elif op.need_inc:
                    ins.then_inc(eng_sem[op.eng], 1)

        @block.sync
        def _(e):
            run("sp", e)

        @block.scalar
        def _(e):
            run("act", e)

        @block.gpsimd
        def _(e):
            run("pool", e)

        @block.vector
        def _(e):
            run("dve", e)

        @block.tensor
        def _(e):
            run("pe", e)


class Arena:
    def __init__(self, nc, name, nbytes, ap=None):
        self.words = nbytes // 4
        if ap is None:
            self.t = nc.alloc_sbuf_tensor(name, [128, self.words], F32)
            self.ap = self.t.ap()
        else:
            self.ap = ap
        self.off = 0
        self.hi = 0
        self.last = (0, 0)

    def sub(self, rng):
        off, words = rng
        return Arena(None, None, words * 4, ap=self.ap[:, off:off + words])

    def mark(self):
        return self.off

    def reset(self, m):
        self.off = m

    def alloc(self, free_shape, dt):
        n = 1
        for s in free_shape:
            n *= s
        words = n if dt == F32 else (n + 1) // 2
        words = (words + 7) // 8 * 8
        assert self.off + words <= self.words, ("arena overflow", self.off, words, self.words)
        v = self.ap[:, self.off:self.off + words]
        self.last = (self.off, words)
        self.off += words
        self.hi = max(self.hi, self.off)
        if dt != F32:
            v = v.bitcast(dt)
        v = v[:, 0:n]
        if len(free_shape) == 2:
            v = v.rearrange("p (a b) -> p a b", a=free_shape[0])
        elif len(free_shape) == 3:
            v = v.rearrange("p (a b c) -> p a b c", a=free_shape[0], b=free_shape[1])
        return v


def build_nc(debug=False, max_ops=None, do_schedule=True):
    nc = bass.Bass("TRN2", target_bir_lowering=False)

    def din(name, shape):
        return nc.dram_tensor(name, list(shape), F32, kind="ExternalInput").ap()

    x_d = din("x", [T, D])
    p_d = din("p", [T, 256])
    win_d = din("w_in", [D, DIN])
    wa_d = din("w_a", [512, D])
    wb_d = din("w_b", [512, D])
    wout_d = din("w_out", [D, D])
    wple_d = din("w_ple", [256, D])
    wg_d = din("w_g", [D, D])
    wup_d = din("w_up", [16, 256])
    lbl_d = din("lbl", [128, 8])
    gpre_d = din("gpre", [1, D])
    hna_d = din("hna", [128, 4])
    hnb_d = din("hnb", [128, 4])
    bgl_d = din("bgl", [128, 2])
    gpost_d = din("gpost", [1, D])
    gple_d = din("gple", [1, D])
    ident_d = din("ident", [128, 128])
    maskbd_d = din("maskbd", [128, 128])
    mask01_d = din("mask01", [128, 512])
    out_d = nc.dram_tensor("out", [T, D], F32, kind="ExternalOutput").ap()
    if debug:
        dbg_a = nc.dram_tensor("dbg_a", [128, 4 * T], F32, kind="ExternalOutput").ap()
        dbg_b = nc.dram_tensor("dbg_b", [128, 4 * T], F32, kind="ExternalOutput").ap()

    win_v = win_d.rearrange("(k p) n -> p k n", p=128)
    wa_v = wa_d.rearrange("(k p) n -> p k n", p=128)
    wb_v = wb_d.rearrange("(k p) n -> p k n", p=128)
    wout_v = wout_d.rearrange("(k p) n -> p k n", p=128)
    wple_v = wple_d.rearrange("(k p) n -> p k n", p=128)
    wg_v = wg_d.rearrange("(k p) n -> p k n", p=128)

    P = Prog()
    P.max_ops = max_ops
    P.do_schedule = do_schedule
    with contextlib.ExitStack() as st:
        ar = Arena(nc, "arena", 207 * 1024 + 768)
        banks = [st.enter_context(nc.psum_tensor("bank%d" % i, [128, 512], F32)) for i in range(8)]

        def bankf(i):
            return banks[i][:]

        def bankb(i):
            return banks[i][:].bitcast(BF16)

        hT = ar.alloc([8, T], BF16)
        hT_rng = ar.last
        oTa = ar.alloc([4, T], BF16)
        oTa_rng = ar.last
        oTb = ar.alloc([4, T], BF16)
        oTb_rng = ar.last
        ident = ar.alloc([128], BF16)
        maskbd = ar.alloc([128], F32)
        mask01 = ar.alloc([512], F32)
        sm_lbl = ar.alloc([8], F32)
        hna = ar.alloc([4], F32)
        hnb = ar.alloc([4], F32)
        hna2 = ar.alloc([4], F32)
        hnb2 = ar.alloc([4], F32)
        bgl = ar.alloc([2], F32)
        nbgl = ar.alloc([2], F32)
        c1 = ar.alloc([4], F32)
        nc1 = ar.alloc([4], F32)
        lnc1 = ar.alloc([4], F32)
        c0 = ar.alloc([4], F32)
        lbd = ar.alloc([4], F32)
        lbt = ar.alloc([4], F32)
        epsc = ar.alloc([1], F32)
        onec = ar.alloc([1], F32)
        nhalf = ar.alloc([8], F32)
        wup = ar.alloc([256], BF16)
        stage = None
        bufX = ar.alloc([8, 2048], BF16)
        bufX_rng = ar.last
        bufY = ar.alloc([8, 1552], BF16)
        bufY_rng = ar.last
        dummy = ar.alloc([8], F32)
        dummy2 = ar.alloc([8], F32)
        P.join_fn = lambda e: e.memset(dummy, 0.0)
        work0 = ar.mark()

        ukey = [0]

        def dma(dst, src, key=None, reads=(), writes=(), q="sp", nbytes=65536):
            if key is None:
                ukey[0] += 1
                key = "u%d" % ukey[0]
            lat = 2.0 + nbytes * 128 / 200e3
            occ = 0.45 if q == "sp" else 1.1
            o_ = P.add(q, lambda e: e.dma_start(out=dst, in_=src), reads=reads, writes=writes, dma_key=key,
                       cost=occ, lat=lat)
            if o_ is not None:
                nel = 1
                for s_ in src.shape:
                    nel *= s_
                o_.dbytes = max(nel * 4, 16384)

        stage_ctr = [0]

        def load_cast(dst, src, ncols, scale, dst_res, eng="pool", extra_reads=()):
            c = 0
            while c < ncols:
                w = min(1024, ncols - c)
                sl = stage_ctr[0] % 2
                stage_ctr[0] += 1
                sres = "stg%d" % sl
                sv = stage[sl][:, 0:w]
                dma(sv, src[:, c:c + w], sres, writes=[sres], nbytes=4 * w)
                dv = dst[:, c:c + w]
                if eng == "pool":
                    P.add("pool", lambda e, dv=dv, sv=sv: e.tensor_scalar(out=dv, in0=sv, scalar1=scale, scalar2=0.0,
                                                                          op0=ALU.mult, op1=ALU.add),
                          reads=[sres] + list(extra_reads), writes=[dst_res], cost=c_pool(w) * 0.6)
                else:
                    P.add("dve", lambda e, dv=dv, sv=sv: e.tensor_scalar(out=dv, in0=sv, scalar1=scale, scalar2=None,
                                                                         op0=ALU.mult),
                          reads=[sres] + list(extra_reads), writes=[dst_res], cost=c_dve(w))
                c += w

        def mm_acc(out, pairs, reads, writes):
            def fn(e):
                n = len(pairs)
                for i, (l, r) in enumerate(pairs):
                    ins = e.matmul(out, lhsT=l, rhs=r, start=(i == 0), stop=(i == n - 1))
                return ins
            cols = [r.shape[-1] for (_, r) in pairs]
            P.add("pe", fn, reads=reads, writes=writes, cost=c_pe(cols))

        dma(ident, ident_d, reads=(), writes=["ident"], q="pool", nbytes=256)
        dma(wup[0:16, :], wup_d, writes=["wup"], q="pool", nbytes=512)
        dma(maskbd, maskbd_d, writes=["maskbd"], nbytes=512)
        dma(mask01, mask01_d, writes=["mask01"], nbytes=2048)
        dma(sm_lbl, lbl_d, writes=["lbl"], nbytes=32)
        dma(hna, hna_d, writes=["hna"], nbytes=16)
        dma(hnb, hnb_d, writes=["hnb"], nbytes=16)
        dma(bgl, bgl_d, writes=["bgl"], nbytes=8)
        P.add("pool", lambda e: e.memset(epsc, EPS), writes=["epsc"], cost=0.3)
        P.add("pool", lambda e: e.memset(onec, 1.0), writes=["onec"], cost=0.3)
        P.add("pool", lambda e: e.memset(nhalf, -0.5), writes=["nhalf"], cost=0.3)
        P.add("dve", lambda e: e.tensor_tensor(out=lbd, in0=sm_lbl[:, 0:4], in1=sm_lbl[:, 4:8], op=ALU.subtract),
              reads=["lbl"], writes=["lbd"], cost=0.2)
        P.add("act", lambda e: e.activation(out=lbt, in_=lbd, func=AF.Tanh, scale=0.5), reads=["lbd"], writes=["lbt"],
              cost=0.3, tset="tanh")
        P.add("dve", lambda e: e.tensor_scalar(out=c1, in0=lbt, scalar1=-0.25, scalar2=0.25, op0=ALU.mult, op1=ALU.add),
              reads=["lbt"], writes=["c1"], cost=0.2)
        P.add("dve", lambda e: e.tensor_scalar(out=c0, in0=lbt, scalar1=0.25, scalar2=0.75, op0=ALU.mult, op1=ALU.add),
              reads=["lbt"], writes=["c0"], cost=0.2)
        P.add("dve", lambda e: e.tensor_scalar(out=nc1, in0=lbt, scalar1=0.25, scalar2=-0.25, op0=ALU.mult, op1=ALU.add),
              reads=["lbt"], writes=["nc1"], cost=0.2)
        P.add("act", lambda e: e.activation(out=lnc1, in_=c1, func=AF.Ln), reads=["c1"], writes=["lnc1"], cost=0.3,
              tset="ln")
        P.add("dve", lambda e: e.tensor_scalar(out=nbgl, in0=bgl, scalar1=-1.0, scalar2=None, op0=ALU.mult),
              reads=["bgl"], writes=["nbgl"], cost=0.2)
        P.add("dve", lambda e: e.tensor_scalar(out=hna2, in0=hna, scalar1=1.0, scalar2=None, op0=ALU.mult),
              reads=["hna"], writes=["hna2"], cost=0.2)
        P.add("dve", lambda e: e.tensor_scalar(out=hnb2, in0=hnb, scalar1=1.0, scalar2=None, op0=ALU.mult),
              reads=["hnb"], writes=["hnb2"], cost=0.2)
        P.mark("consts_done")

        a0 = ar.sub(bufY_rng)
        gpre_bc = a0.alloc([D], F32)
        xs = [a0.alloc([D], F32) for _ in range(3)]
        hb = [a0.alloc([D], BF16) for _ in range(2)]
        junk0 = a0.alloc([D], BF16)
        st0 = [a0.alloc([4], F32) for _ in range(2)]
        dma(gpre_bc, gpre_d.partition_broadcast(128).rearrange("p a b -> p (a b)"), writes=["gpre_bc"], nbytes=4096)
        ph0_res = ["gpre_bc", "junk0"]

        def wa_load(grp, reads=()):
            c0_ = grp * 512
            dma(bufX[:, :, c0_:c0_ + 512], win_v[:, :, c0_:c0_ + 512], reads=reads, writes=["bufX_%d" % grp],
                q="pool", nbytes=8 * 1024)


        for t in range(NT):
            sl3 = t % 3
            sl = t % 2
            xr, hr, sr = "xs%d" % sl3, "hb%d" % sl, "st0_%d" % sl
            ph0_res += [xr, hr, sr + "a", sr + "b", sr + "c"]
            tpb = bankb(6 + sl).rearrange("p (k c) -> p k c", k=8)
            dma(xs[sl3], x_d[t * 128:(t + 1) * 128, :], xr, writes=[xr], nbytes=4096)
            if t < 4:
                wa_load(t)
            P.add("act", lambda e, sl=sl, sl3=sl3: e.activation(out=junk0, in_=xs[sl3], func=AF.Square,
                                                               accum_out=st0[sl][:, 0:1]),
                  reads=[xr], writes=["junk0", sr + "a"], cost=c_act(D, acc=True))
            P.add("act", lambda e, sl=sl: e.activation(out=st0[sl][:, 1:2], in_=st0[sl][:, 0:1], func=AF.Ln,
                                                       scale=1.0 / D, bias=epsc[:, 0:1]),
                  reads=[sr + "a", "epsc"], writes=[sr + "b"], cost=0.3, tset="ln")
            P.add("act", lambda e, sl=sl: e.activation(out=st0[sl][:, 2:3], in_=st0[sl][:, 1:2], func=AF.Exp,
                                                       scale=-0.5),
                  reads=[sr + "b"], writes=[sr + "c"], cost=0.3)
            P.add("dve", lambda e, sl=sl, sl3=sl3: e.scalar_tensor_tensor(out=hb[sl], in0=xs[sl3],
                                                                         scalar=st0[sl][:, 2:3], in1=gpre_bc,
                                                                         op0=ALU.mult, op1=ALU.mult),
                  reads=[xr, sr + "c", "gpre_bc"], writes=[hr], cost=c_dve(D))

            def tr(e, sl=sl, tpb=tpb):
                for k in range(8):
                    ins = e.transpose(out=tpb[:, k, :], in_=hb[sl][:, k * 128:(k + 1) * 128], identity=ident)
                return ins
            P.add("pe", tr, reads=[hr, "ident"], writes=["bank%d" % (6 + sl)], cost=8 * 0.08)
            P.add("dve", lambda e, t=t, tpb=tpb: e.tensor_copy(out=hT[:, :, t * 128:(t + 1) * 128], in_=tpb),
                  reads=["bank%d" % (6 + sl)], writes=["hT%d" % (t // 4)], cost=c_dve(D) * 0.7)
        P.mark("phase0_done")

        ar.reset(work0)
        NB = 4
        tmp = [[ar.alloc([512], F32) for _ in range(6)] for _ in range(2)]
        qTs = [ar.alloc([4, 512], BF16) for _ in range(2)]
        kTs = [ar.alloc([NB, 512], BF16) for _ in range(2)]
        ecs = [ar.alloc([NB, 8], F32) for _ in range(3)]
        NBT = 4
        vbfs = [ar.alloc([512], BF16) for _ in range(NBT)]
        Gs = [ar.alloc([512], F32) for _ in range(NBT)]
        tgs = Gs
        kts = [ar.alloc([4, 128], BF16) for _ in range(NBT)]
        smks = [ar.alloc([4, 128], BF16) for _ in range(NBT)]
        onbs = [ar.alloc([512], BF16) for _ in range(NBT)]
        on1 = ar.alloc([4, 128], F32)
        Tts = [ar.alloc([NB, 128], F32) for _ in range(2)]
        Sbs = [ar.alloc([NB, 128], BF16) for _ in range(2)]
        st4s = [ar.alloc([16], F32) for _ in range(4)]
        aTs = [ar.alloc([512], BF16) for _ in range(2)]
        junk4 = ar.alloc([4, 128], BF16)

        def gla_branch(br, prefetch=None):
            isA = br == "A"
            nblk = 4 if isA else 2
            dk = 128 if isA else 64
            W = bufX if isA else bufY
            oT = oTa if isA else oTb
            ores = "oTa" if isA else "oTb"
            qc0, fc0, vc0, gc0 = (0, 512, 1024, 1536) if isA else (0, 256, 512, 1024)
            if isA:
                wq, wf, wv, wg_ = ["bufX_0"], ["bufX_1"], ["bufX_2"], ["bufX_3"]
            else:
                wq, wf, wv, wg_ = ["bufY_0"], ["bufY_0"], ["bufY_1"], ["bufY_2"]

            P.add("pool", lambda e: e.memset(Tts[1], 0.0), writes=["Tt1_%d" % b for b in range(NB)], cost=c_pool(512))
            P.add("pool", lambda e: e.memset(Sbs[0], 0.0), writes=["Sb0"], cost=c_pool(256))
            for i in range(3):
                P.add("pool", lambda e, i=i: e.memset(ecs[i], 0.0), writes=["ec%d" % i], cost=0.3)
            if not isA:
                for i in range(2):
                    P.add("pool", lambda e, i=i: e.memset(qTs[i], 0.0), writes=["qT%d" % i], cost=c_pool(1024))
                for i in range(NBT):
                    P.add("pool", lambda e, i=i: e.memset(kts[i], 0.0), writes=["kt%d" % i], cost=c_pool(256))

            def prep(s):
                ss = s % 2
                qT, kT, ec = qTs[ss], kTs[ss], ecs[s % 3]
                qres, kres, ecres = "qT%d" % ss, "kT%d" % ss, "ec%d" % (s % 3)
                tok = slice(s * 512, (s + 1) * 512)
                hres = "hT%d" % s
                aT = aTs[ss]
                if not isA:
                    mm_acc(bankf(4)[0:16, :], [(W[:, k, 1536:1552], hT[:, k, tok]) for k in range(8)],
                           reads=["bufY_3", hres], writes=["bank4"])
                    P.add("act", lambda e, aT=aT: e.activation(out=aT[0:16, :], in_=bankf(4)[0:16, :], func=AF.Copy),
                          reads=["bank4"], writes=["aT%d" % ss], cost=c_act(512))
                for blk in range(nblk):
                    ts_ = (s * nblk + blk) % 2
                    t1, t2, t3, t4, Ebt, Enb = tmp[ts_]
                    r1, r2, r3, r4, rE, rN = ["tmp%d_%d" % (ts_, i) for i in range(6)]
                    qcol = slice(qc0 + blk * 128, qc0 + (blk + 1) * 128)
                    fcol = slice(fc0 + blk * 128, fc0 + (blk + 1) * 128)
                    mm_acc(bankf(0), [(W[:, k, qcol], hT[:, k, tok]) for k in range(8)],
                           reads=wq + [hres], writes=["bank0"])
                    mm_acc(bankf(1), [(W[:, k, fcol], hT[:, k, tok]) for k in range(8)],
                           reads=wf + [hres], writes=["bank1"])
                    if isA:
                        P.add("act", lambda e, t1=t1: e.activation(out=t1, in_=bankf(0), func=AF.Silu),
                              reads=["bank0"], writes=[r1], cost=c_act(512), tset="tanh")
                        P.add("act", lambda e, t2=t2: e.activation(out=t2, in_=bankf(1), func=AF.Tanh, scale=0.5),
                              reads=["bank1"], writes=[r2], cost=c_act(512), tset="tanh")
                        P.add("act", lambda e, blk=blk, t2=t2, t3=t3: e.activation(
                            out=t3, in_=t2, func=AF.Ln, scale=c1[:, blk:blk + 1], bias=c0[:, blk:blk + 1]),
                            reads=[r2, "c1", "c0"], writes=[r3], cost=c_act(512, 2), tset="ln")
                    else:
                        mm_acc(bankf(4), [(wup[0:16, blk * 128:(blk + 1) * 128], aT[0:16, :])],
                               reads=["wup", "aT%d" % ss], writes=["bank4"])
                        P.add("act", lambda e, blk=blk, t1=t1: e.activation(out=t1, in_=bankf(4), func=AF.Exp, scale=-1.0,
                                                                            bias=nbgl[:, blk:blk + 1]),
                              reads=["bank4", "nbgl"], writes=[r1], cost=c_act(512, 1))
                        P.add("act", lambda e, t1=t1, t3=t3: e.activation(out=t3, in_=t1, func=AF.Ln, bias=onec[:, 0:1]),
                              reads=[r1, "onec"], writes=[r3], cost=c_act(512, 1), tset="ln")
                    P.add("dve", lambda e, t3=t3, t4=t4: e.tensor_tensor_scan(out=t4, data0=mask01, data1=t3, initial=0.0,
                                                                              op0=ALU.mult, op1=ALU.add),
                          reads=[r3, "mask01"], writes=[r4], cost=c_dve(512, scan=True))
                    sE, sN = (1.0, -1.0) if isA else (-1.0 / 16, 1.0 / 16)
                    P.add("act", lambda e, t4=t4, Ebt=Ebt, sE=sE: e.activation(out=Ebt, in_=t4, func=AF.Exp, scale=sE),
                          reads=[r4], writes=[rE], cost=c_act(512))
                    if isA:
                        P.add("act", lambda e, t4=t4, Enb=Enb, blk=blk: e.activation(out=Enb, in_=t4, func=AF.Exp,
                                                                                     scale=-1.0, bias=lnc1[:, blk:blk + 1]),
                              reads=[r4, "lnc1"], writes=[rN], cost=c_act(512, 1))
                    else:
                        P.add("act", lambda e, t4=t4, Enb=Enb, sN=sN: e.activation(out=Enb, in_=t4, func=AF.Exp, scale=sN),
                              reads=[r4], writes=[rN], cost=c_act(512))
                    P.add("dve", lambda e, Ebt=Ebt, ec=ec, blk=blk: e.tensor_copy(
                        out=ec[:, blk, :], in_=Ebt.rearrange("p (c j) -> p c j", j=64)[:, :, 63]),
                        reads=[rE], writes=[ecres], cost=0.2)
                    if isA:
                        P.add("dve", lambda e, blk=blk, t1=t1, Ebt=Ebt, qT=qT: e.scalar_tensor_tensor(
                            out=qT[:, blk, :], in0=t1, scalar=-(dk ** -0.5), in1=Ebt, op0=ALU.mult, op1=ALU.mult),
                            reads=[r1, rE], writes=[qres], cost=c_dve(512))
                        P.add("dve", lambda e, blk=blk, t2=t2, Enb=Enb, kT=kT: e.scalar_tensor_tensor(
                            out=kT[:, blk, :], in0=t2, scalar=-1.0, in1=Enb, op0=ALU.add, op1=ALU.mult),
                            reads=[r2, rN, r3], writes=[kres], cost=c_dve(512))
                    else:
                        for hh in range(2):
                            pq = slice(hh * 64, (hh + 1) * 64)
                            P.add("dve", lambda e, blk=blk, hh=hh, pq=pq, Ebt=Ebt, qT=qT: e.scalar_tensor_tensor(
                                out=qT[pq, 2 * blk + hh, :], in0=bankf(0)[pq, :], scalar=dk ** -0.5,
                                in1=Ebt[pq, :], op0=ALU.mult, op1=ALU.mult),
                                reads=["bank0", rE], writes=[qres], cost=c_dve(512))
                        P.add("dve", lambda e, blk=blk, Enb=Enb, kT=kT: e.tensor_tensor(out=kT[:, blk, :], in0=bankf(1),
                                                                                       in1=Enb, op=ALU.mult),
                              reads=["bank1", rN], writes=[kres], cost=c_dve(512))

            def tile(t):
                s, tt = divmod(t, 4)
                ss = s % 2
                tp_ = t % NBT
                qT, kT, ec = qTs[ss], kTs[ss], ecs[s % 3]
                qres, kres, ecres = "qT%d" % ss, "kT%d" % ss, "ec%d" % (s % 3)
                vbf, tg, G, kt, smk, onb, st4 = vbfs[tp_], tgs[tp_], Gs[tp_], kts[tp_], smks[tp_], onbs[tp_], st4s[tp_]
                rv, rtg, rG, rkt, rsm, ron, rst = ["%s%d" % (n_, tp_) for n_ in ("vbf", "tg", "G", "kt", "smk", "onb", "st4")]
                hres = "hT%d" % s
                tl = slice(tt * 128, (tt + 1) * 128)
                tg_ = slice(t * 128, (t + 1) * 128)
                mm_acc(bankf(2), [(hT[:, k, tg_], W[:, k, vc0:vc0 + 512]) for k in range(8)],
                       reads=wv + [hres], writes=["bank2"])
                P.add("dve", lambda e: e.tensor_copy(out=vbf, in_=bankf(2)),
                      reads=["bank2"], writes=[rv], cost=c_dve(512))
                mm_acc(bankf(2), [(hT[:, k, tg_], W[:, k, gc0:gc0 + 512]) for k in range(8)],
                       reads=wg_ + [hres], writes=["bank2"])
                P.add("act", lambda e: e.activation(out=G, in_=bankf(2), func=AF.Silu),
                      reads=["bank2"], writes=[rG], cost=c_act(512), tset="tanh")
                tpk = bankb(3)[:, 0:nblk * 128].rearrange("p (b c) -> p b c", b=nblk)

                def trk(e):
                    for blk in range(nblk):
                        ins = e.transpose(out=tpk[:, blk, :], in_=kT[:, blk, tl], identity=ident)
                    return ins
                P.add("pe", trk, reads=[kres, "ident"], writes=["bank3"], cost=nblk * 0.08)
                if isA:
                    P.add("act", lambda e: e.activation(out=kt, in_=tpk, func=AF.Copy),
                          reads=["bank3"], writes=[rkt], cost=c_act(512))
                else:
                    ktv = kt.rearrange("p (b h) c -> p b h c", h=2)
                    for hh in range(2):
                        cs_ = slice(hh * 64, (hh + 1) * 64)
                        P.add("act", lambda e, hh=hh, cs_=cs_: e.activation(
                            out=ktv[:, :, hh, cs_], in_=tpk[:, :, cs_], func=AF.Copy),
                            reads=["bank3"], writes=[rkt], cost=c_act(128))
                scp = bankf(4).rearrange("p (h c) -> p h c", h=4)

                def scf(e):
                    for h in range(4):
                        blk = h if isA else h // 2
                        ins = e.matmul(scp[:, h, :], lhsT=kT[:, blk, tl], rhs=qT[:, h, tl], start=True, stop=True)
                    return ins
                P.add("pe", scf, reads=[kres, qres], writes=["bank4"], cost=c_pe([128] * 4))
                P.add("dve", lambda e: e.tensor_tensor(out=smk, in0=scp,
                                                       in1=maskbd.unsqueeze(1).to_broadcast([128, 4, 128]),
                                                       op=ALU.mult),
                      reads=["bank4", "maskbd"], writes=[rsm], cost=c_dve(512))
                op_ = bankf(5).rearrange("p (h c) -> p h c", h=4)

                def oin(e):
                    for h in range(4):
                        ins = e.matmul(op_[:, h, :], lhsT=smk[:, h, :], rhs=vbf[:, h * 128:(h + 1) * 128],
                                       start=(h == 0), stop=False, skip_group_check=True)
                    return ins
                P.add("pe", oin, reads=[rsm, rv], writes=["bank5"], cost=c_pe([128] * 4))
                pp = bankf(6)[:, 0:nblk * 128].rearrange("p (b c) -> p b c", b=nblk)
                for c in range(2):
                    cg = 2 * t + c
                    cl8 = 2 * tt + c
                    cl = slice(tt * 128 + c * 64, tt * 128 + (c + 1) * 64)
                    pr = slice(c * 64, (c + 1) * 64)
                    Sb_cur, Sb_nxt = Sbs[cg % 2], Sbs[(cg + 1) % 2]
                    rSc, rSn = "Sb%d" % (cg % 2), "Sb%d" % ((cg + 1) % 2)

                    def ointer(e, cl=cl, pr=pr, c=c, Sb_cur=Sb_cur):
                        for h in range(4):
                            blk = h if isA else h // 2
                            ins = e.matmul(op_[pr, h, :], lhsT=qT[:, h, cl], rhs=Sb_cur[:, blk, :],
                                           start=False, stop=(c == 1 and h == 3), skip_group_check=True)
                        return ins
                    P.add("pe", ointer, reads=[qres, rSc, "bank5"], writes=["bank5"], cost=c_pe([128] * 4))

                    def pst(e, pr=pr):
                        for h in range(4):
                            blk = h if isA else h // 2
                            first = True if isA else (h % 2 == 0)
                            last = True if isA else (h % 2 == 1)
                            ins = e.matmul(pp[:, blk, :], lhsT=kt[pr, h, :],
                                           rhs=vbf[pr, h * 128:(h + 1) * 128], start=first, stop=last)
                        return ins
                    P.add("pe", pst, reads=[rkt, rv], writes=["bank6"], cost=c_pe([128] * 4))
                    if cl8 == 0:
                        ecp, ecpres, pcol = ecs[(s - 1) % 3], "ec%d" % ((s - 1) % 3), 7
                    else:
                        ecp, ecpres, pcol = ec, ecres, cl8 - 1
                    T_prev, T_cur = Tts[(cg + 1) % 2], Tts[cg % 2]
                    rTp, rTc = "Tt%d" % ((cg + 1) % 2), "Tt%d" % (cg % 2)
                    for blk in range(nblk):
                        P.add("dve", lambda e, blk=blk, ecp=ecp, pcol=pcol, T_prev=T_prev, T_cur=T_cur:
                              e.scalar_tensor_tensor(out=T_cur[:, blk, :], in0=T_prev[:, blk, :],
                                                     scalar=ecp[:, blk, pcol:pcol + 1], in1=pp[:, blk, :],
                                                     op0=ALU.mult, op1=ALU.add),
                              reads=[rTp + "_%d" % blk, ecpres, "bank6"], writes=[rTc + "_%d" % blk],
                              cost=c_dve(128) + 0.07)
                    ebc = ec[:, 0:nblk, cl8:cl8 + 1].to_broadcast([128, nblk, 128])
                    P.add("pool", lambda e, ebc=ebc, Sb_nxt=Sb_nxt, T_cur=T_cur: e.tensor_tensor(
                        out=Sb_nxt[:, 0:nblk, :], in0=T_cur[:, 0:nblk, :], in1=ebc, op=ALU.mult),
                        reads=[rTc + "_%d" % b for b in range(nblk)] + [ecres], writes=[rSn],
                        cost=c_pool(128 * nblk))
                for h in range(4):
                    P.add("act", lambda e, h=h: e.activation(out=junk4[:, h, :], in_=op_[:, h, :], func=AF.Square,
                                                             accum_out=st4[:, h:h + 1]),
                          reads=["bank5"], writes=["junk4_%d" % h, rst + "a%d" % h], cost=c_act(128, acc=True))
                P.add("dve", lambda e: e.tensor_scalar(out=st4[:, 4:8], in0=st4[:, 0:4], scalar1=1.0 / 128,
                                                       scalar2=EPS, op0=ALU.mult, op1=ALU.add),
                      reads=[rst + "a%d" % h for h in range(4)], writes=[rst + "b"], cost=0.2)
                P.add("pool", lambda e: e.tensor_tensor(out=st4[:, 8:12], in0=st4[:, 4:8], in1=nhalf[:, 0:4],
                                                        op=ALU.pow),
                      reads=[rst + "b", "nhalf"], writes=[rst + "c"], cost=1.0)
                P.add("dve", lambda e: e.tensor_tensor(out=on1, in0=op_,
                                                       in1=st4[:, 8:12].unsqueeze(2).to_broadcast([128, 4, 128]),
                                                       op=ALU.mult),
                      reads=["bank5", rst + "c"], writes=["on1"], cost=c_dve(512))
                P.add("pool", lambda e: e.tensor_tensor(out=onb, in0=on1.rearrange("p h c -> p (h c)"), in1=G,
                                                        op=ALU.mult),
                      reads=["on1", rG], writes=[ron], cost=c_pool(512))
                tpo = bankb(7)[:, 512:1024].rearrange("p (b c) -> p b c", b=4)

                def tro(e):
                    for h in range(4):
                        ins = e.transpose(out=tpo[:, h, :], in_=onb[:, h * 128:(h + 1) * 128], identity=ident)
                    return ins
                P.add("pe", tro, reads=[ron, "ident"], writes=["bank7"], cost=4 * 0.08)
                hn2 = hna2 if isA else hnb2
                for h in range(4):
                    eng_ = "dve"
                    if eng_ == "dve":
                        P.add("dve", lambda e, h=h: e.tensor_scalar(out=oT[:, h, tg_], in0=tpo[:, h, :],
                                                                    scalar1=hn2[:, h:h + 1], scalar2=None, op0=ALU.mult),
                              reads=["bank7", "hna2", "hnb2"], writes=[ores + "_%d" % h], cost=c_dve(128) * 0.8)
                    else:
                        P.add("act", lambda e, h=h: e.activation(out=oT[:, h, tg_], in_=tpo[:, h, :], func=AF.Copy,
                                                                 scale=hn2[:, h:h + 1]),
                              reads=["bank7", "hna2", "hnb2"], writes=[ores + "_%d" % h], cost=c_act(128, 1))

            prep(0)
            for s in range(NS):
                if s == 1 and prefetch is not None:
                    prefetch()
                if s + 1 < NS:
                    prep(s + 1)
                P.mark(br + "_tiles_s%d" % s)
                for tt in range(4):
                    tile(4 * s + tt)

        def prefetch_WB():
            grp = [(0, 512), (512, 512), (1024, 512), (1536, 16)]
            P.add("pool", lambda e: e.memset(dummy2, 0.0), writes=ph0_res + ["ph0_done"], cost=0.3)
            for i, (c0_, w) in enumerate(grp):
                dma(bufY[:, :, c0_:c0_ + w], win_v[:, :, 2048 + c0_:2048 + c0_ + w], reads=["ph0_done"],
                    writes=["bufY_%d" % i], q="pool", nbytes=8 * 2 * w)

        def prefetch_WZ():
            for i in range(4):
                dma(bufX[:, :, i * 512:(i + 1) * 512], win_v[:, :, 3600 + i * 512:3600 + (i + 1) * 512],
                    writes=["bufX_%d" % i], q="pool", nbytes=8 * 1024)

        gla_branch("A", prefetch_WB)
        P.mark("A_done")
        gla_branch("B", prefetch_WZ)
        P.mark("B_done")
        P.barrier()

        if debug:
            ar.reset(work0)
            dbf = ar.alloc([4 * T], F32)
            P.add("dve", lambda e: e.tensor_copy(out=dbf, in_=oTa.rearrange("p a b -> p (a b)")), reads=["oTa_%d" % h for h in range(4)],
                  writes=["dbf"])
            dma(dbg_a, dbf, "dbg", reads=["dbf"], writes=["dbg_a"])
            P.add("dve", lambda e: e.tensor_copy(out=dbf, in_=oTb.rearrange("p a b -> p (a b)")),
                  reads=["oTb_%d" % h for h in range(4)] + ["dbg_a"], writes=["dbf"])
            dma(dbg_b, dbf, "dbg", reads=["dbf"], writes=["dbg_b"])
            P.barrier()

        ar.reset(work0)
        by = ar.sub(bufY_rng)
        Wab = by.alloc([8, D], BF16)
        Wple = by.alloc([2, D], BF16)
        gpost_bc = by.alloc([D], F32)
        ypT = ar.alloc([8, T], BF16)
        Wout = ar.alloc([8, D], BF16)
        gple_bc = ar.alloc([D], F32)
        tas = [ar.alloc([512], F32) for _ in range(2)]
        tbs = [ar.alloc([512], F32) for _ in range(2)]
        Wg = ar.alloc([8, D], BF16)

        dma(Wab[:, 0:4, :], wa_v, writes=["Wab_a"], q="pool", nbytes=8192)
        dma(Wab[:, 4:8, :], wb_v, writes=["Wab_b"], q="pool", nbytes=8192)

        def prefetch_C2():
            for i in range(2):
                dma(Wout[:, :, i * 512:(i + 1) * 512], wout_v[:, :, i * 512:(i + 1) * 512], writes=["Wout%d" % i],
                    q="pool", nbytes=8 * 1024)
            dma(Wple, wple_v, writes=["Wple"], q="pool", nbytes=4096)
            for i in range(2):
                dma(Wg[:, :, i * 512:(i + 1) * 512], wg_v[:, :, i * 512:(i + 1) * 512], writes=["Wg%d" % i],
                    q="pool", nbytes=8 * 1024)
            dma(gpost_bc, gpost_d.partition_broadcast(128).rearrange("p a b -> p (a b)"), writes=["gpost"], nbytes=4096)
            dma(gple_bc, gple_d.partition_broadcast(128).rearrange("p a b -> p (a b)"), writes=["gple"], nbytes=4096)

        for s in range(NS):
            tok = slice(s * 512, (s + 1) * 512)
            hres = "hT%d" % s
            if s == 1:
                prefetch_C2()
            for fb in range(8):
                par = fb % 2
                b0 = 4 * par
                ta, tb = tas[fb % 2], tbs[fb % 2]
                rta, rtb = "ta%d" % (fb % 2), "tb%d" % (fb % 2)
                fcol = slice(fb * 128, (fb + 1) * 128)
                mm_acc(bankf(b0), [(bufX[:, k, fcol], hT[:, k, tok]) for k in range(8)],
                       reads=["bufX_%d" % (fb // 4), hres], writes=["bank%d" % b0])
                mm_acc(bankf(b0 + 1), [(bufX[:, k, 1024 + fb * 128:1024 + (fb + 1) * 128], hT[:, k, tok])
                                       for k in range(8)],
                       reads=["bufX_%d" % (2 + fb // 4), hres], writes=["bank%d" % (b0 + 1)])
                mm_acc(bankf(b0 + 2), [(Wab[:, k, fcol], oTa[:, k, tok]) for k in range(4)],
                       reads=["Wab_a"] + ["oTa_%d" % h for h in range(4)], writes=["bank%d" % (b0 + 2)])
                mm_acc(bankf(b0 + 3), [(Wab[:, 4 + k, fcol], oTb[:, k, tok]) for k in range(4)],
                       reads=["Wab_b"] + ["oTb_%d" % h for h in range(4)], writes=["bank%d" % (b0 + 3)])
                P.add("act", lambda e, ta=ta, b0=b0: e.activation(out=ta, in_=bankf(b0), func=AF.Sigmoid),
                      reads=["bank%d" % b0], writes=[rta], cost=c_act(512), tset="sig")
                P.add("act", lambda e, tb=tb, b0=b0: e.activation(out=tb, in_=bankf(b0 + 1), func=AF.Sigmoid),
                      reads=["bank%d" % (b0 + 1)], writes=[rtb], cost=c_act(512), tset="sig")
                P.add("dve", lambda e, ta=ta, b0=b0: e.tensor_tensor(out=ta, in0=ta, in1=bankf(b0 + 2), op=ALU.mult),
                      reads=[rta, "bank%d" % (b0 + 2)], writes=[rta], cost=c_dve(512))
                P.add("dve", lambda e, tb=tb, b0=b0: e.tensor_tensor(out=tb, in0=tb, in1=bankf(b0 + 3), op=ALU.mult),
                      reads=[rtb, "bank%d" % (b0 + 3)], writes=[rtb], cost=c_dve(512))
                P.add("pool", lambda e, fb=fb, tok=tok, ta=ta, tb=tb: e.tensor_tensor(out=ypT[:, fb, tok], in0=ta,
                                                                                      in1=tb, op=ALU.add),
                      reads=[rta, rtb], writes=["ypT%d" % s], cost=c_pool(512))
        P.mark("C1_done")
        P.barrier()

        ah = ar.sub(hT_rng)
        ax = ar.sub(bufX_rng)
        ao = ar.sub(oTa_rng)
        ab = ar.sub(oTb_rng)
        NBC = 3
        xsC = [ah.alloc([D], F32) for _ in range(NBC)]
        yo = [ah.alloc([D], F32) for _ in range(NBC)]
        tgC = [(ah if i < 2 else ab).alloc([D], F32) for i in range(NBC)]
        en = [ab.alloc([D], F32) for _ in range(NBC)]
        x1b = [ax.alloc([D], BF16) for _ in range(NBC)]
        x1T = [ax.alloc([8, 128], BF16) for _ in range(NBC)]
        psb = [ao.alloc([256], F32) for _ in range(NBC)]
        pb = [ao.alloc([256], BF16) for _ in range(NBC)]
        pT = [ao.alloc([2, 128], BF16) for _ in range(NBC)]
        junkCs = [ao.alloc([512], BF16) for _ in range(4)]
        stC = [ao.alloc([16], F32) for _ in range(NBC)]
        for t in range(NT):
            sl = t % NBC
            tl = slice(t * 128, (t + 1) * 128)
            xr, pr_, yr = "xsC%d" % sl, "psb%d" % sl, "yo%d" % sl
            sC = stC[sl]
            rs = "stC%d" % sl
            dma(xsC[sl], x_d[tl, :], xr, writes=[xr], nbytes=4096)
            dma(psb[sl], p_d[tl, :], pr_, writes=[pr_], nbytes=1024)
            for hf in range(2):
                mm_acc(bankf(4 + hf), [(ypT[:, k, tl], Wout[:, k, hf * 512:(hf + 1) * 512]) for k in range(8)],
                       reads=["ypT%d" % (t // 4), "Wout%d" % hf], writes=["bank%d" % (4 + hf)])
                P.add("act", lambda e, hf=hf, sC=sC: e.activation(out=junkCs[hf], in_=bankf(4 + hf), func=AF.Square,
                                                                  accum_out=sC[:, hf:hf + 1]),
                      reads=["bank%d" % (4 + hf)], writes=["junkC%d" % hf, rs + "a%d" % hf], cost=c_act(512, acc=True))
            P.add("dve", lambda e, sC=sC: e.tensor_tensor(out=sC[:, 2:3], in0=sC[:, 0:1], in1=sC[:, 1:2], op=ALU.add),
                  reads=[rs + "a0", rs + "a1"], writes=[rs + "b"], cost=0.2)
            P.add("dve", lambda e, sC=sC: e.tensor_scalar(out=sC[:, 3:4], in0=sC[:, 2:3], scalar1=1.0 / D, scalar2=EPS,
                                                          op0=ALU.mult, op1=ALU.add),
                  reads=[rs + "b"], writes=[rs + "c"], cost=0.2)
            P.add("pool", lambda e, sC=sC: e.tensor_tensor(out=sC[:, 4:5], in0=sC[:, 3:4], in1=nhalf[:, 0:1], op=ALU.pow),
                  reads=[rs + "c", "nhalf"], writes=[rs + "d"], cost=0.8)
            for hf in range(2):
                hs = slice(hf * 512, (hf + 1) * 512)
                P.add("dve", lambda e, hf=hf, hs=hs, sl=sl, sC=sC: e.scalar_tensor_tensor(
                    out=yo[sl][:, hs], in0=bankf(4 + hf), scalar=sC[:, 4:5], in1=gpost_bc[:, hs],
                    op0=ALU.mult, op1=ALU.mult),
                    reads=["bank%d" % (4 + hf), rs + "d", "gpost"], writes=[yr + "h%d" % hf], cost=c_dve(512))
            P.add("pool", lambda e, sl=sl: e.tensor_tensor(out=xsC[sl], in0=xsC[sl], in1=yo[sl], op=ALU.add),
                  reads=[xr, yr + "h0", yr + "h1"], writes=[xr], cost=c_pool(D))
            P.add("act", lambda e, sl=sl: e.activation(out=x1b[sl], in_=xsC[sl], func=AF.Copy),
                  reads=[xr], writes=["x1b%d" % sl], cost=c_act(D))
            tpx = bankb(7).rearrange("p (k c) -> p k c", k=8)

            def trx(e, tpx=tpx, sl=sl):
                for k in range(8):
                    ins = e.transpose(out=tpx[:, k, :], in_=x1b[sl][:, k * 128:(k + 1) * 128], identity=ident)
                return ins
            P.add("pe", trx, reads=["x1b%d" % sl, "ident"], writes=["bank7"], cost=8 * 0.08)
            P.add("act", lambda e, tpx=tpx, sl=sl: e.activation(out=x1T[sl], in_=tpx, func=AF.Copy),
                  reads=["bank7"], writes=["x1T%d" % sl], cost=c_act(D))
            P.add("pool", lambda e, sl=sl: e.tensor_copy(out=pb[sl], in_=psb[sl]), reads=[pr_], writes=["pb%d" % sl],
                  cost=c_pool(256))
            tpp = bankb(6)[:, 0:256].rearrange("p (k c) -> p k c", k=2)

            def trp(e, tpp=tpp, sl=sl):
                for k in range(2):
                    ins = e.transpose(out=tpp[:, k, :], in_=pb[sl][:, k * 128:(k + 1) * 128], identity=ident)
                return ins
            P.add("pe", trp, reads=["pb%d" % sl, "ident"], writes=["bank6"], cost=2 * 0.08)
            P.add("act", lambda e, tpp=tpp, sl=sl: e.activation(out=pT[sl], in_=tpp, func=AF.Copy),
                  reads=["bank6"], writes=["pT%d" % sl], cost=c_act(256))
            for hf in range(2):
                bk = 2 + hf
                mm_acc(bankf(bk), [(pT[sl][:, k, :], Wple[:, k, hf * 512:(hf + 1) * 512]) for k in range(2)],
                       reads=["pT%d" % sl, "Wple"], writes=["bank%d" % bk])
                P.add("act", lambda e, hf=hf, bk=bk, sC=sC: e.activation(out=junkCs[2 + hf], in_=bankf(bk), func=AF.Square,
                                                                         accum_out=sC[:, 8 + hf:9 + hf]),
                      reads=["bank%d" % bk], writes=["junkC%d" % (2 + hf), rs + "e%d" % hf], cost=c_act(512, acc=True))
            P.add("dve", lambda e, sC=sC: e.tensor_tensor(out=sC[:, 10:11], in0=sC[:, 8:9], in1=sC[:, 9:10], op=ALU.add),
                  reads=[rs + "e0", rs + "e1"], writes=[rs + "f"], cost=0.2)
            P.add("dve", lambda e, sC=sC: e.tensor_scalar(out=sC[:, 11:12], in0=sC[:, 10:11], scalar1=1.0 / D,
                                                          scalar2=EPS, op0=ALU.mult, op1=ALU.add),
                  reads=[rs + "f"], writes=[rs + "g"], cost=0.2)
            P.add("pool", lambda e, sC=sC: e.tensor_tensor(out=sC[:, 12:13], in0=sC[:, 11:12], in1=nhalf[:, 0:1],
                                                           op=ALU.pow),
                  reads=[rs + "g", "nhalf"], writes=[rs + "h"], cost=0.8)
            for hf in range(2):
                hs = slice(hf * 512, (hf + 1) * 512)
                bk = 2 + hf
                P.add("dve", lambda e, hs=hs, bk=bk, sl=sl, sC=sC: e.scalar_tensor_tensor(
                    out=en[sl][:, hs], in0=bankf(bk), scalar=sC[:, 12:13], in1=gple_bc[:, hs],
                    op0=ALU.mult, op1=ALU.mult),
                    reads=["bank%d" % bk, rs + "h", "gple"], writes=["en%d_%d" % (sl, hf)], cost=c_dve(512))
            for hf in range(2):
                hs = slice(hf * 512, (hf + 1) * 512)
                mm_acc(bankf(hf), [(x1T[sl][:, k, :], Wg[:, k, hs]) for k in range(8)],
                       reads=["x1T%d" % sl, "Wg%d" % hf], writes=["bank%d" % hf])
                P.add("act", lambda e, hf=hf, hs=hs, sl=sl: e.activation(out=tgC[sl][:, hs], in_=bankf(hf),
                                                                         func=AF.Sigmoid),
                      reads=["bank%d" % hf], writes=["tgC%d_%d" % (sl, hf)], cost=c_act(512), tset="sig")
                P.add("dve", lambda e, hs=hs, sl=sl: e.tensor_tensor(out=en[sl][:, hs], in0=tgC[sl][:, hs],
                                                                     in1=en[sl][:, hs], op=ALU.mult),
                      reads=["tgC%d_%d" % (sl, hf), "en%d_%d" % (sl, hf)], writes=["en%d_%d" % (sl, hf)],
                      cost=c_dve(512))
                P.add("dve", lambda e, hs=hs, sl=sl: e.tensor_tensor(out=yo[sl][:, hs], in0=en[sl][:, hs],
                                                                     in1=xsC[sl][:, hs], op=ALU.add),
                      reads=["en%d_%d" % (sl, hf), xr, yr + "h%d" % hf], writes=[yr + "h%d" % hf],
                      cost=c_dve(512))
            dma(out_d[tl, :], yo[sl], "out%d" % sl, reads=[yr + "h0", yr + "h1"],
                writes=[yr + "h0", yr + "h1", "OUT%d" % sl], nbytes=4096)
        P.barrier()
        P.add("sp", None, reads=["OUT0", "OUT1", "OUT2"] + (["dbg_a", "dbg_b"] if debug else []), force=True)
        build_nc.n_ops = len(P.ops)
        build_nc.marks = P.marks
        P.emit(nc, st)
        build_nc.est_time = getattr(P, "est_time", None)
        fin = getattr(P, "finish", None)
        if fin:
            build_nc.mark_times = [(nm, max(fin[:i]) if i else 0.0) for nm, i in P.marks]
            build_nc.busy = P.busy
            build_nc.P = P
    return nc


def _consts():
    ident = np.eye(128, dtype=np.float32)
    j = np.arange(128)[:, None]
    i = np.arange(128)[None, :]
    maskbd = ((j <= i) & ((j // 64) == (i // 64))).astype(np.float32)
    m01 = np.ones((128, 512), np.float32)
    m01[:, ::64] = 0.0
    return ident, maskbd, m01


def make_in_maps(x, p, w_in, lb_logits, w_gla_up, b_gla, norm_pre, norm_post, head_norm_a, head_norm_b,
                 w_branch_a, w_branch_b, w_out, w_ple, w_ple_gate, norm_ple):
    f = lambda a: np.ascontiguousarray(np.asarray(a, dtype=np.float32))
    ident, maskbd, m01 = _consts()
    lbl = f(np.asarray(lb_logits).reshape(2, 4, 128).transpose(2, 0, 1).reshape(128, 8))
    shared = {
        "w_in": f(np.asarray(w_in)[0]),
        "w_a": f(np.asarray(w_branch_a)[0]),
        "w_b": f(np.asarray(w_branch_b)[0]),
        "w_out": f(np.asarray(w_out)[0]),
        "w_ple": f(np.asarray(w_ple)[0]),
        "w_g": f(np.asarray(w_ple_gate)[0]),
        "w_up": f(np.asarray(w_gla_up)[0]),
        "lbl": lbl,
        "gpre": f(np.asarray(norm_pre)[0].reshape(1, D)),
        "hna": f(np.asarray(head_norm_a)[0].reshape(4, 128).T),
        "hnb": f(np.asarray(head_norm_b)[0].reshape(4, 128).T),
        "bgl": f(np.asarray(b_gla)[0].reshape(2, 128).T),
        "gpost": f(np.asarray(norm_post)[0].reshape(1, D)),
        "gple": f(np.asarray(norm_ple)[0].reshape(1, D)),
        "ident": ident,
        "maskbd": maskbd,
        "mask01": m01,
    }
    xs = np.asarray(x)
    ps = np.asarray(p)
    maps = []
    for b in range(8):
        m = dict(shared)
        m["x"] = f(xs[b])
        m["p"] = f(ps[0, b])
        maps.append(m)
    return maps


_NC_CACHE = {}


def kernel(x, p, w_in, lb_logits, w_gla_up, b_gla, norm_pre, norm_post, head_norm_a, head_norm_b,
           w_branch_a, w_branch_b, w_out, w_ple, w_ple_gate, norm_ple):
    in_maps = make_in_maps(x, p, w_in, lb_logits, w_gla_up, b_gla, norm_pre, norm_post, head_norm_a, head_norm_b,
                           w_branch_a, w_branch_b, w_out, w_ple, w_ple_gate, norm_ple)
    if "nc" not in _NC_CACHE:
        _NC_CACHE["nc"] = build_nc(False)
    nc = _NC_CACHE["nc"]
    res = run_bass_kernel_spmd(nc, in_maps, core_ids=list(range(8)))
    out = np.stack([np.asarray(r["out"]) for r in res.results], axis=0).astype(np.float32)
    return out
```
